# Optimizing a Trainium2 kernel written in Bass

```python
import math
import jax, jax.numpy as jnp
from jax import lax
import numpy as np

D_MODEL = 1024
BATCH = 4
SEQ = 8192
DEPTH = 4
DEC_BATCH = 8
DEC_SEQ = 2048
PAST_LEN = 128

ATT_HEAD_DIM = 64
ATT_HEADS_PER_GROUP = 4
DILATED_PATTERNS = ((128, 1), (512, 4), (2048, 16))
N_ATT_HEADS = ATT_HEADS_PER_GROUP * len(DILATED_PATTERNS)
ATT_WIDTH = N_ATT_HEADS * ATT_HEAD_DIM
ATT_OUT_WIDTH = ATT_HEADS_PER_GROUP * ATT_HEAD_DIM
ROPE_THETA = 500000.0
ROPE_DIM = ATT_HEAD_DIM // 4

SSM_WIDTH = 512
SSM_GROUP = 16
SSM_GROUPS = SSM_WIDTH // SSM_GROUP
SSM_STATE = 64
DT_MIN = 0.001
DT_MAX = 0.1
LAMBDA_RE_MAX = -1e-4

MEM_LEN = 256
MEM_HEADS = 4
MEM_HEAD_DIM = 128
MEM_WIDTH = MEM_HEADS * MEM_HEAD_DIM

N_BRANCHES = 3
IN_SPLITS = (ATT_WIDTH, 2 * ATT_WIDTH, 3 * ATT_WIDTH, 3 * ATT_WIDTH + SSM_WIDTH, 3 * ATT_WIDTH + SSM_WIDTH + MEM_WIDTH)
IN_WIDTH = IN_SPLITS[-1] + N_BRANCHES * D_MODEL

PEER_HEADS = 8
PEER_N_KEYS = 128
PEER_N_EXPERTS = PEER_N_KEYS * PEER_N_KEYS
PEER_QUERY_DIM = 256
PEER_HALF = PEER_QUERY_DIM // 2
PEER_TOPK = 16
PEER_TOKEN_BLOCK = 128

DEEPNORM_ALPHA = (2 * DEPTH) ** 0.25
DEEPNORM_BETA = (8 * DEPTH) ** -0.25
LN_EPS = 1e-5
MASK_VALUE = -1e30

kernel_name = 'hybrid_dilated_s5_peer_encoder'


def layer_norm(x, gain, bias):
    xf = x.astype(jnp.float32)
    mu = jnp.mean(xf, axis=-1, keepdims=True)
    var = jnp.mean(jnp.square(xf - mu), axis=-1, keepdims=True)
    y = (xf - mu) * lax.rsqrt(var + LN_EPS)
    return (y * gain.astype(jnp.float32) + bias.astype(jnp.float32)).astype(x.dtype)


def partial_rope(t):
    seq = t.shape[1]
    half = ROPE_DIM // 2
    inv_freq = ROPE_THETA ** (-jnp.arange(0, ROPE_DIM, 2, dtype=jnp.float32) / ROPE_DIM)
    ang = jnp.arange(seq, dtype=jnp.float32)[:, None] * inv_freq[None, :]
    cos = jnp.cos(ang)[None, :, None, :]
    sin = jnp.sin(ang)[None, :, None, :]
    x1 = t[..., :half]
    x2 = t[..., half:ROPE_DIM]
    return jnp.concatenate([x1 * cos - x2 * sin, x2 * cos + x1 * sin, t[..., ROPE_DIM:]], axis=-1)


def dilated_window_attention(q, k, v, window, dilation):
    bsz, seq, nh, hd = q.shape
    reach = window // 2 // dilation
    blk = reach
    n_sub = seq // dilation
    nb = -(-n_sub // blk)
    n_pad = nb * blk

    def residues(t):
        return t.reshape(bsz, n_sub, dilation, nh, hd).transpose(0, 2, 1, 3, 4)

    qb = jnp.pad(residues(q), ((0, 0), (0, 0), (0, n_pad - n_sub), (0, 0), (0, 0)))
    qb = qb.reshape(bsz, dilation, nb, blk, nh, hd)

    def key_blocks(t):
        tp = jnp.pad(residues(t), ((0, 0), (0, 0), (blk, n_pad - n_sub + blk), (0, 0), (0, 0)))
        tp = tp.reshape(bsz, dilation, nb + 2, blk, nh, hd)
        return jnp.concatenate([tp[:, :, :-2], tp[:, :, 1:-1], tp[:, :, 2:]], axis=3)

    kb = key_blocks(k)
    vb = key_blocks(v)
    scores = jnp.einsum('brnqhe,brnkhe->brnhqk', qb, kb) / math.sqrt(hd)
    m_q = jnp.arange(nb)[:, None, None] * blk + jnp.arange(blk)[None, :, None]
    m_k = jnp.arange(nb)[:, None, None] * blk - blk + jnp.arange(3 * blk)[None, None, :]
    valid = (jnp.abs(m_k - m_q) <= reach) & (m_k >= 0) & (m_k < n_sub)
    scores = jnp.where(valid[None, None, :, None], scores, MASK_VALUE)
    lse = jax.nn.logsumexp(scores, axis=-1)
    probs = jnp.exp(scores - lse[..., None])
    out = jnp.einsum('brnhqk,brnkhe->brnqhe', probs, vb)
    out = out.reshape(bsz, dilation, n_pad, nh, hd)[:, :, :n_sub]
    out = out.transpose(0, 2, 1, 3, 4).reshape(bsz, seq, nh, hd)
    lse = lse.transpose(0, 1, 2, 4, 3).reshape(bsz, dilation, n_pad, nh)[:, :, :n_sub]
    lse = lse.transpose(0, 2, 1, 3).reshape(bsz, seq, nh)
    return out, lse


def dilated_attention_branch(q, k, v):
    bsz, seq, _ = q.shape
    shape = (bsz, seq, N_ATT_HEADS, ATT_HEAD_DIM)
    qh = partial_rope(q.astype(jnp.float32).reshape(shape))
    kh = partial_rope(k.astype(jnp.float32).reshape(shape))
    vh = v.astype(jnp.float32).reshape(shape)
    outs = []
    lses = []
    for g, (window, dilation) in enumerate(DILATED_PATTERNS):
        hs = slice(g * ATT_HEADS_PER_GROUP, (g + 1) * ATT_HEADS_PER_GROUP)
        o, l = dilated_window_attention(qh[:, :, hs], kh[:, :, hs], vh[:, :, hs], window, dilation)
        outs.append(o)
        lses.append(l)
    weights = jax.nn.softmax(jnp.stack(lses), axis=0)
    out = jnp.sum(weights[..., None] * jnp.stack(outs), axis=0)
    return out.reshape(bsz, seq, ATT_OUT_WIDTH).astype(q.dtype)


def _ssm_combine(e1, e2):
    a1r, a1i, b1r, b1i = e1
    a2r, a2i, b2r, b2i = e2
    return (a1r * a2r - a1i * a2i,
            a1r * a2i + a1i * a2r,
            a2r * b1r - a2i * b1i + b2r,
            a2r * b1i + a2i * b1r + b2i)


def ssm_scan_direction(u, lam_re, lam_im, log_step, b_re, b_im, c_re, c_im, reverse):
    seq = u.shape[1]
    lam_re = jnp.minimum(lam_re.astype(jnp.float32), LAMBDA_RE_MAX)
    lam_im = lam_im.astype(jnp.float32)
    dt = jnp.exp(log_step.astype(jnp.float32))[:, None]
    mag = jnp.exp(dt * lam_re)
    a_re = mag * jnp.cos(dt * lam_im)
    a_im = mag * jnp.sin(dt * lam_im)
    den = lam_re * lam_re + lam_im * lam_im
    z_re = ((a_re - 1.0) * lam_re + a_im * lam_im) / den
    z_im = (a_im * lam_re - (a_re - 1.0) * lam_im) / den
    b_re = b_re.astype(jnp.float32)
    b_im = b_im.astype(jnp.float32)
    bb_re = z_re[..., None] * b_re - z_im[..., None] * b_im
    bb_im = z_re[..., None] * b_im + z_im[..., None] * b_re
    bu_re = jnp.einsum('bsgc,gpc->bsgp', u, bb_re)
    bu_im = jnp.einsum('bsgc,gpc->bsgp', u, bb_im)
    a_re_t = jnp.broadcast_to(a_re, (seq,) + a_re.shape)
    a_im_t = jnp.broadcast_to(a_im, (seq,) + a_im.shape)

    def scan_one(br, bi):
        res = lax.associative_scan(_ssm_combine, (a_re_t, a_im_t, br, bi), reverse=reverse, axis=0)
        return res[2], res[3]

    x_re, x_im = jax.vmap(scan_one)(bu_re, bu_im)
    return (jnp.einsum('bsgp,gcp->bsgc', x_re, c_re.astype(jnp.float32))
            - jnp.einsum('bsgp,gcp->bsgc', x_im, c_im.astype(jnp.float32)))


def ssm_branch(u, lam_re, lam_im, log_step, b_re, b_im, c_re, c_im, d_skip, w_glu, b_glu):
    bsz, seq, _ = u.shape
    uf = u.astype(jnp.float32).reshape(bsz, seq, SSM_GROUPS, SSM_GROUP)
    y = d_skip.astype(jnp.float32).reshape(SSM_GROUPS, SSM_GROUP) * uf
    for direction in range(2):
        y = y + ssm_scan_direction(uf, lam_re[direction], lam_im[direction], log_step[direction],
                                   b_re[direction], b_im[direction], c_re[direction], c_im[direction],
                                   reverse=(direction == 1))
    y = y.reshape(bsz, seq, SSM_WIDTH)
    h = jax.nn.gelu(y, approximate=False)
    h = h * jax.nn.sigmoid(h @ w_glu.astype(jnp.float32) + b_glu.astype(jnp.float32))
    return h.astype(u.dtype)


def memory_branch(q, mem, w_mem_kv):
    bsz, seq, _ = q.shape
    n_mem = mem.shape[1]
    kv = (mem @ w_mem_kv).astype(jnp.float32)
    k, v = jnp.split(kv, 2, axis=-1)
    k = k.reshape(bsz, n_mem, MEM_HEADS, MEM_HEAD_DIM)
    v = v.reshape(bsz, n_mem, MEM_HEADS, MEM_HEAD_DIM)
    qh = q.astype(jnp.float32).reshape(bsz, seq, MEM_HEADS, MEM_HEAD_DIM)
    scores = jnp.einsum('bshe,bmhe->bhsm', qh, k) / math.sqrt(MEM_HEAD_DIM)
    probs = jax.nn.softmax(scores, axis=-1)
    out = jnp.einsum('bhsm,bmhe->bshe', probs, v)
    return out.reshape(bsz, seq, MEM_WIDTH).astype(q.dtype)


def peer_ffn(x, w_query, sub_keys, expert_u, expert_v):
    bsz, seq, d = x.shape
    keys_f = sub_keys.astype(jnp.float32)

    def mix_block(xt):
        nt = xt.shape[0]
        q = (xt @ w_query).astype(jnp.float32).reshape(nt, PEER_HEADS, 2, PEER_HALF)
        s = jnp.einsum('thie,ine->thin', q, keys_f)
        top_s, top_i = lax.top_k(s, PEER_TOPK)
        cand_s = top_s[:, :, 0, :, None] + top_s[:, :, 1, None, :]
        cand_i = top_i[:, :, 0, :, None] * PEER_N_KEYS + top_i[:, :, 1, None, :]
        cand_s = cand_s.reshape(nt, PEER_HEADS, PEER_TOPK * PEER_TOPK)
        cand_i = cand_i.reshape(nt, PEER_HEADS, PEER_TOPK * PEER_TOPK)
        best_s, best_pos = lax.top_k(cand_s, PEER_TOPK)
        experts = jnp.take_along_axis(cand_i, best_pos, axis=-1)
        gate = jax.nn.softmax(best_s, axis=-1)
        u_sel = jnp.take(expert_u, experts, axis=0)
        v_sel = jnp.take(expert_v, experts, axis=0)
        act = jax.nn.gelu(jnp.einsum('td,thkd->thk', xt, u_sel).astype(jnp.float32), approximate=False)
        return jnp.einsum('thk,thkd->td', (gate * act).astype(xt.dtype), v_sel)

    out = lax.map(mix_block, x.reshape(-1, PEER_TOKEN_BLOCK, d))
    return out.reshape(bsz, seq, d)


def encoder_layer(x, mem, w_in, b_gate, w_att_out, w_ssm_out, w_mem_out, w_mix_out, w_mem_kv,
                  lam_re, lam_im, log_step, b_re, b_im, c_re, c_im, d_skip, w_glu, b_glu,
                  ln1_g, ln1_b, w_query, sub_keys, expert_u, expert_v, ln2_g, ln2_b):
    bsz, seq, d = x.shape
    proj = x @ w_in
    q_a, k_a, v_a, u_s, q_m, gate_logits = jnp.split(proj, IN_SPLITS, axis=-1)
    gates = jax.nn.sigmoid((gate_logits + b_gate).astype(jnp.float32)).reshape(bsz, seq, N_BRANCHES, d)
    y_att = dilated_attention_branch(q_a, k_a, v_a) @ w_att_out
    y_ssm = ssm_branch(u_s, lam_re, lam_im, log_step, b_re, b_im, c_re, c_im, d_skip, w_glu, b_glu) @ w_ssm_out
    y_mem = memory_branch(q_m, mem, w_mem_kv) @ w_mem_out
    merged = gates[:, :, 0] * y_att + gates[:, :, 1] * y_ssm + gates[:, :, 2] * y_mem
    mixed = merged.astype(x.dtype) @ w_mix_out
    x = layer_norm(DEEPNORM_ALPHA * x + mixed, ln1_g, ln1_b)
    x = layer_norm(DEEPNORM_ALPHA * x + peer_ffn(x, w_query, sub_keys, expert_u, expert_v), ln2_g, ln2_b)
    return x


def run_trunk(x, mem, w_in, b_gate, w_att_out, w_ssm_out, w_mem_out, w_mix_out, w_mem_kv,
              lam_re, lam_im, log_step, b_re, b_im, c_re, c_im, d_skip, w_glu, b_glu,
              ln1_g, ln1_b, w_query, sub_keys, expert_u, expert_v, ln2_g, ln2_b):
    for l in range(DEPTH):
        x = encoder_layer(x, mem, w_in[l], b_gate[l], w_att_out[l], w_ssm_out[l], w_mem_out[l], w_mix_out[l],
                          w_mem_kv[l], lam_re[l], lam_im[l], log_step[l], b_re[l], b_im[l], c_re[l], c_im[l],
                          d_skip[l], w_glu[l], b_glu[l], ln1_g[l], ln1_b[l], w_query[l], sub_keys[l],
                          expert_u[l], expert_v[l], ln2_g[l], ln2_b[l])
    return x


def setup_inputs(seed: int = 0) -> dict:
    key = jax.random.key(seed)
    ks = jax.random.split(key, 29)
    f32 = jnp.float32

    def nrm(k, shape, scale):
        return jax.random.normal(k, shape, f32) * scale

    beta = DEEPNORM_BETA
    lam_im_base = jnp.pi * jnp.arange(SSM_STATE, dtype=f32)
    return {
        'x_prompt': nrm(ks[0], (BATCH, SEQ, D_MODEL), 1.0),
        'x_sample': nrm(ks[1], (DEC_BATCH, DEC_SEQ, D_MODEL), 1.0),
        'mem_prompt': nrm(ks[2], (BATCH, MEM_LEN, D_MODEL), 1.0),
        'mem_sample': nrm(ks[3], (DEC_BATCH, MEM_LEN, D_MODEL), 1.0),
        'w_in': nrm(ks[4], (DEPTH, D_MODEL, IN_WIDTH), D_MODEL ** -0.5),
        'b_gate': nrm(ks[5], (DEPTH, N_BRANCHES * D_MODEL), 0.1),
        'w_att_out': nrm(ks[6], (DEPTH, ATT_OUT_WIDTH, D_MODEL), beta * ATT_OUT_WIDTH ** -0.5),
        'w_ssm_out': nrm(ks[7], (DEPTH, SSM_WIDTH, D_MODEL), beta * SSM_WIDTH ** -0.5),
        'w_mem_out': nrm(ks[8], (DEPTH, MEM_WIDTH, D_MODEL), beta * MEM_WIDTH ** -0.5),
        'w_mix_out': nrm(ks[9], (DEPTH, D_MODEL, D_MODEL), beta * D_MODEL ** -0.5),
        'w_mem_kv': nrm(ks[10], (DEPTH, D_MODEL, 2 * MEM_WIDTH), D_MODEL ** -0.5),
        'lam_re': -0.5 + nrm(ks[11], (DEPTH, 2, SSM_GROUPS, SSM_STATE), 0.01),
        'lam_im': lam_im_base + nrm(ks[12], (DEPTH, 2, SSM_GROUPS, SSM_STATE), 0.01),
        'log_step': jax.random.uniform(ks[13], (DEPTH, 2, SSM_GROUPS), f32, math.log(DT_MIN), math.log(DT_MAX)),
        'b_re': nrm(ks[14], (DEPTH, 2, SSM_GROUPS, SSM_STATE, SSM_GROUP), (2 * SSM_GROUP) ** -0.5),
        'b_im': nrm(ks[15], (DEPTH, 2, SSM_GROUPS, SSM_STATE, SSM_GROUP), (2 * SSM_GROUP) ** -0.5),
        'c_re': nrm(ks[16], (DEPTH, 2, SSM_GROUPS, SSM_GROUP, SSM_STATE), SSM_STATE ** -0.5),
        'c_im': nrm(ks[17], (DEPTH, 2, SSM_GROUPS, SSM_GROUP, SSM_STATE), SSM_STATE ** -0.5),
        'd_skip': nrm(ks[18], (DEPTH, SSM_WIDTH), 1.0),
        'w_glu': nrm(ks[19], (DEPTH, SSM_WIDTH, SSM_WIDTH), SSM_WIDTH ** -0.5),
        'b_glu': nrm(ks[20], (DEPTH, SSM_WIDTH), 0.02),
        'ln1_g': 1.0 + nrm(ks[21], (DEPTH, D_MODEL), 0.02),
        'ln1_b': nrm(ks[22], (DEPTH, D_MODEL), 0.02),
        'w_query': nrm(ks[23], (DEPTH, D_MODEL, PEER_HEADS * PEER_QUERY_DIM), D_MODEL ** -0.5),
        'sub_keys': nrm(ks[24], (DEPTH, 2, PEER_N_KEYS, PEER_HALF), PEER_HALF ** -0.5),
        'expert_u': nrm(ks[25], (DEPTH, PEER_N_EXPERTS, D_MODEL), D_MODEL ** -0.5),
        'expert_v': nrm(ks[26], (DEPTH, PEER_N_EXPERTS, D_MODEL), beta * PEER_HEADS ** -0.5),
        'ln2_g': 1.0 + nrm(ks[27], (DEPTH, D_MODEL), 0.02),
        'ln2_b': nrm(ks[28], (DEPTH, D_MODEL), 0.02),
    }


def reference(x_prompt, x_sample, mem_prompt, mem_sample, w_in, b_gate, w_att_out, w_ssm_out, w_mem_out,
              w_mix_out, w_mem_kv, lam_re, lam_im, log_step, b_re, b_im, c_re, c_im, d_skip, w_glu, b_glu,
              ln1_g, ln1_b, w_query, sub_keys, expert_u, expert_v, ln2_g, ln2_b):
    y_prompt = run_trunk(x_prompt, mem_prompt, w_in, b_gate, w_att_out, w_ssm_out, w_mem_out, w_mix_out,
                         w_mem_kv, lam_re, lam_im, log_step, b_re, b_im, c_re, c_im, d_skip, w_glu, b_glu,
                         ln1_g, ln1_b, w_query, sub_keys, expert_u, expert_v, ln2_g, ln2_b)
    y_sample = run_trunk(x_sample, mem_sample, w_in, b_gate, w_att_out, w_ssm_out, w_mem_out, w_mix_out,
                         w_mem_kv, lam_re, lam_im, log_step, b_re, b_im, c_re, c_im, d_skip, w_glu, b_glu,
                         ln1_g, ln1_b, w_query, sub_keys, expert_u, expert_v, ln2_g, ln2_b)
    return (y_prompt, y_sample)
```

```python
import math
import numpy as np
import concourse.bass as bass
import concourse.mybir as mybir
from concourse.bass_utils import run_bass_kernel_spmd

F32 = mybir.dt.float32; BF16 = mybir.dt.bfloat16; U32 = mybir.dt.uint32
AF = mybir.ActivationFunctionType; ALU = mybir.AluOpType; AX = mybir.AxisListType

D = 1024; NSEG = 4; MEM = 256
ALPHA = 8.0 ** 0.25; EPS = 1e-5
PATTERNS = ((128, 1), (512, 4), (2048, 16))
GD = {0: (-1, 1), 1: (-2, 2), 2: (-8, 8)}
COMBOS = [(g, dl) for g in range(3) for dl in range(GD[g][0], GD[g][1] + 1)]
L = 16


class KB:
    NDMA = 24

    def __init__(self, nc):
        self.nc = nc
        self.eng = {"pe": nc.tensor, "act": nc.scalar, "dve": nc.vector, "pool": nc.gpsimd, "sp": nc.sync}
        self.sem = {}; self.cnt = {}
        for e in ["pe", "act", "dve", "pool"]:
            self.sem[e] = nc.semaphore("sem_" + e).__enter__(); self.cnt[e] = 0
        for i in range(self.NDMA):
            self.sem[("dma", i)] = nc.semaphore("sem_dma%d" % i).__enter__(); self.cnt[("dma", i)] = 0
        self.dma_rr = 0
        self.waited = {e: {} for e in ["pe", "act", "dve", "pool", "sp"]}
        self.last_w = {}; self.readers = {}; self.nops = 0

    def _deps(self, reads, writes):
        deps = []
        for b in list(reads) + list(writes):
            if b in self.last_w:
                deps.append(self.last_w[b])
        for b in writes:
            deps.extend(self.readers.get(b, []))
        return deps

    def _wait(self, e, deps):
        w = self.waited[e]; need = {}
        for (s, v) in deps:
            if v > w.get(s, 0) and v > need.get(s, 0):
                need[s] = v
        for s, v in need.items():
            self.eng[e].wait_ge(self.sem[s], v); w[s] = v

    def _record(self, tok, reads, writes):
        for b in writes:
            self.last_w[b] = tok; self.readers[b] = []
        for b in reads:
            self.readers.setdefault(b, []).append(tok)

    def op(self, e, fn, reads=(), writes=()):
        self._wait(e, self._deps(reads, writes))
        ins = fn()
        self.cnt[e] += 1
        ins.then_inc(self.sem[e], 1)
        self._record((e, self.cnt[e]), reads, writes)
        self.nops += 1
        return ins

    def dma(self, out, in_, reads=(), writes=(), **kw):
        s = ("dma", self.dma_rr); self.dma_rr = (self.dma_rr + 1) % self.NDMA
        deps = self._deps(reads, writes); deps.append((s, self.cnt[s]))
        self._wait("sp", deps)
        ins = self.nc.sync.dma_start(out=out, in_=in_, **kw)
        self.cnt[s] += 16
        ins.then_inc(self.sem[s], 16)
        self._record((s, self.cnt[s]), reads, writes)
        self.nops += 1

    def barrier(self):
        allc = [(k, v) for k, v in self.cnt.items() if v > 0]
        for e in ["pe", "act", "dve", "pool", "sp"]:
            self._wait(e, allc)

    def finish(self, bufs):
        self._wait("sp", [self.last_w[b] for b in bufs if b in self.last_w])


WNAMES = ["w_in", "b_gate", "w_att_out", "w_ssm_out", "w_mem_out", "w_mix_out", "w_mem_kv", "lam_re", "lam_im",
          "log_step", "b_re", "b_im", "c_re", "c_im", "d_skip", "w_glu", "b_glu", "ln1_g", "ln1_b", "w_query",
          "sub_keys", "expert_u", "expert_v", "ln2_g", "ln2_b"]


def build(SEG, DEPTH, wshapes, debug=()):
    T = NSEG * SEG; NT = T // 128; NB = T // 512; NCH = T // L; CPS = SEG // L
    assert CPS == 128 or SEG < 2048
    nc = bass.Bass("TRN2", target_bir_lowering=False)
    kb = KB(nc)
    op = kb.op; dma = kb.dma
    V = nc.vector; A = nc.scalar; G = nc.gpsimd; PE = nc.tensor

    def din(name, shape, dt=F32):
        return nc.dram_tensor(name, list(shape), dt, kind="ExternalInput").ap()

    def dscr(name, shape, dt):
        kind = "ExternalOutput" if name in debug else "Internal"
        return nc.dram_tensor(name, list(shape), dt, kind=kind).ap()

    x_in = din("x", [T, D]); mem_in = din("mem", [NSEG, MEM, D])
    W = {n: din(n, wshapes[n]) for n in WNAMES}
    c_ident = din("c_ident", [128, 128]); c_pm = din("c_pm", [128, 128])
    c_cos = din("c_cos", [128, T]); c_sin = din("c_sin", [128, T])
    c_amask = din("c_amask", [128, 25 * 128]); c_aflag = din("c_aflag", [128, NT * 17])
    c_lflag = din("c_lflag", [128, NSEG]); c_iota = din("c_iota", [128, 128])
    c_mL = din("c_mL", [128, 128]); c_mU = din("c_mU", [128, 128])
    y_out = nc.dram_tensor("y", [T, D], F32, kind="ExternalOutput").ap()

    xs_d = [dscr("xs0", [T, D], F32), dscr("xs1", [T, D], F32)]
    x1_d = dscr("x1_d", [T, D], F32)
    xT_d = dscr("xT_d", [8, 128, T], BF16); x1T_d = dscr("x1T_d", [8, 128, T], BF16)
    qk_d = dscr("qk_d", [1536, T], BF16); vu_d = dscr("vu_d", [T, 1280], BF16)
    h_d = dscr("h_d", [T, 512], BF16); am_d = dscr("am_d", [T, 768], BF16)
    UT_d = dscr("UT_d", [DEPTH, 128, 128, 1024], BF16); VS_d = dscr("VS_d", [DEPTH, 128, 128, 1024], BF16)

    stack = []

    uid = [0]

    def sb(name, shape, dt):
        uid[0] += 1
        cm = nc.sbuf_tensor("%s_%d" % (name, uid[0]), list(shape), dt); t = cm.__enter__(); stack.append(cm); return t

    def release(n0):
        kb.barrier()
        while len(stack) > n0:
            stack.pop().__exit__(None, None, None)

    PS = [nc.psum_tensor("ps%d" % i, [128, 512], F32).__enter__() for i in range(8)]
    psk = lambda b: ("ps", b)

    ident_f = sb("ident_f", [128, 128], F32); ident_b = sb("ident_b", [128, 128], BF16)
    pm_f = sb("pm_f", [128, 128], F32)
    amask = sb("amask", [128, 25, 128], BF16); aflag = sb("aflag", [128, NT * 17], F32)
    lflag = sb("lflag", [128, NSEG], F32); iota_f = sb("iota_f", [128, 128], F32); iota_b = sb("iota_b", [128, 128], BF16)
    mL = sb("mL", [128, 128], F32); mU = sb("mU", [128, 128], F32)
    nstg = len(stack)
    stg = sb("stg_c", [128, 25 * 128], F32)
    dma(ident_f[:], c_ident[:, :], writes=["ident_f"]); dma(pm_f[:], c_pm[:, :], writes=["pm_f"])
    dma(aflag[:], c_aflag[:, :], writes=["aflag"]); dma(lflag[:], c_lflag[:, :], writes=["lflag"])
    dma(iota_f[:], c_iota[:, :], writes=["iota_f"]); dma(mL[:], c_mL[:, :], writes=["mL"]); dma(mU[:], c_mU[:, :], writes=["mU"])
    dma(stg[:], c_amask[:, :], writes=["stg_c"])
    op("dve", lambda: V.tensor_copy(amask[:].rearrange("p a b -> p (a b)"), stg[:]), reads=["stg_c"], writes=["amask"])
    op("dve", lambda: V.tensor_copy(ident_b[:], ident_f[:]), reads=["ident_f"], writes=["ident_b"])
    op("dve", lambda: V.tensor_copy(iota_b[:], iota_f[:]), reads=["iota_f"], writes=["iota_b"])
    release(nstg)
    NG = len(stack)

    rr = {"ev": 0, "cast": 0}

    def evac(out, in_, reads, writes, eng=None):
        if eng is None:
            eng = ("act", "dve")[rr["ev"] % 2]; rr["ev"] += 1
        if eng == "act":
            op("act", lambda: A.copy(out, in_), reads=reads, writes=writes)
        else:
            op("dve", lambda: V.tensor_copy(out, in_), reads=reads, writes=writes)

    def cast(out, in_, reads, writes, eng=None):
        if eng is None:
            eng = ("pool", "dve", "act")[rr["cast"] % 3]; rr["cast"] += 1
        if eng == "pool":
            op("pool", lambda: G.tensor_copy(out, in_), reads=reads, writes=writes)
        elif eng == "dve":
            op("dve", lambda: V.tensor_copy(out, in_), reads=reads, writes=writes)
        else:
            op("act", lambda: A.copy(out, in_), reads=reads, writes=writes)

    def load_w_bf16(dst, dkey, src_ap, rows_chunks, ncols, stgt, skey):
        cw = min(ncols, 2048)
        i = 0
        for kc in range(rows_chunks):
            for c0 in range(0, ncols, cw):
                s = i % 2; i += 1; w_ = min(cw, ncols - c0)
                dma(stgt[:, s, 0:w_], src_ap[kc * 128:(kc + 1) * 128, c0:c0 + w_], writes=[(skey, s)])
                cast(dst[:, kc, c0:c0 + w_], stgt[:, s, 0:w_], reads=[(skey, s)], writes=[dkey])

    def bcast_row(dst, dkey, src_row_ap, n):
        dma(dst, src_row_ap.partition_broadcast(128), writes=[dkey])

    n0 = len(stack)
    ustg = sb("ustg", [128, 2, 1024], F32); ubf = sb("ubf", [128, 2, 1024], BF16)
    utsb = sb("utsb", [128, 2, 1024], BF16); vbf = sb("vbf", [128, 2, 1024], BF16); vstg = sb("vstg", [128, 2, 1024], F32)
    it = 0
    for l in range(DEPTH):
        for i in range(128):
            s = it % 2; b = it % 2; it += 1
            dma(ustg[:, s, :], W["expert_u"][l, i * 128:(i + 1) * 128, :], writes=[("ustg", s)])
            cast(ubf[:, s, :], ustg[:, s, :], reads=[("ustg", s)], writes=[("ubf", s)], eng="pool")
            pT = PS[b][:].bitcast(BF16)
            for kc in range(8):
                op("pe", lambda kc=kc: PE.transpose(pT[:, kc * 128:(kc + 1) * 128], ubf[:, s, kc * 128:(kc + 1) * 128], ident_b[:]),
                   reads=[("ubf", s), "ident_b"], writes=[psk(b)])
            evac(utsb[:, s, :], pT, reads=[psk(b)], writes=[("utsb", s)])
            dma(UT_d[l, i], utsb[:, s, :], reads=[("utsb", s)], writes=["UT_d"])
            dma(vstg[:, s, :], W["expert_v"][l, i * 128:(i + 1) * 128, :], writes=[("vstg", s)])
            cast(vbf[:, s, :], vstg[:, s, :], reads=[("vstg", s)], writes=[("vbf", s)], eng="dve" if i % 2 else "pool")
            dma(VS_d[l, i], vbf[:, s, :], reads=[("vbf", s)], writes=["VS_d"])
    release(n0)

    NG2 = len(stack)

    for l in range(DEPTH):
        x_src = x_in if l == 0 else xs_d[(l - 1) % 2]
        x_dst = y_out if l == DEPTH - 1 else xs_d[l % 2]
        xsk = "xsrc%d" % l; xdk = "xsrc%d" % (l + 1)

        n0 = len(stack)
        wA = sb("wA", [128, 8, 2816], BF16); wstg = sb("wstgA", [128, 2, 2048], F32)
        load_w_bf16(wA, "wA", W["w_in"][l][:, 0:2816], 8, 2816, wstg, "wstgA")
        xf = sb("xfA", [128, 2, 1024], F32); xb = sb("xbA", [128, 2, 1024], BF16)
        xT = sb("xTA", [128, 2, 8, 512], BF16)
        cs = sb("csA", [128, 2, 512], F32); sn = sb("snA", [128, 2, 512], F32)
        qs = sb("qsA", [128, 2, 512], F32); t1 = sb("t1A", [128, 2, 512], F32); t2 = sb("t2A", [128, 2, 512], F32)
        qr = sb("qrA", [128, 2, 512], BF16); vu = sb("vuA", [128, 2, 1280], BF16)
        pb = 0
        for bi in range(NB):
            t0 = bi * 512; xs_ = bi % 2
            dma(cs[:, xs_, :], c_cos[:, t0:t0 + 512], writes=[("csA", xs_)])
            dma(sn[:, xs_, :], c_sin[:, t0:t0 + 512], writes=[("snA", xs_)])
            for tt in range(4):
                s = tt % 2
                dma(xf[:, s, :], x_src[t0 + tt * 128:t0 + (tt + 1) * 128, :], reads=[xsk], writes=[("xfA", s)])
                cast(xb[:, s, :], xf[:, s, :], reads=[("xfA", s)], writes=[("xbA", s)])
                b = pb % 8; pb += 1
                pT = PS[b][:].bitcast(BF16)
                for kc in range(8):
                    op("pe", lambda kc=kc: PE.transpose(pT[:, kc * 128:(kc + 1) * 128], xb[:, s, kc * 128:(kc + 1) * 128], ident_b[:]),
                       reads=[("xbA", s), "ident_b"], writes=[psk(b)])
                evac(xT[:, xs_, :, tt * 128:(tt + 1) * 128], pT.rearrange("p (k t) -> p k t", k=8), reads=[psk(b)], writes=[("xTA", xs_)])
            dma(xT_d[:, :, t0:t0 + 512].rearrange("k p t -> p k t"), xT[:, xs_, :, :], reads=[("xTA", xs_)], writes=["xT_d"])
            for c in range(12):
                b = pb % 8; pb += 1; s = c % 2
                for kc in range(8):
                    op("pe", lambda kc=kc: PE.matmul(PS[b][:], wA[:, kc, c * 128:(c + 1) * 128], xT[:, xs_, kc, :], start=(kc == 0), stop=(kc == 7)),
                       reads=["wA", ("xTA", xs_)], writes=[psk(b)])
                op("act", lambda: A.copy(qs[:, s, :], PS[b][:]), reads=[psk(b)], writes=[("qsA", s)])
                b2 = pb % 8; pb += 1
                op("pe", lambda: PE.matmul(PS[b2][:], pm_f[:], qs[:, s, :], start=True, stop=True), reads=["pm_f", ("qsA", s)], writes=[psk(b2)])
                op("dve", lambda: V.tensor_tensor(t1[:, s, :], qs[:, s, :], cs[:, xs_, :], op=ALU.mult), reads=[("qsA", s), ("csA", xs_)], writes=[("t1A", s)])
                op("dve", lambda: V.tensor_tensor(t2[:, s, :], PS[b2][:], sn[:, xs_, :], op=ALU.mult), reads=[psk(b2), ("snA", xs_)], writes=[("t2A", s)])
                op("pool", lambda: G.tensor_tensor(qr[:, s, :], t1[:, s, :], t2[:, s, :], op=ALU.add), reads=[("t1A", s), ("t2A", s)], writes=[("qrA", s)])
                dma(qk_d[c * 128:(c + 1) * 128, t0:t0 + 512], qr[:, s, :], reads=[("qrA", s)], writes=["qk_d"])
            for tt in range(4):
                s = tt % 2
                for (c0, c1, o0) in ((1536, 2048, 0), (2048, 2304, 512), (2304, 2816, 768)):
                    b = pb % 8; pb += 1
                    for kc in range(8):
                        op("pe", lambda kc=kc: PE.matmul(PS[b][:, 0:c1 - c0], xT[:, xs_, kc, tt * 128:(tt + 1) * 128], wA[:, kc, c0:c1], start=(kc == 0), stop=(kc == 7)),
                           reads=["wA", ("xTA", xs_)], writes=[psk(b)])
                    evac(vu[:, s, o0:o0 + c1 - c0], PS[b][:, 0:c1 - c0], reads=[psk(b)], writes=[("vuA", s)])
                dma(vu_d[t0 + tt * 128:t0 + (tt + 1) * 128, :], vu[:, s, :], reads=[("vuA", s)], writes=["vu_d"])
        release(n0)

        for gh in range(2):
            n0 = len(stack)
            NGR = 16; g0 = gh * 16
            Tm = sb("Tm", [128, NGR, 3, 128], BF16)
            WB = sb("WB", [128, NGR, 2, 2, 128], BF16)
            CK = sb("CK", [128, NGR, 2, 256], BF16)
            dcol = sb("dcol", [128, NGR], F32)
            Z = sb("Z", [128, 2, NGR], F32); A1 = sb("A1", [128, 2, NGR], F32); A2 = sb("A2", [128, 2, NGR], F32)
            tA = sb("tA", [128, 2, NGR], F32); tB2 = sb("tB2", [128, 2, NGR], F32); Zc = sb("Zc", [128, 2, NGR], F32)
            nT = len(stack)
            lre = sb("lre", [128, NGR], F32); lim = sb("lim", [128, NGR], F32); dt_ = sb("dt", [128, NGR], F32)
            for d_ in range(2):
                dma(lre[d_ * 64:(d_ + 1) * 64, :], W["lam_re"][l, d_, g0:g0 + 16].rearrange("g p -> p g"), writes=["lre"], allow_slow_non_contiguous=True)
                dma(lim[d_ * 64:(d_ + 1) * 64, :], W["lam_im"][l, d_, g0:g0 + 16].rearrange("g p -> p g"), writes=["lim"], allow_slow_non_contiguous=True)
                dma(dt_[d_ * 64:(d_ + 1) * 64, :], W["log_step"][l, d_, g0:g0 + 16].partition_broadcast(64), writes=["dt"])
            Br = sb("Br", [128, NGR, 16], F32); Bi = sb("Bi", [128, NGR, 16], F32)
            for d_ in range(2):
                dma(Br[d_ * 64:(d_ + 1) * 64, :, :], W["b_re"][l, d_, g0:g0 + 16].rearrange("g p c -> p g c"), writes=["Br"])
                dma(Bi[d_ * 64:(d_ + 1) * 64, :, :], W["b_im"][l, d_, g0:g0 + 16].rearrange("g p c -> p g c"), writes=["Bi"])
            Cr = sb("Cr", [128, NGR, 16], F32); Ci = sb("Ci", [128, NGR, 16], F32)
            cstg = sb("cstg", [128, 2, 128], F32)
            it = 0
            for (src, dstt, dk) in ((W["c_re"], Cr, "Cr"), (W["c_im"], Ci, "Ci")):
                for gq in range(2):
                    b = it % 8; s = it % 2; it += 1
                    for d_ in range(2):
                        dma(cstg[:, s, d_ * 64:(d_ + 1) * 64], src[l, d_, g0 + gq * 8:g0 + (gq + 1) * 8].rearrange("g c p -> (g c) p"), writes=[("cstg", s)])
                    op("pe", lambda: PE.transpose(PS[b][:, 0:128], cstg[:, s, :], ident_f[:]), reads=[("cstg", s), "ident_f"], writes=[psk(b)])
                    evac(dstt[:, gq * 8:(gq + 1) * 8, :].rearrange("p g c -> p (g c)"), PS[b][:, 0:128], reads=[psk(b)], writes=[dk])
            dsk = sb("dsk", [128, 4], F32)
            sA = lambda nm: sb(nm, [128, NGR], F32)
            xr_ = sA("xr_"); xi_ = sA("xi_"); mag = sA("mag"); are = sA("are"); aim = sA("aim"); kk = sA("kk"); tq = sA("tq"); yy = sA("yy")
            op("act", lambda: A.activation(dt_[:], dt_[:], AF.Exp), reads=["dt"], writes=["dt"])
            op("dve", lambda: V.tensor_scalar(lre[:], lre[:], -1e-4, None, op0=ALU.min), reads=["lre"], writes=["lre"])
            op("dve", lambda: V.tensor_tensor(xr_[:], dt_[:], lre[:], op=ALU.mult), reads=["dt", "lre"], writes=["xr_"])
            op("dve", lambda: V.tensor_tensor(xi_[:], dt_[:], lim[:], op=ALU.mult), reads=["dt", "lim"], writes=["xi_"])
            op("act", lambda: A.activation(mag[:], xr_[:], AF.Exp), reads=["xr_"], writes=["mag"])

            def sin_of(dst, dkey, shift):
                op("dve", lambda: V.tensor_scalar(yy[:], xi_[:], float(shift), None, op0=ALU.add), reads=["xi_"], writes=["yy"])
                op("dve", lambda: V.memset(kk[:], 0.0), writes=["kk"])
                for j in range(1, 6):
                    op("dve", lambda j=j: V.tensor_scalar(tq[:], yy[:], float((2 * j - 1) * math.pi), None, op0=ALU.is_ge), reads=["yy"], writes=["tq"])
                    op("dve", lambda: V.tensor_tensor(kk[:], kk[:], tq[:], op=ALU.add), reads=["kk", "tq"], writes=["kk"])
                op("dve", lambda: V.scalar_tensor_tensor(yy[:], kk[:], float(-2 * math.pi), yy[:], op0=ALU.mult, op1=ALU.add), reads=["kk", "yy"], writes=["yy"])
                op("act", lambda: A.activation(dst[:], yy[:], AF.Sin), reads=["yy"], writes=[dkey])
            sin_of(aim, "aim", 0.0); sin_of(are, "are", math.pi / 2)
            op("dve", lambda: V.tensor_tensor(are[:], are[:], mag[:], op=ALU.mult), reads=["are", "mag"], writes=["are"])
            op("dve", lambda: V.tensor_tensor(aim[:], aim[:], mag[:], op=ALU.mult), reads=["aim", "mag"], writes=["aim"])
            den = sA("den"); zr = sA("zr"); zi = sA("zi"); am1 = sA("am1"); tz = sA("tz")
            op("dve", lambda: V.tensor_tensor(den[:], lre[:], lre[:], op=ALU.mult), reads=["lre"], writes=["den"])
            op("dve", lambda: V.tensor_tensor(tz[:], lim[:], lim[:], op=ALU.mult), reads=["lim"], writes=["tz"])
            op("dve", lambda: V.tensor_tensor(den[:], den[:], tz[:], op=ALU.add), reads=["den", "tz"], writes=["den"])
            op("dve", lambda: V.reciprocal(den[:], den[:]), reads=["den"], writes=["den"])
            op("dve", lambda: V.tensor_scalar(am1[:], are[:], -1.0, None, op0=ALU.add), reads=["are"], writes=["am1"])
            op("dve", lambda: V.tensor_tensor(zr[:], am1[:], lre[:], op=ALU.mult), reads=["am1", "lre"], writes=["zr"])
            op("dve", lambda: V.tensor_tensor(tz[:], aim[:], lim[:], op=ALU.mult), reads=["aim", "lim"], writes=["tz"])
            op("dve", lambda: V.tensor_tensor(zr[:], zr[:], tz[:], op=ALU.add), reads=["zr", "tz"], writes=["zr"])
            op("dve", lambda: V.tensor_tensor(zr[:], zr[:], den[:], op=ALU.mult), reads=["zr", "den"], writes=["zr"])
            op("dve", lambda: V.tensor_tensor(zi[:], aim[:], lre[:], op=ALU.mult), reads=["aim", "lre"], writes=["zi"])
            op("dve", lambda: V.tensor_tensor(tz[:], am1[:], lim[:], op=ALU.mult), reads=["am1", "lim"], writes=["tz"])
            op("dve", lambda: V.tensor_tensor(zi[:], zi[:], tz[:], op=ALU.subtract), reads=["zi", "tz"], writes=["zi"])
            op("dve", lambda: V.tensor_tensor(zi[:], zi[:], den[:], op=ALU.mult), reads=["zi", "den"], writes=["zi"])

            def cmul(orr, oi, ork, oik, ar, ai, ark, aik, br, bi, brk, bik, tmpa, tmpak, eng="dve"):
                E = V if eng == "dve" else G
                op(eng, lambda: E.tensor_tensor(tmpa, ai, bi, op=ALU.mult), reads=aik + bik, writes=[tmpak])
                op(eng, lambda: E.tensor_tensor(orr, ar, br, op=ALU.mult), reads=ark + brk, writes=[ork])
                op(eng, lambda: E.tensor_tensor(orr, orr, tmpa, op=ALU.subtract), reads=[ork, tmpak], writes=[ork])
                op(eng, lambda: E.tensor_tensor(tmpa, ai, br, op=ALU.mult), reads=aik + brk, writes=[tmpak])
                op(eng, lambda: E.tensor_tensor(oi, ar, bi, op=ALU.mult), reads=ark + bik, writes=[oik])
                op(eng, lambda: E.tensor_tensor(oi, oi, tmpa, op=ALU.add), reads=[oik, tmpak], writes=[oik])
            Bbr = sb("Bbr", [128, NGR, 16], F32); Bbi = sb("Bbi", [128, NGR, 16], F32); tB = sb("tB", [128, NGR, 16], F32)
            zb = lambda t: t[:].unsqueeze(2).to_broadcast([128, NGR, 16])
            cmul(Bbr[:], Bbi[:], "Bbr", "Bbi", zb(zr), zb(zi), ["zr"], ["zi"], Br[:], Bi[:], ["Br"], ["Bi"], tB[:], "tB")
            pwr = sb("pwr", [128, NGR, 33], F32); pwi = sb("pwi", [128, NGR, 33], F32)
            ipr = sb("ipr", [128, NGR, 17], F32); ipi = sb("ipi", [128, NGR, 17], F32)
            air = sA("air"); aii = sA("aii"); tp = sA("tp")
            op("dve", lambda: V.tensor_tensor(tz[:], mag[:], mag[:], op=ALU.mult), reads=["mag"], writes=["tz"])
            op("dve", lambda: V.reciprocal(tz[:], tz[:]), reads=["tz"], writes=["tz"])
            op("dve", lambda: V.tensor_tensor(air[:], are[:], tz[:], op=ALU.mult), reads=["are", "tz"], writes=["air"])
            op("dve", lambda: V.scalar_tensor_tensor(aii[:], aim[:], -1.0, tz[:], op0=ALU.mult, op1=ALU.mult), reads=["aim", "tz"], writes=["aii"])
            op("dve", lambda: V.memset(pwr[:, :, 0:1], 1.0), writes=[("pw", 0)]); op("dve", lambda: V.memset(pwi[:, :, 0:1], 0.0), writes=[("pwi_", 0)])
            op("pool", lambda: G.memset(ipr[:, :, 0:1], 1.0), writes=[("ip", 0)]); op("pool", lambda: G.memset(ipi[:, :, 0:1], 0.0), writes=[("ipi_", 0)])
            tp2 = sA("tp2")
            for m in range(32):
                cmul(pwr[:, :, m + 1], pwi[:, :, m + 1], ("pw", m + 1), ("pwi_", m + 1), pwr[:, :, m], pwi[:, :, m], [("pw", m)], [("pwi_", m)],
                     are[:], aim[:], ["are"], ["aim"], tp[:], "tp")
            for m in range(16):
                cmul(ipr[:, :, m + 1], ipi[:, :, m + 1], ("ip", m + 1), ("ipi_", m + 1), ipr[:, :, m], ipi[:, :, m], [("ip", m)], [("ipi_", m)],
                     air[:], aii[:], ["air"], ["aii"], tp2[:], "tp2", eng="pool")
            PWK = [("pw", m) for m in range(33)] + [("pwi_", m) for m in range(33)]
            IPK = [("ip", m) for m in range(17)] + [("ipi_", m) for m in range(17)]
            rpr = sb("rpr", [128, NGR, 16], F32); rpi = sb("rpi", [128, NGR, 16], F32); tR = sb("tR", [128, NGR, 16], F32)
            a16r = pwr[:, :, 16:17].to_broadcast([128, NGR, 16]); a16i = pwi[:, :, 16:17].to_broadcast([128, NGR, 16])
            cmul(rpr[:], rpi[:], "rpr", "rpi", a16r, a16i, PWK, PWK, ipr[:, :, 0:16], ipi[:, :, 0:16], IPK, IPK, tR[:], "tR")
            op("dve", lambda: V.tensor_copy(A1[:, 0, :], pwr[:, :, 16]), reads=PWK, writes=["A1"])
            op("dve", lambda: V.tensor_copy(A1[:, 1, :], pwr[:, :, 16]), reads=PWK, writes=["A1"])
            op("dve", lambda: V.tensor_scalar(A2[:, 0, :], pwi[:, :, 16], -1.0, None, op0=ALU.mult), reads=PWK, writes=["A2"])
            op("dve", lambda: V.tensor_copy(A2[:, 1, :], pwi[:, :, 16]), reads=PWK, writes=["A2"])
            tBr = sb("tBr", [128, NGR, 16], F32); tBi = sb("tBi", [128, NGR, 16], F32)
            tCr = sb("tCr", [128, NGR, 16], F32); tCi = sb("tCi", [128, NGR, 16], F32)
            tKr = sb("tKr", [128, NGR, 16], F32); tKi = sb("tKi", [128, NGR, 16], F32)
            F_ = slice(0, 64); B_ = slice(64, 128)
            cp = lambda dst, src, rk, wk, e="dve": op(e, (lambda: V.tensor_copy(dst, src)) if e == "dve" else (lambda: G.tensor_copy(dst, src)), reads=rk, writes=[wk])
            cp(tBr[F_], ipr[F_, :, 0:16], IPK, "tBr"); cp(tBi[F_], ipi[F_, :, 0:16], IPK, "tBi")
            cp(tBr[B_], pwr[B_, :, 0:16], PWK, "tBr", "pool"); cp(tBi[B_], pwi[B_, :, 0:16], PWK, "tBi", "pool")
            cp(tCr[F_], pwr[F_, :, 0:16], PWK, "tCr"); cp(tCi[F_], pwi[F_, :, 0:16], PWK, "tCi")
            cp(tCr[B_, :, 0:8], ipr[B_, :, 0:8], IPK, "tCr", "pool"); cp(tCi[B_, :, 0:8], ipi[B_, :, 0:8], IPK, "tCi", "pool")
            cp(tCr[B_, :, 8:16], rpr[B_, :, 8:16], ["rpr"], "tCr", "pool"); cp(tCi[B_, :, 8:16], rpi[B_, :, 8:16], ["rpi"], "tCi", "pool")
            cp(tKr[F_], pwr[F_, :, 16:32], PWK, "tKr"); cp(tKi[F_], pwi[F_, :, 16:32], PWK, "tKi")
            cp(tKr[B_], rpr[B_], ["rpr"], "tKr", "pool"); cp(tKi[B_], rpi[B_], ["rpi"], "tKi", "pool")
            for tl in range(8):
                dma(dcol[tl * 16:(tl + 1) * 16, :], W["d_skip"][l, g0 * 16:(g0 + 16) * 16].rearrange("(g c) -> c g", c=16), writes=["dcol"], allow_slow_non_contiguous=True)
            GB = 2
            EBr = sb("EBr", [128, GB, 16, 16], F32); EBi = sb("EBi", [128, GB, 16, 16], F32)
            CTr = sb("CTr", [128, GB, 16, 16], F32); CTi = sb("CTi", [128, GB, 16, 16], F32)
            CKr = sb("CKr", [128, GB, 16, 16], F32); CKi = sb("CKi", [128, GB, 16, 16], F32)
            tE = sb("tE", [128, GB, 16, 16], F32)
            tsb = sb("tsbT", [128, 2, 128], F32)
            pb = 0
            for gb in range(NGR // GB):
                gs = slice(gb * GB, (gb + 1) * GB)
                pwb = lambda t: t[:, gs, :].unsqueeze(3).to_broadcast([128, GB, 16, 16])
                vb = lambda t: t[:, gs, :].unsqueeze(2).to_broadcast([128, GB, 16, 16])
                cmul(EBr[:], EBi[:], "EBr", "EBi", pwb(tBr), pwb(tBi), ["tBr"], ["tBi"], vb(Bbr), vb(Bbi), ["Bbr"], ["Bbi"], tE[:], "tE")
                cmul(CTr[:], CTi[:], "CTr", "CTi", pwb(tCr), pwb(tCi), ["tCr"], ["tCi"], vb(Cr), vb(Ci), ["Cr"], ["Ci"], tE[:], "tE")
                cmul(CKr[:], CKi[:], "CKr", "CKi", pwb(tKr), pwb(tKi), ["tKr"], ["tKi"], vb(Cr), vb(Ci), ["Cr"], ["Ci"], tE[:], "tE")
                op("dve", lambda: V.tensor_scalar(CTi[:], CTi[:], -1.0, None, op0=ALU.mult), reads=["CTi"], writes=["CTi"])
                op("dve", lambda: V.tensor_scalar(CKi[:], CKi[:], -1.0, None, op0=ALU.mult), reads=["CKi"], writes=["CKi"])
                op("act", lambda: A.copy(CK[:, gs, 0, :], CKr[:].rearrange("p g t c -> p g (t c)")), reads=["CKr"], writes=["CK"])
                op("act", lambda: A.copy(CK[:, gs, 1, :], CKi[:].rearrange("p g t c -> p g (t c)")), reads=["CKi"], writes=["CK"])
                for gl in range(GB):
                    g = gb * GB + gl
                    bF = pb % 8; pb += 1; bB = pb % 8; pb += 1
                    eb = lambda t, pr: t[pr, gl, 0:8, :].rearrange("p t c -> p (t c)")
                    ca = lambda t, pr: t[pr, gl, :, :].rearrange("p t c -> p (t c)")
                    for (bank, pr) in ((bF, F_), (bB, B_)):
                        op("pe", lambda bank=bank, pr=pr: PE.matmul(PS[bank][:, 0:256], eb(EBr, pr), ca(CTr, pr), start=True, stop=False),
                           reads=["EBr", "CTr"], writes=[psk(bank)])
                        op("pe", lambda bank=bank, pr=pr: PE.matmul(PS[bank][:, 0:256], eb(EBi, pr), ca(CTi, pr), start=False, stop=True),
                           reads=["EBi", "CTi"], writes=[psk(bank)])
                    s = g % 2
                    op("dve", lambda: V.tensor_tensor(tsb[:, s, :], PS[bF][:, 0:128], mL[:], op=ALU.mult), reads=[psk(bF), "mL"], writes=[("tsbT", s)])
                    op("dve", lambda: V.scalar_tensor_tensor(tsb[:, s, :], ident_f[:], dcol[:, g:g + 1], tsb[:, s, :], op0=ALU.mult, op1=ALU.add),
                       reads=["ident_f", "dcol", ("tsbT", s)], writes=[("tsbT", s)])
                    op("dve", lambda: V.tensor_tensor(Tm[:, g, 0, :], PS[bB][:, 0:128], mU[:], op=ALU.mult), reads=[psk(bB), "mU"], writes=["Tm"])
                    op("pool", lambda: G.tensor_tensor(Tm[:, g, 0, :], Tm[:, g, 0, :], tsb[:, s, :], op=ALU.add), reads=["Tm", ("tsbT", s)], writes=["Tm"])
                    op("act", lambda: A.copy(Tm[:, g, 1, :], PS[bF][:, 128:256]), reads=[psk(bF)], writes=["Tm"])
                    op("act", lambda: A.copy(Tm[:, g, 2, :], PS[bB][:, 128:256]), reads=[psk(bB)], writes=["Tm"])
                    bW = pb % 8; pb += 1
                    for tb in range(2):
                        for ri, EBt, ek in ((0, EBr, "EBr"), (1, EBi, "EBi")):
                            op("pe", lambda tb=tb, ri=ri, EBt=EBt: PE.transpose(PS[bW][:, (tb * 2 + ri) * 128:(tb * 2 + ri + 1) * 128],
                                                                                  EBt[:, gl, tb * 8:(tb + 1) * 8, :].rearrange("p t c -> p (t c)"), ident_f[:]),
                               reads=[ek, "ident_f"], writes=[psk(bW)])
                    evac(WB[:, g, :, :, :].rearrange("p a b m -> p (a b m)"), PS[bW][:], reads=[psk(bW)], writes=["WB"])
            release(nT)
            SX = sb("SX", [128, NCH, 2, NGR], BF16)
            ucm = sb("ucm", [128, L * 256], BF16)
            ucp = sb("ucp", [128, L * 256], BF16)
            ucpv = ucp[:].rearrange("p (g x) -> p g x", g=NGR)
            U3 = sb("U3", [128, NGR, 2, 128], BF16)
            ucv = ucm[:].rearrange("p (t g c) -> p t g c", t=L, g=NGR)
            NSEGC = NCH // 128 if NCH >= 128 else 1
            NPT = min(128, NCH)

            def load_U3(ti):
                dma(ucm[0:NPT, :].rearrange("p (t c) -> p t c", t=L), vu_d[ti * NPT * L:(ti + 1) * NPT * L, 768 + g0 * 16:768 + g0 * 16 + 256].rearrange("(n t) c -> n t c", t=L), reads=["vu_d"], writes=["ucm"])
                op("pool", lambda: G.tensor_copy(ucp[0:NPT, :].rearrange("p (g t c) -> p g t c", g=NGR, t=L), ucm[0:NPT, :].rearrange("p (t g c) -> p g t c", t=L, g=NGR)), reads=["ucm"], writes=["ucp"])
                pbl = 0
                for g in range(NGR):
                    if g % 4 == 0:
                        bq = pbl % 4; pbl += 1
                        pT = PS[bq][:].bitcast(BF16)
                    for tb in range(2):
                        o = ((g % 4) * 2 + tb) * 128
                        op("pe", lambda g=g, tb=tb, o=o, pT=pT: PE.transpose(pT[:, o:o + NPT], ucpv[0:NPT, g, tb * 128:(tb + 1) * 128], ident_b[0:NPT, 0:NPT]),
                           reads=["ucp", "ident_b"], writes=[psk(bq)])
                    if g % 4 == 3:
                        evac(U3[:, g - 3:g + 1, :, 0:NPT], pT.rearrange("p (g t n) -> p g t n", g=4, t=2)[:, :, :, 0:NPT], reads=[psk(bq)], writes=["U3"])
            for ti in range(NCH // NPT):
                load_U3(ti)
                for g in range(NGR):
                    b = 4 + (g % 4)
                    for ri in range(2):
                        for tb in range(2):
                            op("pe", lambda g=g, ri=ri, tb=tb: PE.matmul(PS[b][:, ri * 128:ri * 128 + NPT], WB[:, g, tb, ri, :], U3[:, g, tb, 0:NPT], start=(tb == 0), stop=(tb == 1)),
                               reads=["WB", "U3"], writes=[psk(b)])
                    evac(SX[:, ti * NPT:(ti + 1) * NPT, :, g].rearrange("p n r -> p r n"), PS[b][:, 0:256].rearrange("p (r n) -> p r n", r=2)[:, :, 0:NPT],
                         reads=[psk(b)], writes=["SX"])
            op("dve", lambda: V.memset(Z[:], 0.0), writes=["Z"])
            for k in range(NCH):
                kf = k; kbw = NCH - 1 - k
                if k % CPS == 0 and k > 0:
                    sgi = k // CPS
                    op("dve", lambda sgi=sgi: V.tensor_scalar(Z[:], Z[:], lflag[:, sgi:sgi + 1], None, op0=ALU.mult), reads=["Z", "lflag"], writes=["Z"])
                op("dve", lambda: V.tensor_tensor(tA[:], A1[:], Z[:], op=ALU.mult), reads=["A1", "Z"], writes=["tA"])
                op("dve", lambda: V.tensor_tensor(tB2[:, 0, :], A2[:, 0, :], Z[:, 1, :], op=ALU.mult), reads=["A2", "Z"], writes=["tB2"])
                op("dve", lambda: V.tensor_tensor(tB2[:, 1, :], A2[:, 1, :], Z[:, 0, :], op=ALU.mult), reads=["A2", "Z"], writes=["tB2"])
                op("dve", lambda: V.tensor_tensor(tA[:], tA[:], tB2[:], op=ALU.add), reads=["tA", "tB2"], writes=["tA"])
                op("pool", lambda: G.tensor_copy(Zc[F_], Z[F_]), reads=["Z"], writes=["Zc"])
                op("pool", lambda: G.tensor_copy(Zc[B_], Z[B_]), reads=["Z"], writes=["Zc"])
                op("dve", lambda: V.tensor_tensor(Z[F_], tA[F_], SX[F_, kf, :, :], op=ALU.add), reads=["tA", "SX", "Zc"], writes=["Z"])
                op("dve", lambda: V.tensor_tensor(Z[B_], tA[B_], SX[B_, kbw, :, :], op=ALU.add), reads=["tA", "SX", "Zc"], writes=["Z"])
                op("pool", lambda: G.tensor_copy(SX[F_, kf, :, :], Zc[F_]), reads=["Zc", "Z"], writes=["SX"])
                op("pool", lambda: G.tensor_copy(SX[B_, kbw, :, :], Zc[B_]), reads=["Zc", "Z"], writes=["SX"])
            ysb = sb("ysb", [128, 2, 2, 128], BF16)
            hcm = ucm
            hcv = hcm[:].rearrange("p (t g c) -> p t g c", t=L, g=NGR)
            for ti in range(NCH // NPT):
                load_U3(ti)
                for g in range(NGR):
                    s = g % 2
                    for tbo in range(2):
                        b = 4 + (g * 2 + tbo) % 4
                        mm = []
                        mm.append((Tm[:, g, 0, :], U3[:, g, tbo, 0:NPT]))
                        if tbo == 1:
                            mm.append((Tm[:, g, 1, :], U3[:, g, 0, 0:NPT]))
                        else:
                            mm.append((Tm[:, g, 2, :], U3[:, g, 1, 0:NPT]))
                        for ri in range(2):
                            mm.append((CK[F_, g, ri, tbo * 128:(tbo + 1) * 128], SX[F_, ti * NPT:(ti + 1) * NPT, ri, g]))
                            mm.append((CK[B_, g, ri, tbo * 128:(tbo + 1) * 128], SX[B_, ti * NPT:(ti + 1) * NPT, ri, g]))
                        for q, (lh, rh) in enumerate(mm):
                            op("pe", lambda lh=lh, rh=rh, q=q: PE.matmul(PS[b][:, 0:NPT], lh, rh, start=(q == 0), stop=(q == len(mm) - 1)),
                               reads=["Tm", "U3", "CK", "SX"], writes=[psk(b)])
                        op("act", lambda tbo=tbo: A.activation(ysb[:, s, tbo, 0:NPT], PS[b][:, 0:NPT], AF.Gelu), reads=[psk(b)], writes=[("ysb", s, tbo)])
                    bq = (g % 2) * 2
                    for tbo in range(2):
                        pT = PS[bq + tbo][:].bitcast(BF16)
                        op("pe", lambda tbo=tbo, pT=pT: PE.transpose(pT[0:NPT, 0:128], ysb[:, s, tbo, 0:NPT], ident_b[:]), reads=[("ysb", s, tbo), "ident_b"], writes=[psk(bq + tbo)])
                        evac(ucpv[0:NPT, g, tbo * 128:(tbo + 1) * 128], pT[0:NPT, 0:128], reads=[psk(bq + tbo)], writes=["ucp"])
                op("pool", lambda: G.tensor_copy(ucm[0:NPT, :].rearrange("p (t g c) -> p g t c", t=L, g=NGR), ucp[0:NPT, :].rearrange("p (g t c) -> p g t c", g=NGR, t=L)), reads=["ucp"], writes=["ucm"])
                dma(h_d[ti * NPT * L:(ti + 1) * NPT * L, g0 * 16:g0 * 16 + 256].rearrange("(n t) c -> n t c", t=L), ucm[0:NPT, :].rearrange("p (t c) -> p t c", t=L), reads=["ucm"], writes=["h_d"])
            release(n0)

        n0 = len(stack)
        R = 20
        wstg = sb("wstgB", [128, 2, 2048], F32)
        wqm = sb("wqm", [128, 8, 512], BF16); wkv = sb("wkv", [128, 8, 1024], BF16)
        load_w_bf16(wqm, "wqm", W["w_in"][l][:, 2816:3328], 8, 512, wstg, "wstgB")
        load_w_bf16(wkv, "wkv", W["w_mem_kv"][l], 8, 1024, wstg, "wstgB")
        KmT = sb("KmT", [128, NSEG, 4, MEM], BF16); Vm = sb("Vm", [128, NSEG, 2, 4, 129], BF16)
        memT = sb("memT", [128, NSEG, 8, MEM], BF16)
        mstg = sb("mstg", [128, 2, 1024], F32); mbf = sb("mbf", [128, 2, 1024], BF16)
        it = 0
        for sg in range(NSEG):
            for mb in range(2):
                s = it % 2; it += 1
                dma(mstg[:, s, :], mem_in[sg, mb * 128:(mb + 1) * 128, :], writes=[("mstg", s)])
                cast(mbf[:, s, :], mstg[:, s, :], reads=[("mstg", s)], writes=[("mbf", s)])
                pT = PS[s][:].bitcast(BF16)
                for kc in range(8):
                    op("pe", lambda kc=kc: PE.transpose(pT[:, kc * 128:(kc + 1) * 128], mbf[:, s, kc * 128:(kc + 1) * 128], ident_b[:]),
                       reads=[("mbf", s), "ident_b"], writes=[psk(s)])
                evac(memT[:, sg, :, mb * 128:(mb + 1) * 128], pT.rearrange("p (k m) -> p k m", k=8), reads=[psk(s)], writes=["memT"])

        op("pool", lambda: G.memset(Vm[:, :, :, :, 128:129], 1.0), writes=["Vm"])
        pb = 0
        for sg in range(NSEG):
            for h in range(4):
                b = pb % 8; pb += 1
                for kc in range(8):
                    op("pe", lambda kc=kc: PE.matmul(PS[b][:, 0:MEM], wkv[:, kc, h * 128:(h + 1) * 128], memT[:, sg, kc, :], start=(kc == 0), stop=(kc == 7)),
                       reads=["wkv", "memT"], writes=[psk(b)])
                evac(KmT[:, sg, h, :], PS[b][:, 0:MEM], reads=[psk(b)], writes=["KmT"])
            for mb in range(2):
                b = pb % 8; pb += 1
                for kc in range(8):
                    op("pe", lambda kc=kc: PE.matmul(PS[b][:], memT[:, sg, kc, mb * 128:(mb + 1) * 128], wkv[:, kc, 512:1024], start=(kc == 0), stop=(kc == 7)),
                       reads=["wkv", "memT"], writes=[psk(b)])
                evac(Vm[:, sg, mb, :, 0:128], PS[b][:].rearrange("p (h e) -> p h e", h=4), reads=[psk(b)], writes=["Vm"])
        kring = sb("kring", [128, 6, R * 128], BF16); vring = sb("vring", [128, R, 12, 65], BF16)
        op("pool", lambda: G.memset(vring[:, :, :, 64:65], 1.0), writes=[("vr", s_) for s_ in range(R)])
        qt = sb("qt", [128, 2, 6, 128], BF16); xt = sb("xtB", [128, 2, 8, 128], BF16)
        esb = sb("esb", [128, 2, 512], BF16); psb = sb("psb", [128, 2, 4, 128], BF16)
        qmT = sb("qmT", [128, 4, 128], BF16); emb = sb("emb", [128, 2, 512], BF16)
        amo = sb("amo", [128, 2, 768], BF16); rec = sb("rec", [128, 8], F32)
        zt_ = sb("zt_", [128, 260], BF16)
        op("pool", lambda: G.memset(zt_[:], 0.0), writes=["zt_"])

        def load_blk(bk):
            sl = bk % R
            dma(kring[:, :, sl * 128:(sl + 1) * 128], qk_d[768:1536, bk * 128:(bk + 1) * 128].rearrange("(c p) t -> p c t", p=128), reads=["qk_d"], writes=[("kr", sl)])
            dma(vring[:, sl, :, 0:64], vu_d[bk * 128:(bk + 1) * 128, 0:768].rearrange("t (h e) -> t h e", e=64), reads=["vu_d"], writes=[("vr", sl)])
        for bk in range(min(NT, 10)):
            load_blk(bk)
        SC_M = 1.0 / math.sqrt(128.0)
        for j in range(NT):
            if j + 10 < NT:
                load_blk(j + 10)
            js = j % 2
            dma(qt[:, js, :, :], qk_d[0:768, j * 128:(j + 1) * 128].rearrange("(c p) t -> p c t", p=128), reads=["qk_d"], writes=[("qt", js)])
            dma(xt[:, js, :, :], xT_d[:, :, j * 128:(j + 1) * 128].rearrange("k p t -> p k t"), reads=["xT_d"], writes=[("xtB", js)])
            cl = [(g, dl) for (g, dl) in COMBOS if 0 <= j + dl < NT]
            OA = 0
            op("pe", lambda: PE.matmul(PS[OA][:, 0:260], zt_[:, 0:128], zt_[:, 0:260], start=True, stop=False), reads=["zt_"], writes=[psk(OA)])
            for ci, (g, dl) in enumerate(cl):
                bk = j + dl; sl = bk % R; b = 2 + ci % 2; s = ci % 2
                cidx = COMBOS.index((g, dl))
                for hs in range(4):
                    head = 4 * g + hs; c = head // 2; pr = slice((head % 2) * 64, (head % 2) * 64 + 64)
                    op("pe", lambda hs=hs, c=c, pr=pr: PE.matmul(PS[b][:, hs * 128:(hs + 1) * 128], kring[pr, c, sl * 128:(sl + 1) * 128], qt[pr, js, c, :], start=True, stop=True),
                       reads=[("kr", sl), ("qt", js)], writes=[psk(b)])
                op("act", lambda: A.activation(esb[:, s, :], PS[b][:], AF.Exp, scale=0.125), reads=[psk(b)], writes=[("esb", s)])
                fi = j * 17 + dl + 8
                op("dve", lambda: V.scalar_tensor_tensor(psb[:, s, :, :], esb[:, s, :].rearrange("p (h q) -> p h q", h=4), aflag[:, fi:fi + 1],
                                                           amask[:, cidx, :].unsqueeze(1).to_broadcast([128, 4, 128]), op0=ALU.mult, op1=ALU.mult),
                   reads=[("esb", s), "aflag", "amask"], writes=[("psb", s)])
                for hs in range(4):
                    op("pe", lambda hs=hs: PE.matmul(PS[OA][:, hs * 65:(hs + 1) * 65], psb[:, s, hs, :], vring[:, sl, 4 * g + hs, :], start=False, stop=(ci == len(cl) - 1 and hs == 3)),
                       reads=[("psb", s), ("vr", sl)], writes=[psk(OA)])
            oav = PS[OA][:, 0:260].rearrange("p (h e) -> p h e", h=4)
            op("dve", lambda: V.reciprocal(rec[:, 0:4], oav[:, :, 64]), reads=[psk(OA)], writes=["rec"])
            op("dve", lambda: V.tensor_tensor(amo[:, js, 0:256].rearrange("p (h e) -> p h e", h=4), oav[:, :, 0:64], rec[:, 0:4].unsqueeze(2).to_broadcast([128, 4, 64]), op=ALU.mult),
               reads=[psk(OA), "rec"], writes=[("amo", js)])
            sg = (j * 128) // SEG
            bQ = 1
            for h in range(4):
                for kc in range(8):
                    op("pe", lambda h=h, kc=kc: PE.matmul(PS[bQ][:, h * 128:(h + 1) * 128], wqm[:, kc, h * 128:(h + 1) * 128], xt[:, js, kc, :], start=(kc == 0), stop=(kc == 7)),
                       reads=["wqm", ("xtB", js)], writes=[psk(bQ)])
            evac(qmT[:].rearrange("p h t -> p (h t)"), PS[bQ][:], reads=[psk(bQ)], writes=["qmT"])
            for half in range(2):
                b = 4 + half
                for hh in range(2):
                    h = 2 * half + hh
                    for mb in range(2):
                        o = (hh * 2 + mb) * 128
                        op("pe", lambda h=h, mb=mb, o=o: PE.matmul(PS[b][:, o:o + 128], KmT[:, sg, h, mb * 128:(mb + 1) * 128], qmT[:, h, :], start=True, stop=True),
                           reads=["KmT", "qmT"], writes=[psk(b)])
                op("act", lambda half=half: A.activation(emb[:, half, :], PS[b][:], AF.Exp, scale=SC_M), reads=[psk(b)], writes=[("emb", half)])
                bo = 6 + half
                for hh in range(2):
                    h = 2 * half + hh
                    for mb in range(2):
                        o = (hh * 2 + mb) * 128
                        op("pe", lambda h=h, mb=mb, o=o, hh=hh: PE.matmul(PS[bo][:, hh * 129:(hh + 1) * 129], emb[:, half, o:o + 128], Vm[:, sg, mb, h, :], start=(mb == 0), stop=(mb == 1)),
                           reads=[("emb", half), "Vm"], writes=[psk(bo)])
                omv = PS[bo][:, 0:258].rearrange("p (h e) -> p h e", h=2)
                op("dve", lambda half=half: V.reciprocal(rec[:, 4 + 2 * half:6 + 2 * half], omv[:, :, 128]), reads=[psk(bo)], writes=["rec"])
                op("dve", lambda half=half: V.tensor_tensor(amo[:, js, 256 + half * 256:512 + half * 256].rearrange("p (h e) -> p h e", h=2), omv[:, :, 0:128],
                                                             rec[:, 4 + 2 * half:6 + 2 * half].unsqueeze(2).to_broadcast([128, 2, 128]), op=ALU.mult),
                   reads=[psk(bo), "rec"], writes=[("amo", js)])
            dma(am_d[j * 128:(j + 1) * 128, :], amo[:, js, :], reads=[("amo", js)], writes=["am_d"])
        release(n0)

        n0 = len(stack)
        wstg = sb("wstgC", [128, 2, 2048], F32)
        wg = sb("wg", [128, 8, 3072], BF16); watt = sb("watt", [128, 2, 1024], BF16); wssm = sb("wssm", [128, 4, 1024], BF16)
        wmem = sb("wmem", [128, 4, 1024], BF16); wmix = sb("wmix", [128, 8, 1024], BF16); wglu = sb("wglu", [128, 4, 512], BF16)
        load_w_bf16(wg, "wg", W["w_in"][l][:, 3328:6400], 8, 3072, wstg, "wstgC")
        load_w_bf16(watt, "watt", W["w_att_out"][l], 2, 1024, wstg, "wstgC")
        load_w_bf16(wssm, "wssm", W["w_ssm_out"][l], 4, 1024, wstg, "wstgC")
        load_w_bf16(wmem, "wmem", W["w_mem_out"][l], 4, 1024, wstg, "wstgC")
        load_w_bf16(wmix, "wmix", W["w_mix_out"][l], 8, 1024, wstg, "wstgC")
        load_w_bf16(wglu, "wglu", W["w_glu"][l], 4, 512, wstg, "wstgC")
        bgate = sb("bgate", [128, 24], F32); bglu = sb("bglu", [128, 4], F32)
        dma(bgate[:], W["b_gate"][l].rearrange("(c p) -> p c", p=128), writes=["bgate"], allow_slow_non_contiguous=True)
        dma(bglu[:], W["b_glu"][l].rearrange("(c p) -> p c", p=128), writes=["bglu"], allow_slow_non_contiguous=True)
        g1 = sb("g1", [128, 1024], F32); b1 = sb("b1", [128, 1024], F32)
        bcast_row(g1[:], "g1", W["ln1_g"][l], 1024); bcast_row(b1[:], "b1", W["ln1_b"][l], 1024)
        xTb = sb("xTb", [128, 8, 512], BF16); amt = sb("amt", [128, 2, 768], BF16); ht = sb("htk", [128, 2, 512], BF16)
        amT = sb("amT", [128, 6, 512], BF16); hT = sb("hT", [128, 4, 512], BF16); hg = sb("hg", [128, 4, 512], BF16)
        sig = sb("sig", [128, 2, 512], BF16); mg = sb("mg", [128, 2, 512], F32); mrg = sb("mrg", [128, 8, 512], BF16)
        xfC = sb("xfC", [128, 2, 1024], F32); zs = sb("zs", [128, 2, 1024], F32); st = sb("st", [128, 2, 6], F32); mv = sb("mv", [128, 2], F32)
        x1b = sb("x1b", [128, 2, 1024], BF16); x1T = sb("x1T", [128, 2, 8, 128], BF16)

        def layer_norm(zt, zk, gt, bt, gk, bk_, out_f32, outk):
            op("dve", lambda: V.bn_stats(st[:, 0, :], zt[:, 0:512]), reads=[zk], writes=["st"])
            op("dve", lambda: V.bn_stats(st[:, 1, :], zt[:, 512:1024]), reads=[zk], writes=["st"])
            op("dve", lambda: V.bn_aggr(mv[:], st[:]), reads=["st"], writes=["mv"])
            op("dve", lambda: V.tensor_scalar(mv[:, 1:2], mv[:, 1:2], EPS, None, op0=ALU.add), reads=["mv"], writes=["mv"])
            op("act", lambda: A.activation(mv[:, 1:2], mv[:, 1:2], AF.Sqrt), reads=["mv"], writes=["mv"])
            op("dve", lambda: V.reciprocal(mv[:, 1:2], mv[:, 1:2]), reads=["mv"], writes=["mv"])
            op("dve", lambda: V.tensor_scalar(zt, zt, mv[:, 0:1], mv[:, 1:2], op0=ALU.subtract, op1=ALU.mult), reads=[zk, "mv"], writes=[zk])
            op("dve", lambda: V.tensor_tensor(zt, zt, gt, op=ALU.mult), reads=[zk, gk], writes=[zk])
            op("dve", lambda: V.tensor_tensor(out_f32, zt, bt, op=ALU.add), reads=[zk, bk_], writes=[outk])
        pb = 0
        for bi in range(NB):
            t0 = bi * 512
            dma(xTb[:], xT_d[:, :, t0:t0 + 512].rearrange("k p t -> p k t"), reads=["xT_d"], writes=["xTb"])
            for tt in range(4):
                s = tt % 2
                dma(amt[:, s, :], am_d[t0 + tt * 128:t0 + (tt + 1) * 128, :], reads=["am_d"], writes=[("amt", s)])
                dma(ht[:, s, :], h_d[t0 + tt * 128:t0 + (tt + 1) * 128, :], reads=["h_d"], writes=[("htk", s)])
                b = pb % 8; pb += 1
                pT = PS[b][:].bitcast(BF16)
                for c in range(6):
                    op("pe", lambda c=c: PE.transpose(pT[:, c * 128:(c + 1) * 128], amt[:, s, c * 128:(c + 1) * 128], ident_b[:]), reads=[("amt", s), "ident_b"], writes=[psk(b)])
                evac(amT[:, :, tt * 128:(tt + 1) * 128], pT[:, 0:768].rearrange("p (c t) -> p c t", c=6), reads=[psk(b)], writes=["amT"])
                b = pb % 8; pb += 1
                pT = PS[b][:].bitcast(BF16)
                for c in range(4):
                    op("pe", lambda c=c: PE.transpose(pT[:, c * 128:(c + 1) * 128], ht[:, s, c * 128:(c + 1) * 128], ident_b[:]), reads=[("htk", s), "ident_b"], writes=[psk(b)])
                evac(hT[:, :, tt * 128:(tt + 1) * 128], pT[:, 0:512].rearrange("p (c t) -> p c t", c=4), reads=[psk(b)], writes=["hT"])
            for c in range(4):
                b = pb % 8; pb += 1; s = c % 2
                for kc in range(4):
                    op("pe", lambda kc=kc: PE.matmul(PS[b][:], wglu[:, kc, c * 128:(c + 1) * 128], hT[:, kc, :], start=(kc == 0), stop=(kc == 3)), reads=["wglu", "hT"], writes=[psk(b)])
                op("act", lambda: A.activation(sig[:, s, :], PS[b][:], AF.Sigmoid, bias=bglu[:, c:c + 1]), reads=[psk(b), "bglu"], writes=[("sig", s)])
                op("dve", lambda: V.tensor_tensor(hg[:, c, :], hT[:, c, :], sig[:, s, :], op=ALU.mult), reads=["hT", ("sig", s)], writes=["hg"])
            for dm in range(8):
                ms = dm % 2
                for br, (wt, wk_, nk, rhs_of, rk) in enumerate(((watt, "watt", 2, lambda kc: amT[:, kc, :], "amT"),
                                                                  (wssm, "wssm", 4, lambda kc: hg[:, kc, :], "hg"),
                                                                  (wmem, "wmem", 4, lambda kc: amT[:, 2 + kc, :], "amT"))):
                    b = pb % 8; pb += 1; b2 = pb % 8; pb += 1; s = br % 2
                    for kc in range(nk):
                        op("pe", lambda kc=kc: PE.matmul(PS[b][:], wt[:, kc, dm * 128:(dm + 1) * 128], rhs_of(kc), start=(kc == 0), stop=(kc == nk - 1)), reads=[wk_, rk], writes=[psk(b)])
                    for kc in range(8):
                        op("pe", lambda kc=kc: PE.matmul(PS[b2][:], wg[:, kc, br * 1024 + dm * 128:br * 1024 + (dm + 1) * 128], xTb[:, kc, :], start=(kc == 0), stop=(kc == 7)),
                           reads=["wg", "xTb"], writes=[psk(b2)])
                    op("act", lambda: A.activation(sig[:, s, :], PS[b2][:], AF.Sigmoid, bias=bgate[:, br * 8 + dm:br * 8 + dm + 1]), reads=[psk(b2), "bgate"], writes=[("sig", s)])
                    if br == 0:
                        op("dve", lambda: V.tensor_tensor(mg[:, ms, :], PS[b][:], sig[:, s, :], op=ALU.mult), reads=[psk(b), ("sig", s)], writes=[("mg", ms)])
                    else:
                        op("dve", lambda: V.tensor_tensor(sig[:, s, :], PS[b][:], sig[:, s, :], op=ALU.mult), reads=[psk(b), ("sig", s)], writes=[("sig", s)])
                        if br == 1:
                            op("pool", lambda: G.tensor_tensor(mg[:, ms, :], mg[:, ms, :], sig[:, s, :], op=ALU.add), reads=[("mg", ms), ("sig", s)], writes=[("mg", ms)])
                        else:
                            op("pool", lambda: G.tensor_tensor(mrg[:, dm, :], mg[:, ms, :], sig[:, s, :], op=ALU.add), reads=[("mg", ms), ("sig", s)], writes=["mrg"])
            for tt in range(4):
                s = tt % 2
                dma(xfC[:, s, :], x_src[t0 + tt * 128:t0 + (tt + 1) * 128, :], reads=[xsk], writes=[("xfC", s)])
                bb = []
                for half in range(2):
                    b = pb % 8; pb += 1; bb.append(b)
                    for kc in range(8):
                        op("pe", lambda kc=kc: PE.matmul(PS[b][:], mrg[:, kc, tt * 128:(tt + 1) * 128], wmix[:, kc, half * 512:(half + 1) * 512], start=(kc == 0), stop=(kc == 7)),
                           reads=["mrg", "wmix"], writes=[psk(b)])
                    op("dve", lambda half=half: V.scalar_tensor_tensor(zs[:, s, half * 512:(half + 1) * 512], xfC[:, s, half * 512:(half + 1) * 512], ALPHA, PS[b][:], op0=ALU.mult, op1=ALU.add),
                       reads=[("xfC", s), psk(b)], writes=[("zs", s)])
                layer_norm(zs[:, s, :], ("zs", s), g1[:], b1[:], "g1", "b1", zs[:, s, :], ("zs", s))
                dma(x1_d[t0 + tt * 128:t0 + (tt + 1) * 128, :], zs[:, s, :], reads=[("zs", s)], writes=["x1_d"])
                cast(x1b[:, s, :], zs[:, s, :], reads=[("zs", s)], writes=[("x1b", s)])
                b = pb % 8; pb += 1
                pT = PS[b][:].bitcast(BF16)
                for kc in range(8):
                    op("pe", lambda kc=kc: PE.transpose(pT[:, kc * 128:(kc + 1) * 128], x1b[:, s, kc * 128:(kc + 1) * 128], ident_b[:]), reads=[("x1b", s), "ident_b"], writes=[psk(b)])
                evac(x1T[:, s, :, :], pT.rearrange("p (k t) -> p k t", k=8), reads=[psk(b)], writes=[("x1T", s)])
                dma(x1T_d[:, :, t0 + tt * 128:t0 + (tt + 1) * 128].rearrange("k p t -> p k t"), x1T[:, s, :, :], reads=[("x1T", s)], writes=["x1T_d"])
        release(n0)

        n0 = len(stack)
        ST = 256; NST = T // ST
        wq = sb("wq", [128, 8, 2048], BF16)
        keysT = sb("keysT", [128, 2, 128], BF16)
        nW = len(stack)
        wstg = sb("wstgP", [128, 2, 2048], F32)
        load_w_bf16(wq, "wq", W["w_query"][l], 8, 2048, wstg, "wstgP")
        kst = sb("kst", [128, 128], F32); ksb = sb("ksb", [128, 128], BF16)
        for hf in range(2):
            dma(kst[:], W["sub_keys"][l, hf], writes=["kst"])
            op("dve", lambda: V.tensor_copy(ksb[:], kst[:]), reads=["kst"], writes=["ksb"])
            pT = PS[hf][:].bitcast(BF16)
            op("pe", lambda: PE.transpose(pT[:, 0:128], ksb[:], ident_b[:]), reads=["ksb", "ident_b"], writes=[psk(hf)])
            evac(keysT[:, hf, :], pT[:, 0:128], reads=[psk(hf)], writes=["keysT"])
        release(nW)
        g2 = sb("g2", [128, 1024], F32); b2_ = sb("b2", [128, 1024], F32)
        bcast_row(g2[:], "g2", W["ln2_g"][l], 1024); bcast_row(b2_[:], "b2", W["ln2_b"][l], 1024)
        x1Tp = sb("x1Tp", [128, 8, ST], BF16); qpT = sb("qpT", [128, 16, ST], BF16)
        sc = sb("sc", [128, 16, 128], F32)
        mxa = sb("mxa", [128, 2, 8], F32); mxb_ = sb("mxb", [128, 2, 8], F32); mia = sb("mia", [128, 2, 8], U32); mib = sb("mib", [128, 2, 8], U32)
        tmpk = sb("tmpk", [128, 2, 256], F32)
        V12 = sb("V12", [128, 16, 16], F32); I12 = sb("I12", [128, 16, 16], F32)
        cand = sb("cand", [128, 8, 256], F32); best = sb("best", [128, 8, 16], F32); posu = sb("posu", [128, 8, 16], U32)
        pa = sb("pa", [128, 8, 16], U32); pbb = sb("pbb", [128, 8, 16], U32); paf = sb("paf", [128, 8, 16], F32); pbf = sb("pbf", [128, 8, 16], F32)
        eq = sb("eq", [128, 8, 16, 16], F32); gate = sb("gate", [128, 8, 16], F32); zsum = sb("zsum", [128, 8], F32)
        isel = sb("isel", [128, 3, 128], F32)
        iT = sb("iT", [128, 3, ST], BF16)
        Ab = sb("Ab", [128, 2, 16, 128], BF16); Bb = sb("Bb", [128, 2, 16, 128], BF16)
        Wsb = sb("Wsb", [128, ST, 128], BF16)
        UTi = sb("UTi", [128, 2, 1024], BF16); Vi = sb("Vi", [128, 2, 1024], BF16)
        gl_ = sb("gl", [128, 2, ST], BF16); Gt = sb("Gt", [128, 2, ST], BF16)
        xfP = sb("xfP", [128, 2, 1024], F32); zs = sb("zsP", [128, 2, 1024], F32); st = sb("stP", [128, 2, 6], F32); mv = sb("mvP", [128, 2], F32)

        def topk16(src_ap, srck, n, vout, iout, outk, s):
            op("dve", lambda: V.max(mxa[:, s, :], src_ap), reads=[srck], writes=[("mxa", s)])
            op("dve", lambda: V.max_index(mia[:, s, :], mxa[:, s, :], src_ap), reads=[srck, ("mxa", s)], writes=[("mia", s)])
            op("dve", lambda: V.match_replace(tmpk[:, s, 0:n], mxa[:, s, :], src_ap, -1e30), reads=[srck, ("mxa", s)], writes=[("tmpk", s)])
            op("dve", lambda: V.max(mxb_[:, s, :], tmpk[:, s, 0:n]), reads=[("tmpk", s)], writes=[("mxb", s)])
            op("dve", lambda: V.max_index(mib[:, s, :], mxb_[:, s, :], tmpk[:, s, 0:n]), reads=[("tmpk", s), ("mxb", s)], writes=[("mib", s)])
            op("pool", lambda: G.tensor_copy(vout[:, 0:8], mxa[:, s, :]), reads=[("mxa", s)], writes=[outk])
            op("pool", lambda: G.tensor_copy(vout[:, 8:16], mxb_[:, s, :]), reads=[("mxb", s)], writes=[outk])
            op("pool", lambda: G.tensor_copy(iout[:, 0:8], mia[:, s, :]), reads=[("mia", s)], writes=[outk])
            op("pool", lambda: G.tensor_copy(iout[:, 8:16], mib[:, s, :]), reads=[("mib", s)], writes=[outk])

        def layer_norm2(zt, zk, gt, bt, gk, bk_, out_f32, outk):
            op("dve", lambda: V.bn_stats(st[:, 0, :], zt[:, 0:512]), reads=[zk], writes=["stP"])
            op("dve", lambda: V.bn_stats(st[:, 1, :], zt[:, 512:1024]), reads=[zk], writes=["stP"])
            op("dve", lambda: V.bn_aggr(mv[:], st[:]), reads=["stP"], writes=["mvP"])
            op("dve", lambda: V.tensor_scalar(mv[:, 1:2], mv[:, 1:2], EPS, None, op0=ALU.add), reads=["mvP"], writes=["mvP"])
            op("act", lambda: A.activation(mv[:, 1:2], mv[:, 1:2], AF.Sqrt), reads=["mvP"], writes=["mvP"])
            op("dve", lambda: V.reciprocal(mv[:, 1:2], mv[:, 1:2]), reads=["mvP"], writes=["mvP"])
            op("dve", lambda: V.tensor_scalar(zt, zt, mv[:, 0:1], mv[:, 1:2], op0=ALU.subtract, op1=ALU.mult), reads=[zk, "mvP"], writes=[zk])
            op("dve", lambda: V.tensor_tensor(zt, zt, gt, op=ALU.mult), reads=[zk, gk], writes=[zk])
            op("dve", lambda: V.tensor_tensor(out_f32, zt, bt, op=ALU.add), reads=[zk, bk_], writes=[outk])

        for si in range(NST):
            t0 = si * ST
            dma(x1Tp[:], x1T_d[:, :, t0:t0 + ST].rearrange("k p t -> p k t"), reads=["x1T_d"], writes=["x1Tp"])
            for c in range(16):
                b = c % 2
                for kc in range(8):
                    op("pe", lambda kc=kc: PE.matmul(PS[b][:, 0:ST], wq[:, kc, c * 128:(c + 1) * 128], x1Tp[:, kc, :], start=(kc == 0), stop=(kc == 7)), reads=["wq", "x1Tp"], writes=[psk(b)])
                evac(qpT[:, c, :], PS[b][:, 0:ST], reads=[psk(b)], writes=["qpT"])
            for tt in range(ST // 128):
                for cq in range(4):
                    b = 2 + cq % 2
                    for cc in range(4):
                        c = cq * 4 + cc
                        op("pe", lambda c=c, cc=cc: PE.matmul(PS[b][:, cc * 128:(cc + 1) * 128], qpT[:, c, tt * 128:(tt + 1) * 128], keysT[:, c % 2, :], start=True, stop=True),
                           reads=["qpT", "keysT"], writes=[psk(b)])
                    evac(sc[:, cq * 4:(cq + 1) * 4, :].rearrange("p c k -> p (c k)"), PS[b][:], reads=[psk(b)], writes=["sc"])
                for c in range(16):
                    topk16(sc[:, c, :], "sc", 128, V12[:, c, :], I12[:, c, :], "V12", c % 2)
                v12 = V12[:].rearrange("p (h f) a -> p h f a", f=2); i12 = I12[:].rearrange("p (h f) a -> p h f a", f=2)
                op("dve", lambda: V.tensor_tensor(cand[:].rearrange("p h (a b) -> p h a b", a=16), v12[:, :, 0, :].unsqueeze(3).to_broadcast([128, 8, 16, 16]),
                                                  v12[:, :, 1, :].unsqueeze(2).to_broadcast([128, 8, 16, 16]), op=ALU.add), reads=["V12"], writes=["cand"])
                for h in range(8):
                    topk16(cand[:, h, :], "cand", 256, best[:, h, :], posu[:, h, :], "best", h % 2)
                op("dve", lambda: V.tensor_tensor(gate[:], best[:], best[:, :, 0:1].to_broadcast([128, 8, 16]), op=ALU.subtract), reads=["best"], writes=["gate"])
                op("act", lambda: A.activation(gate[:], gate[:], AF.Exp), reads=["gate"], writes=["gate"])
                op("dve", lambda: V.tensor_reduce(zsum[:], gate[:], op=ALU.add, axis=AX.X), reads=["gate"], writes=["zsum"])
                op("dve", lambda: V.reciprocal(zsum[:], zsum[:]), reads=["zsum"], writes=["zsum"])
                op("dve", lambda: V.tensor_tensor(isel[:, 2, :].rearrange("p (h k) -> p h k", h=8), gate[:], zsum[:].unsqueeze(2).to_broadcast([128, 8, 16]), op=ALU.mult),
                   reads=["gate", "zsum"], writes=["isel"])
                op("dve", lambda: V.tensor_scalar(pa[:], posu[:], 4, None, op0=ALU.logical_shift_right), reads=["best"], writes=["pa"])
                op("dve", lambda: V.tensor_scalar(pbb[:], posu[:], 15, None, op0=ALU.bitwise_and), reads=["best"], writes=["pbb"])
                op("pool", lambda: G.tensor_copy(paf[:], pa[:]), reads=["pa"], writes=["paf"])
                op("pool", lambda: G.tensor_copy(pbf[:], pbb[:]), reads=["pbb"], writes=["pbf"])
                io16 = iota_f[:, 0:16].unsqueeze(1).unsqueeze(1).to_broadcast([128, 8, 16, 16])
                for which, pf, pfk in ((0, paf, "paf"), (1, pbf, "pbf")):
                    op("dve", lambda pf=pf: V.tensor_tensor(eq[:], pf[:].unsqueeze(3).to_broadcast([128, 8, 16, 16]), io16, op=ALU.is_equal), reads=[pfk, "iota_f"], writes=["eq"])
                    op("dve", lambda which=which: V.tensor_tensor(eq[:], eq[:], i12[:, :, which, :].unsqueeze(2).to_broadcast([128, 8, 16, 16]), op=ALU.mult), reads=["eq", "V12"], writes=["eq"])
                    op("dve", lambda which=which: V.tensor_reduce(isel[:, which, :].rearrange("p (h k) -> p h k", h=8), eq[:], op=ALU.add, axis=AX.X), reads=["eq"], writes=["isel"])
                for w3 in range(3):
                    b = 2 + w3 % 2
                    op("pe", lambda w3=w3: PE.transpose(PS[b][:, 0:128], isel[:, w3, :], ident_f[:]), reads=["isel", "ident_f"], writes=[psk(b)])
                    evac(iT[:, w3, tt * 128:(tt + 1) * 128], PS[b][:, 0:128], reads=[psk(b)], writes=["iT"])
            for sbk in range(ST // 16):
                s = sbk % 2; ts = slice(sbk * 16, (sbk + 1) * 16)
                iob = iota_b[:].unsqueeze(1).to_broadcast([128, 16, 128])
                op("dve", lambda: V.tensor_tensor(Ab[:, s, :, :], iob, iT[:, 0, ts].unsqueeze(2).to_broadcast([128, 16, 128]), op=ALU.is_equal), reads=["iota_b", "iT"], writes=[("Ab", s)])
                op("pool", lambda: G.tensor_tensor(Ab[:, s, :, :], Ab[:, s, :, :], iT[:, 2, ts].unsqueeze(2).to_broadcast([128, 16, 128]), op=ALU.mult), reads=[("Ab", s), "iT"], writes=[("Ab", s)])
                op("dve", lambda: V.tensor_tensor(Bb[:, s, :, :], iob, iT[:, 1, ts].unsqueeze(2).to_broadcast([128, 16, 128]), op=ALU.is_equal), reads=["iota_b", "iT"], writes=[("Bb", s)])
                for q4 in range(4):
                    b = 2 + q4 % 2
                    for u in range(4):
                        tl = q4 * 4 + u
                        op("pe", lambda tl=tl, u=u: PE.matmul(PS[b][:, u * 128:(u + 1) * 128], Bb[:, s, tl, :], Ab[:, s, tl, :], start=True, stop=True),
                           reads=[("Ab", s), ("Bb", s)], writes=[psk(b)])
                    tg = sbk * 16 + q4 * 4
                    evac(Wsb[:, tg:tg + 4, :].rearrange("p t i -> p (t i)"), PS[b][:], reads=[psk(b)], writes=["Wsb"])
            OB = [4, 5, 6, 7]
            for i in range(128):
                s3 = i % 2; s = i % 2
                dma(UTi[:, s3, :], UT_d[l, i], reads=["UT_d"], writes=[("UTi", s3)])
                dma(Vi[:, s3, :], VS_d[l, i], reads=["VS_d"], writes=[("Vi", s3)])
                b = i % 2
                for kc in range(8):
                    op("pe", lambda kc=kc: PE.matmul(PS[b][:, 0:ST], UTi[:, s3, kc * 128:(kc + 1) * 128], x1Tp[:, kc, :], start=(kc == 0), stop=(kc == 7)),
                       reads=[("UTi", s3), "x1Tp"], writes=[psk(b)])
                op("act", lambda: A.activation(gl_[:, s, :], PS[b][:, 0:ST], AF.Gelu), reads=[psk(b)], writes=[("gl", s)])
                op("dve", lambda: V.tensor_tensor(Gt[:, s, :], gl_[:, s, :], Wsb[:, :, i], op=ALU.mult), reads=[("gl", s), "Wsb"], writes=[("Gt", s)])
                for tt in range(2):
                    for half in range(2):
                        op("pe", lambda tt=tt, half=half: PE.matmul(PS[OB[tt * 2 + half]][:], Gt[:, s, tt * 128:(tt + 1) * 128], Vi[:, s3, half * 512:(half + 1) * 512], start=(i == 0), stop=(i == 127)),
                           reads=[("Gt", s), ("Vi", s3)], writes=[psk(OB[tt * 2 + half])])
            for tt in range(2):
                s = tt
                dma(xfP[:, s, :], x1_d[t0 + tt * 128:t0 + (tt + 1) * 128, :], reads=["x1_d"], writes=[("xfP", s)])
                for half in range(2):
                    op("dve", lambda half=half: V.scalar_tensor_tensor(zs[:, s, half * 512:(half + 1) * 512], xfP[:, s, half * 512:(half + 1) * 512], ALPHA, PS[OB[tt * 2 + half]][:], op0=ALU.mult, op1=ALU.add),
                       reads=[("xfP", s), psk(OB[tt * 2 + half])], writes=[("zsP", s)])
                layer_norm2(zs[:, s, :], ("zsP", s), g2[:], b2_[:], "g2", "b2", zs[:, s, :], ("zsP", s))
                dma(x_dst[t0 + tt * 128:t0 + (tt + 1) * 128, :], zs[:, s, :], reads=[("zsP", s)], writes=[xdk])
        release(n0)
    kb.finish(["xsrc%d" % DEPTH])
    return nc, kb


def host_consts(SEG, seq_of_seg, n_valid_seg):
    T = NSEG * SEG; NT = T // 128
    c = {}
    c["c_ident"] = np.eye(128, dtype=np.float32)
    pm = np.zeros((128, 128), np.float32)
    for hb in (0, 64):
        for e in range(8):
            pm[hb + e + 8, hb + e] = 1.0; pm[hb + e, hb + e + 8] = 1.0
    c["c_pm"] = pm
    pos = np.zeros(T, np.float64); segstart = np.zeros(T, np.int64); segend = np.zeros(T, np.int64)
    s = 0
    while s < NSEG:
        e = s
        while e + 1 < NSEG and seq_of_seg[e + 1] == seq_of_seg[s]:
            e += 1
        n = (e - s + 1) * SEG
        pos[s * SEG:s * SEG + n] = np.arange(n)
        segstart[s * SEG:s * SEG + n] = s * SEG; segend[s * SEG:s * SEG + n] = s * SEG + n
        s = e + 1
    inv = (500000.0 ** (-np.arange(0, 16, 2, dtype=np.float32) / 16)).astype(np.float32)
    ang = (pos.astype(np.float32)[None, :] * inv[:, None]).astype(np.float32)
    cosT = np.ones((128, T), np.float32); sinT = np.zeros((128, T), np.float32)
    for hb in (0, 64):
        cosT[hb:hb + 8] = np.cos(ang); cosT[hb + 8:hb + 16] = np.cos(ang)
        sinT[hb:hb + 8] = -np.sin(ang); sinT[hb + 8:hb + 16] = np.sin(ang)
    c["c_cos"] = cosT; c["c_sin"] = sinT
    am = np.zeros((128, 25, 128), np.float32)
    kk = np.arange(128)[:, None]; qq = np.arange(128)[None, :]
    for ci, (g, dl) in enumerate(COMBOS):
        win, dil = PATTERNS[g]
        diff = 128 * dl + kk - qq
        am[:, ci, :] = ((diff % dil == 0) & (np.abs(diff) <= (win // 2 // dil) * dil)).astype(np.float32)
    c["c_amask"] = am.reshape(128, 25 * 128)
    fl = np.zeros((NT, 17), np.float32)
    for j in range(NT):
        for dl in range(-8, 9):
            b = j + dl
            if 0 <= b < NT and segstart[j * 128] <= b * 128 < segend[j * 128]:
                fl[j, dl + 8] = 1.0
    c["c_aflag"] = np.broadcast_to(fl.reshape(1, -1), (128, NT * 17)).copy()
    lf = np.zeros((128, NSEG), np.float32)
    for sg in range(1, NSEG):
        lf[0:64, sg] = 1.0 if seq_of_seg[sg] == seq_of_seg[sg - 1] else 0.0
        bs = NSEG - 1 - sg
        lf[64:128, sg] = 1.0 if seq_of_seg[bs] == seq_of_seg[bs + 1] else 0.0
    c["c_lflag"] = lf
    c["c_iota"] = np.broadcast_to(np.arange(128, dtype=np.float32)[None, :], (128, 128)).copy()
    ti = np.arange(128) // 16
    c["c_mL"] = (ti[None, :] >= ti[:, None]).astype(np.float32)
    c["c_mU"] = (ti[None, :] <= ti[:, None]).astype(np.float32)
    return c


def kernel(**inputs):
    SEG = 2048; DEPTH = 4
    wshapes = {n: inputs[n].shape for n in WNAMES}
    nc, kb = build(SEG, DEPTH, wshapes)
    xp = np.asarray(inputs["x_prompt"], np.float32); xsm = np.asarray(inputs["x_sample"], np.float32)
    mp = np.asarray(inputs["mem_prompt"], np.float32); msm = np.asarray(inputs["mem_sample"], np.float32)
    wd = {n: np.ascontiguousarray(np.asarray(inputs[n], np.float32)) for n in WNAMES}
    in_maps = []
    for core in range(8):
        if core < 4:
            x = xp[core]; mem = np.broadcast_to(mp[core][None], (NSEG, MEM, D)); seqs = [0, 0, 0, 0]
        elif core < 6:
            sl = slice((core - 4) * 4, (core - 4) * 4 + 4)
            x = xsm[sl].reshape(NSEG * SEG, D); mem = msm[sl]; seqs = [0, 1, 2, 3]
        else:
            x = np.zeros((NSEG * SEG, D), np.float32); mem = np.zeros((NSEG, MEM, D), np.float32); seqs = [0, 1, 2, 3]
        m = {"x": np.ascontiguousarray(x), "mem": np.ascontiguousarray(mem)}
        m.update(wd); m.update(host_consts(SEG, seqs, NSEG))
        in_maps.append(m)
    res = run_bass_kernel_spmd(nc, in_maps, core_ids=list(range(8)))
    yp = np.stack([res.results[c]["y"] for c in range(4)], 0).astype(np.float32)
    ys = np.concatenate([res.results[c]["y"].reshape(4, SEG, D) for c in (4, 5)], 0).astype(np.float32)
    return (yp, ys)
```

```python
import math
import numpy as np
import concourse.bass as bass
import concourse.mybir as mybir
from concourse.bass_utils import run_bass_kernel_spmd

F32 = mybir.dt.float32; BF16 = mybir.dt.bfloat16; U32 = mybir.dt.uint32
AF = mybir.ActivationFunctionType; ALU = mybir.AluOpType; AX = mybir.AxisListType

D = 1024; NSEG = 4; MEM = 256
ALPHA = 8.0 ** 0.25; EPS = 1e-5
PATTERNS = ((128, 1), (512, 4), (2048, 16))
GD = {0: (-1, 1), 1: (-2, 2), 2: (-8, 8)}
COMBOS = [(g, dl) for g in range(3) for dl in range(GD[g][0], GD[g][1] + 1)]
L = 16


class KB:
    NDMA = 24

    def __init__(self, nc):
        self.nc = nc
        self.eng = {"pe": nc.tensor, "act": nc.scalar, "dve": nc.vector, "pool": nc.gpsimd, "sp": nc.sync}
        self.sem = {}; self.cnt = {}
        for e in ["pe", "act", "dve", "pool"]:
            self.sem[e] = nc.semaphore("sem_" + e).__enter__(); self.cnt[e] = 0
        for i in range(self.NDMA):
            self.sem[("dma", i)] = nc.semaphore("sem_dma%d" % i).__enter__(); self.cnt[("dma", i)] = 0
        self.dma_rr = 0
        self.waited = {e: {} for e in ["pe", "act", "dve", "pool", "sp"]}
        self.last_w = {}; self.readers = {}; self.nops = 0
        self.snap = {}; self.dirty = {}

    def _deps(self, reads, writes):
        deps = []
        for b in list(reads) + list(writes):
            if b in self.last_w:
                deps.append(self.last_w[b])
        for b in writes:
            deps.extend(self.readers.get(b, []))
        return deps

    def _know(self, s, v):
        h = self.snap.get(s)
        if not h:
            return None
        lo, hi = 0, len(h) - 1
        if h[0][0] > v:
            return None
        while lo < hi:
            mid = (lo + hi + 1) // 2
            if h[mid][0] <= v:
                lo = mid
            else:
                hi = mid - 1
        return h[lo][1]

    def _wait(self, e, deps, keep_last=False):
        w = self.waited[e]; need = {}
        for (s, v) in deps:
            if v > w.get(s, 0) and v > need.get(s, 0):
                need[s] = v
        todo = []
        for s, v in sorted(need.items(), key=lambda kv: (isinstance(kv[0], tuple), -kv[1])):
            if v <= w.get(s, 0):
                continue
            todo.append((s, v)); w[s] = v
            k = self._know(s, v)
            if k:
                for s2, v2 in k.items():
                    if v2 > w.get(s2, 0):
                        w[s2] = v2
        last = None
        if keep_last and todo:
            last = todo.pop()
        for s, v in todo:
            self.eng[e].wait_ge(self.sem[s], v)
        if todo or last is not None:
            self.dirty[e] = True
        return last

    def _record(self, tok, reads, writes):
        for b in writes:
            self.last_w[b] = tok; self.readers[b] = []
        for b in reads:
            self.readers.setdefault(b, []).append(tok)

    def op(self, e, fn, reads=(), writes=(), inline=True):
        last = self._wait(e, self._deps(reads, writes), keep_last=inline)
        ins = fn()
        if last is not None:
            ins._wait_ge(self.sem[last[0]], last[1])
        if self.dirty.get(e, True):
            self.snap.setdefault(e, []).append((self.cnt[e] + 1, dict(self.waited[e]))); self.dirty[e] = False
        self.cnt[e] += 1
        ins.then_inc(self.sem[e], 1)
        self._record((e, self.cnt[e]), reads, writes)
        self.nops += 1
        return ins

    def dma(self, out, in_, reads=(), writes=(), **kw):
        s = ("dma", self.dma_rr); self.dma_rr = (self.dma_rr + 1) % self.NDMA
        deps = self._deps(reads, writes); deps.append((s, self.cnt[s]))
        last = self._wait("sp", deps, keep_last=True)
        ins = self.nc.sync.dma_start(out=out, in_=in_, **kw)
        if last is not None:
            ins._wait_ge(self.sem[last[0]], last[1])
        self.snap.setdefault(s, []).append((self.cnt[s] + 16, dict(self.waited["sp"])))
        self.cnt[s] += 16
        ins.then_inc(self.sem[s], 16)
        self._record((s, self.cnt[s]), reads, writes)
        self.nops += 1

    def barrier(self):
        allc = [(k, v) for k, v in self.cnt.items() if v > 0]
        for e in ["pe", "act", "dve", "pool", "sp"]:
            self._wait(e, allc)

    def finish(self, bufs):
        self._wait("sp", [self.last_w[b] for b in bufs if b in self.last_w])


WNAMES = ["w_in", "b_gate", "w_att_out", "w_ssm_out", "w_mem_out", "w_mix_out", "w_mem_kv", "lam_re", "lam_im",
          "log_step", "b_re", "b_im", "c_re", "c_im", "d_skip", "w_glu", "b_glu", "ln1_g", "ln1_b", "w_query",
          "sub_keys", "expert_u", "expert_v", "ln2_g", "ln2_b"]


def build(SEG, DEPTH, wshapes, debug=()):
    T = NSEG * SEG; NT = T // 128; NB = T // 512; NCH = T // L; CPS = SEG // L
    assert CPS == 128 or SEG < 2048
    nc = bass.Bass("TRN2", target_bir_lowering=False)
    kb = KB(nc)
    op = kb.op; dma = kb.dma
    V = nc.vector; A = nc.scalar; G = nc.gpsimd; PE = nc.tensor

    def din(name, shape, dt=F32):
        return nc.dram_tensor(name, list(shape), dt, kind="ExternalInput").ap()

    def dscr(name, shape, dt):
        kind = "ExternalOutput" if name in debug else "Internal"
        return nc.dram_tensor(name, list(shape), dt, kind=kind).ap()

    x_in = din("x", [T, D]); mem_in = din("mem", [NSEG, MEM, D])
    W = {n: din(n, wshapes[n]) for n in WNAMES}
    c_ident = din("c_ident", [128, 128]); c_pm = din("c_pm", [128, 128])
    c_cos = din("c_cos", [128, T]); c_sin = din("c_sin", [128, T])
    c_amask = din("c_amask", [128, 25 * 128]); c_aflag = din("c_aflag", [128, NT * 17])
    c_lflag = din("c_lflag", [128, NSEG]); c_iota = din("c_iota", [128, 128])
    c_mL = din("c_mL", [128, 128]); c_mU = din("c_mU", [128, 128])
    y_out = nc.dram_tensor("y", [T, D], F32, kind="ExternalOutput").ap()

    xs_d = [dscr("xs0", [T, D], F32), dscr("xs1", [T, D], F32)]
    x1_d = dscr("x1_d", [T, D], F32)
    xT_d = dscr("xT_d", [8, 128, T], BF16); x1T_d = dscr("x1T_d", [8, 128, T], BF16)
    qk_d = dscr("qk_d", [1536, T], BF16); vu_d = dscr("vu_d", [T, 1280], BF16)
    h_d = dscr("h_d", [T, 512], BF16); am_d = dscr("am_d", [T, 768], BF16)
    UT_d = dscr("UT_d", [DEPTH, 128, 128, 1024], BF16); VS_d = dscr("VS_d", [DEPTH, 128, 128, 1024], BF16)

    stack = []

    uid = [0]

    def sb(name, shape, dt):
        uid[0] += 1
        cm = nc.sbuf_tensor("%s_%d" % (name, uid[0]), list(shape), dt); t = cm.__enter__(); stack.append(cm); return t

    def release(n0):
        kb.barrier()
        while len(stack) > n0:
            stack.pop().__exit__(None, None, None)

    PS = [nc.psum_tensor("ps%d" % i, [128, 512], F32).__enter__() for i in range(8)]
    psk = lambda b: ("ps", b)

    ident_f = sb("ident_f", [128, 128], F32); ident_b = sb("ident_b", [128, 128], BF16)
    pm_f = sb("pm_f", [128, 128], F32)
    amask = sb("amask", [128, 25, 128], BF16); aflag = sb("aflag", [128, NT * 17], F32)
    lflag = sb("lflag", [128, NSEG], F32); iota_f = sb("iota_f", [128, 128], F32); iota_b = sb("iota_b", [128, 128], BF16)
    mL = sb("mL", [128, 128], F32); mU = sb("mU", [128, 128], F32)
    nstg = len(stack)
    stg = sb("stg_c", [128, 25 * 128], F32)
    dma(ident_f[:], c_ident[:, :], writes=["ident_f"]); dma(pm_f[:], c_pm[:, :], writes=["pm_f"])
    dma(aflag[:], c_aflag[:, :], writes=["aflag"]); dma(lflag[:], c_lflag[:, :], writes=["lflag"])
    dma(iota_f[:], c_iota[:, :], writes=["iota_f"]); dma(mL[:], c_mL[:, :], writes=["mL"]); dma(mU[:], c_mU[:, :], writes=["mU"])
    dma(stg[:], c_amask[:, :], writes=["stg_c"])
    op("dve", lambda: V.tensor_copy(amask[:].rearrange("p a b -> p (a b)"), stg[:]), reads=["stg_c"], writes=["amask"])
    op("dve", lambda: V.tensor_copy(ident_b[:], ident_f[:]), reads=["ident_f"], writes=["ident_b"])
    op("dve", lambda: V.tensor_copy(iota_b[:], iota_f[:]), reads=["iota_f"], writes=["iota_b"])
    release(nstg)
    NG = len(stack)

    rr = {"ev": 0, "cast": 0}

    def evac(out, in_, reads, writes, eng=None):
        if eng is None:
            eng = ("act", "dve")[rr["ev"] % 2]; rr["ev"] += 1
        if eng == "act":
            op("act", lambda: A.copy(out, in_), reads=reads, writes=writes)
        else:
            op("dve", lambda: V.tensor_copy(out, in_), reads=reads, writes=writes)

    def cast(out, in_, reads, writes, eng=None):
        if eng is None:
            eng = ("pool", "dve", "act")[rr["cast"] % 3]; rr["cast"] += 1
        if eng == "pool":
            op("pool", lambda: G.tensor_copy(out, in_), reads=reads, writes=writes)
        elif eng == "dve":
            op("dve", lambda: V.tensor_copy(out, in_), reads=reads, writes=writes)
        else:
            op("act", lambda: A.copy(out, in_), reads=reads, writes=writes)

    def load_w_bf16(dst, dkey, src_ap, rows_chunks, ncols, stgt, skey):
        cw = min(ncols, 2048)
        i = 0
        for kc in range(rows_chunks):
            for c0 in range(0, ncols, cw):
                s = i % 2; i += 1; w_ = min(cw, ncols - c0)
                dma(stgt[:, s, 0:w_], src_ap[kc * 128:(kc + 1) * 128, c0:c0 + w_], writes=[(skey, s)])
                cast(dst[:, kc, c0:c0 + w_], stgt[:, s, 0:w_], reads=[(skey, s)], writes=[dkey])

    def bcast_row(dst, dkey, src_row_ap, n):
        dma(dst, src_row_ap.partition_broadcast(128), writes=[dkey])

    n0 = len(stack)
    ustg = sb("ustg", [128, 2, 1024], F32); ubf = sb("ubf", [128, 2, 1024], BF16)
    utsb = sb("utsb", [128, 2, 1024], BF16); vbf = sb("vbf", [128, 2, 1024], BF16); vstg = sb("vstg", [128, 2, 1024], F32)
    items = [(l, i) for l in range(DEPTH) for i in range(128)]

    def prep_load(it):
        l, i = items[it]; s = it % 2
        dma(ustg[:, s, :], W["expert_u"][l, i * 128:(i + 1) * 128, :], writes=[("ustg", s)])
        dma(vstg[:, s, :], W["expert_v"][l, i * 128:(i + 1) * 128, :], writes=[("vstg", s)])

    def prep_work(it):
        l, i = items[it]; s = it % 2; b = it % 2
        cast(ubf[:, s, :], ustg[:, s, :], reads=[("ustg", s)], writes=[("ubf", s)], eng="pool")
        pT = PS[b][:].bitcast(BF16)
        for kc in range(8):
            op("pe", lambda kc=kc: PE.transpose(pT[:, kc * 128:(kc + 1) * 128], ubf[:, s, kc * 128:(kc + 1) * 128], ident_b[:]),
               reads=[("ubf", s), "ident_b"], writes=[psk(b)])
        evac(utsb[:, s, :], pT, reads=[psk(b)], writes=[("utsb", s)])
        cast(vbf[:, s, :], vstg[:, s, :], reads=[("vstg", s)], writes=[("vbf", s)], eng="dve" if i % 2 else "act")
        dma(UT_d[l, i], utsb[:, s, :], reads=[("utsb", s)], writes=["UT_d"])
        dma(VS_d[l, i], vbf[:, s, :], reads=[("vbf", s)], writes=["VS_d"])
    prep_load(0)
    for it in range(len(items)):
        if it + 1 < len(items):
            prep_load(it + 1)
        prep_work(it)
    release(n0)

    NG2 = len(stack)

    for l in range(DEPTH):
        x_src = x_in if l == 0 else xs_d[(l - 1) % 2]
        x_dst = y_out if l == DEPTH - 1 else xs_d[l % 2]
        xsk = "xsrc%d" % l; xdk = "xsrc%d" % (l + 1)

        n0 = len(stack)
        wA = sb("wA", [128, 8, 2816], BF16); wstg = sb("wstgA", [128, 2, 2048], F32)
        load_w_bf16(wA, "wA", W["w_in"][l][:, 0:2816], 8, 2816, wstg, "wstgA")
        xf = sb("xfA", [128, 2, 1024], F32); xb = sb("xbA", [128, 2, 1024], BF16)
        xT = sb("xTA", [128, 2, 8, 512], BF16)
        cs = sb("csA", [128, 2, 512], F32); sn = sb("snA", [128, 2, 512], F32)
        qs = sb("qsA", [128, 2, 512], F32); t1 = sb("t1A", [128, 2, 512], F32); t2 = sb("t2A", [128, 2, 512], F32)
        qr = sb("qrA", [128, 2, 512], BF16); vu = sb("vuA", [128, 2, 1280], BF16)
        pb = 0
        for bi in range(NB):
            t0 = bi * 512; xs_ = bi % 2
            dma(cs[:, xs_, :], c_cos[:, t0:t0 + 512], writes=[("csA", xs_)])
            dma(sn[:, xs_, :], c_sin[:, t0:t0 + 512], writes=[("snA", xs_)])
            for tt in range(4):
                s = tt % 2
                dma(xf[:, s, :], x_src[t0 + tt * 128:t0 + (tt + 1) * 128, :], reads=[xsk], writes=[("xfA", s)])
                cast(xb[:, s, :], xf[:, s, :], reads=[("xfA", s)], writes=[("xbA", s)])
                b = pb % 8; pb += 1
                pT = PS[b][:].bitcast(BF16)
                for kc in range(8):
                    op("pe", lambda kc=kc: PE.transpose(pT[:, kc * 128:(kc + 1) * 128], xb[:, s, kc * 128:(kc + 1) * 128], ident_b[:]),
                       reads=[("xbA", s), "ident_b"], writes=[psk(b)])
                evac(xT[:, xs_, :, tt * 128:(tt + 1) * 128], pT.rearrange("p (k t) -> p k t", k=8), reads=[psk(b)], writes=[("xTA", xs_)])
            dma(xT_d[:, :, t0:t0 + 512].rearrange("k p t -> p k t"), xT[:, xs_, :, :], reads=[("xTA", xs_)], writes=["xT_d"])
            for c in range(12):
                b = pb % 8; pb += 1; s = c % 2
                for kc in range(8):
                    op("pe", lambda kc=kc: PE.matmul(PS[b][:], wA[:, kc, c * 128:(c + 1) * 128], xT[:, xs_, kc, :], start=(kc == 0), stop=(kc == 7)),
                       reads=["wA", ("xTA", xs_)], writes=[psk(b)])
                op("act", lambda: A.copy(qs[:, s, :], PS[b][:]), reads=[psk(b)], writes=[("qsA", s)])
                b2 = pb % 8; pb += 1
                op("pe", lambda: PE.matmul(PS[b2][:], pm_f[:], qs[:, s, :], start=True, stop=True), reads=["pm_f", ("qsA", s)], writes=[psk(b2)])
                op("dve", lambda: V.tensor_tensor(t1[:, s, :], qs[:, s, :], cs[:, xs_, :], op=ALU.mult), reads=[("qsA", s), ("csA", xs_)], writes=[("t1A", s)])
                op("dve", lambda: V.tensor_tensor(t2[:, s, :], PS[b2][:], sn[:, xs_, :], op=ALU.mult), reads=[psk(b2), ("snA", xs_)], writes=[("t2A", s)])
                op("pool", lambda: G.tensor_tensor(qr[:, s, :], t1[:, s, :], t2[:, s, :], op=ALU.add), reads=[("t1A", s), ("t2A", s)], writes=[("qrA", s)])
                dma(qk_d[c * 128:(c + 1) * 128, t0:t0 + 512], qr[:, s, :], reads=[("qrA", s)], writes=["qk_d"])
            for tt in range(4):
                s = tt % 2
                for (c0, c1, o0) in ((1536, 2048, 0), (2048, 2304, 512), (2304, 2816, 768)):
                    b = pb % 8; pb += 1
                    for kc in range(8):
                        op("pe", lambda kc=kc: PE.matmul(PS[b][:, 0:c1 - c0], xT[:, xs_, kc, tt * 128:(tt + 1) * 128], wA[:, kc, c0:c1], start=(kc == 0), stop=(kc == 7)),
                           reads=["wA", ("xTA", xs_)], writes=[psk(b)])
                    evac(vu[:, s, o0:o0 + c1 - c0], PS[b][:, 0:c1 - c0], reads=[psk(b)], writes=[("vuA", s)])
                dma(vu_d[t0 + tt * 128:t0 + (tt + 1) * 128, :], vu[:, s, :], reads=[("vuA", s)], writes=["vu_d"])
        release(n0)

        for gh in range(2):
            n0 = len(stack)
            NGR = 16; g0 = gh * 16
            Tm = sb("Tm", [128, NGR, 3, 128], BF16)
            WB = sb("WB", [128, NGR, 2, 2, 128], BF16)
            CK = sb("CK", [128, NGR, 2, 256], BF16)
            dcol = sb("dcol", [128, NGR], F32)
            Z = sb("Z", [128, 2, NGR], F32); A1 = sb("A1", [128, 2, NGR], F32); A2 = sb("A2", [128, 2, NGR], F32)
            tA = sb("tA", [128, 2, NGR], F32); tB2 = sb("tB2", [128, 2, NGR], F32); Zc = sb("Zc", [128, 2, NGR], F32)
            nT = len(stack)
            lre = sb("lre", [128, NGR], F32); lim = sb("lim", [128, NGR], F32); dt_ = sb("dt", [128, NGR], F32)
            for d_ in range(2):
                dma(lre[d_ * 64:(d_ + 1) * 64, :], W["lam_re"][l, d_, g0:g0 + 16].rearrange("g p -> p g"), writes=["lre"], allow_slow_non_contiguous=True)
                dma(lim[d_ * 64:(d_ + 1) * 64, :], W["lam_im"][l, d_, g0:g0 + 16].rearrange("g p -> p g"), writes=["lim"], allow_slow_non_contiguous=True)
                dma(dt_[d_ * 64:(d_ + 1) * 64, :], W["log_step"][l, d_, g0:g0 + 16].partition_broadcast(64), writes=["dt"])
            Br = sb("Br", [128, NGR, 16], F32); Bi = sb("Bi", [128, NGR, 16], F32)
            for d_ in range(2):
                dma(Br[d_ * 64:(d_ + 1) * 64, :, :], W["b_re"][l, d_, g0:g0 + 16].rearrange("g p c -> p g c"), writes=["Br"])
                dma(Bi[d_ * 64:(d_ + 1) * 64, :, :], W["b_im"][l, d_, g0:g0 + 16].rearrange("g p c -> p g c"), writes=["Bi"])
            Cr = sb("Cr", [128, NGR, 16], F32); Ci = sb("Ci", [128, NGR, 16], F32)
            cstg = sb("cstg", [128, 2, 128], F32)
            it = 0
            for (src, dstt, dk) in ((W["c_re"], Cr, "Cr"), (W["c_im"], Ci, "Ci")):
                for gq in range(2):
                    b = it % 8; s = it % 2; it += 1
                    for d_ in range(2):
                        dma(cstg[:, s, d_ * 64:(d_ + 1) * 64], src[l, d_, g0 + gq * 8:g0 + (gq + 1) * 8].rearrange("g c p -> (g c) p"), writes=[("cstg", s)])
                    op("pe", lambda: PE.transpose(PS[b][:, 0:128], cstg[:, s, :], ident_f[:]), reads=[("cstg", s), "ident_f"], writes=[psk(b)])
                    evac(dstt[:, gq * 8:(gq + 1) * 8, :].rearrange("p g c -> p (g c)"), PS[b][:, 0:128], reads=[psk(b)], writes=[dk])
            dsk = sb("dsk", [128, 4], F32)
            sA = lambda nm: sb(nm, [128, NGR], F32)
            xr_ = sA("xr_"); xi_ = sA("xi_"); mag = sA("mag"); are = sA("are"); aim = sA("aim"); kk = sA("kk"); tq = sA("tq"); yy = sA("yy")
            op("act", lambda: A.activation(dt_[:], dt_[:], AF.Exp), reads=["dt"], writes=["dt"])
            op("dve", lambda: V.tensor_scalar(lre[:], lre[:], -1e-4, None, op0=ALU.min), reads=["lre"], writes=["lre"])
            op("dve", lambda: V.tensor_tensor(xr_[:], dt_[:], lre[:], op=ALU.mult), reads=["dt", "lre"], writes=["xr_"])
            op("dve", lambda: V.tensor_tensor(xi_[:], dt_[:], lim[:], op=ALU.mult), reads=["dt", "lim"], writes=["xi_"])
            op("act", lambda: A.activation(mag[:], xr_[:], AF.Exp), reads=["xr_"], writes=["mag"])

            def sin_of(dst, dkey, shift):
                op("dve", lambda: V.tensor_scalar(yy[:], xi_[:], float(shift), None, op0=ALU.add), reads=["xi_"], writes=["yy"])
                op("dve", lambda: V.memset(kk[:], 0.0), writes=["kk"])
                for j in range(1, 6):
                    op("dve", lambda j=j: V.tensor_scalar(tq[:], yy[:], float((2 * j - 1) * math.pi), None, op0=ALU.is_ge), reads=["yy"], writes=["tq"])
                    op("dve", lambda: V.tensor_tensor(kk[:], kk[:], tq[:], op=ALU.add), reads=["kk", "tq"], writes=["kk"])
                op("dve", lambda: V.scalar_tensor_tensor(yy[:], kk[:], float(-2 * math.pi), yy[:], op0=ALU.mult, op1=ALU.add), reads=["kk", "yy"], writes=["yy"])
                op("act", lambda: A.activation(dst[:], yy[:], AF.Sin), reads=["yy"], writes=[dkey])
            sin_of(aim, "aim", 0.0); sin_of(are, "are", math.pi / 2)
            op("dve", lambda: V.tensor_tensor(are[:], are[:], mag[:], op=ALU.mult), reads=["are", "mag"], writes=["are"])
            op("dve", lambda: V.tensor_tensor(aim[:], aim[:], mag[:], op=ALU.mult), reads=["aim", "mag"], writes=["aim"])
            den = sA("den"); zr = sA("zr"); zi = sA("zi"); am1 = sA("am1"); tz = sA("tz")
            op("dve", lambda: V.tensor_tensor(den[:], lre[:], lre[:], op=ALU.mult), reads=["lre"], writes=["den"])
            op("dve", lambda: V.tensor_tensor(tz[:], lim[:], lim[:], op=ALU.mult), reads=["lim"], writes=["tz"])
            op("dve", lambda: V.tensor_tensor(den[:], den[:], tz[:], op=ALU.add), reads=["den", "tz"], writes=["den"])
            op("dve", lambda: V.reciprocal(den[:], den[:]), reads=["den"], writes=["den"])
            op("dve", lambda: V.tensor_scalar(am1[:], are[:], -1.0, None, op0=ALU.add), reads=["are"], writes=["am1"])
            op("dve", lambda: V.tensor_tensor(zr[:], am1[:], lre[:], op=ALU.mult), reads=["am1", "lre"], writes=["zr"])
            op("dve", lambda: V.tensor_tensor(tz[:], aim[:], lim[:], op=ALU.mult), reads=["aim", "lim"], writes=["tz"])
            op("dve", lambda: V.tensor_tensor(zr[:], zr[:], tz[:], op=ALU.add), reads=["zr", "tz"], writes=["zr"])
            op("dve", lambda: V.tensor_tensor(zr[:], zr[:], den[:], op=ALU.mult), reads=["zr", "den"], writes=["zr"])
            op("dve", lambda: V.tensor_tensor(zi[:], aim[:], lre[:], op=ALU.mult), reads=["aim", "lre"], writes=["zi"])
            op("dve", lambda: V.tensor_tensor(tz[:], am1[:], lim[:], op=ALU.mult), reads=["am1", "lim"], writes=["tz"])
            op("dve", lambda: V.tensor_tensor(zi[:], zi[:], tz[:], op=ALU.subtract), reads=["zi", "tz"], writes=["zi"])
            op("dve", lambda: V.tensor_tensor(zi[:], zi[:], den[:], op=ALU.mult), reads=["zi", "den"], writes=["zi"])

            def cmul(orr, oi, ork, oik, ar, ai, ark, aik, br, bi, brk, bik, tmpa, tmpak, eng="dve"):
                E = V if eng == "dve" else G
                op(eng, lambda: E.tensor_tensor(tmpa, ai, bi, op=ALU.mult), reads=aik + bik, writes=[tmpak])
                op(eng, lambda: E.tensor_tensor(orr, ar, br, op=ALU.mult), reads=ark + brk, writes=[ork])
                op(eng, lambda: E.tensor_tensor(orr, orr, tmpa, op=ALU.subtract), reads=[ork, tmpak], writes=[ork])
                op(eng, lambda: E.tensor_tensor(tmpa, ai, br, op=ALU.mult), reads=aik + brk, writes=[tmpak])
                op(eng, lambda: E.tensor_tensor(oi, ar, bi, op=ALU.mult), reads=ark + bik, writes=[oik])
                op(eng, lambda: E.tensor_tensor(oi, oi, tmpa, op=ALU.add), reads=[oik, tmpak], writes=[oik])
            Bbr = sb("Bbr", [128, NGR, 16], F32); Bbi = sb("Bbi", [128, NGR, 16], F32); tB = sb("tB", [128, NGR, 16], F32)
            zb = lambda t: t[:].unsqueeze(2).to_broadcast([128, NGR, 16])
            cmul(Bbr[:], Bbi[:], "Bbr", "Bbi", zb(zr), zb(zi), ["zr"], ["zi"], Br[:], Bi[:], ["Br"], ["Bi"], tB[:], "tB")
            pwr = sb("pwr", [128, NGR, 33], F32); pwi = sb("pwi", [128, NGR, 33], F32)
            ipr = sb("ipr", [128, NGR, 17], F32); ipi = sb("ipi", [128, NGR, 17], F32)
            air = sA("air"); aii = sA("aii"); tp = sA("tp")
            op("dve", lambda: V.tensor_tensor(tz[:], mag[:], mag[:], op=ALU.mult), reads=["mag"], writes=["tz"])
            op("dve", lambda: V.reciprocal(tz[:], tz[:]), reads=["tz"], writes=["tz"])
            op("dve", lambda: V.tensor_tensor(air[:], are[:], tz[:], op=ALU.mult), reads=["are", "tz"], writes=["air"])
            op("dve", lambda: V.scalar_tensor_tensor(aii[:], aim[:], -1.0, tz[:], op0=ALU.mult, op1=ALU.mult), reads=["aim", "tz"], writes=["aii"])
            op("dve", lambda: V.memset(pwr[:, :, 0:1], 1.0), writes=[("pw", 0)]); op("dve", lambda: V.memset(pwi[:, :, 0:1], 0.0), writes=[("pwi_", 0)])
            op("pool", lambda: G.memset(ipr[:, :, 0:1], 1.0), writes=[("ip", 0)]); op("pool", lambda: G.memset(ipi[:, :, 0:1], 0.0), writes=[("ipi_", 0)])
            tp2 = sA("tp2")
            for m in range(32):
                cmul(pwr[:, :, m + 1], pwi[:, :, m + 1], ("pw", m + 1), ("pwi_", m + 1), pwr[:, :, m], pwi[:, :, m], [("pw", m)], [("pwi_", m)],
                     are[:], aim[:], ["are"], ["aim"], tp[:], "tp")
            for m in range(16):
                cmul(ipr[:, :, m + 1], ipi[:, :, m + 1], ("ip", m + 1), ("ipi_", m + 1), ipr[:, :, m], ipi[:, :, m], [("ip", m)], [("ipi_", m)],
                     air[:], aii[:], ["air"], ["aii"], tp2[:], "tp2", eng="pool")
            PWK = [("pw", m) for m in range(33)] + [("pwi_", m) for m in range(33)]
            IPK = [("ip", m) for m in range(17)] + [("ipi_", m) for m in range(17)]
            rpr = sb("rpr", [128, NGR, 16], F32); rpi = sb("rpi", [128, NGR, 16], F32); tR = sb("tR", [128, NGR, 16], F32)
            a16r = pwr[:, :, 16:17].to_broadcast([128, NGR, 16]); a16i = pwi[:, :, 16:17].to_broadcast([128, NGR, 16])
            cmul(rpr[:], rpi[:], "rpr", "rpi", a16r, a16i, PWK, PWK, ipr[:, :, 0:16], ipi[:, :, 0:16], IPK, IPK, tR[:], "tR")
            op("dve", lambda: V.tensor_copy(A1[:, 0, :], pwr[:, :, 16]), reads=PWK, writes=["A1"])
            op("dve", lambda: V.tensor_copy(A1[:, 1, :], pwr[:, :, 16]), reads=PWK, writes=["A1"])
            op("dve", lambda: V.tensor_scalar(A2[:, 0, :], pwi[:, :, 16], -1.0, None, op0=ALU.mult), reads=PWK, writes=["A2"])
            op("dve", lambda: V.tensor_copy(A2[:, 1, :], pwi[:, :, 16]), reads=PWK, writes=["A2"])
            tBr = sb("tBr", [128, NGR, 16], F32); tBi = sb("tBi", [128, NGR, 16], F32)
            tCr = sb("tCr", [128, NGR, 16], F32); tCi = sb("tCi", [128, NGR, 16], F32)
            tKr = sb("tKr", [128, NGR, 16], F32); tKi = sb("tKi", [128, NGR, 16], F32)
            F_ = slice(0, 64); B_ = slice(64, 128)
            cp = lambda dst, src, rk, wk, e="dve": op(e, (lambda: V.tensor_copy(dst, src)) if e == "dve" else (lambda: G.tensor_copy(dst, src)), reads=rk, writes=[wk])
            cp(tBr[F_], ipr[F_, :, 0:16], IPK, "tBr"); cp(tBi[F_], ipi[F_, :, 0:16], IPK, "tBi")
            cp(tBr[B_], pwr[B_, :, 0:16], PWK, "tBr", "pool"); cp(tBi[B_], pwi[B_, :, 0:16], PWK, "tBi", "pool")
            cp(tCr[F_], pwr[F_, :, 0:16], PWK, "tCr"); cp(tCi[F_], pwi[F_, :, 0:16], PWK, "tCi")
            cp(tCr[B_, :, 0:8], ipr[B_, :, 0:8], IPK, "tCr", "pool"); cp(tCi[B_, :, 0:8], ipi[B_, :, 0:8], IPK, "tCi", "pool")
            cp(tCr[B_, :, 8:16], rpr[B_, :, 8:16], ["rpr"], "tCr", "pool"); cp(tCi[B_, :, 8:16], rpi[B_, :, 8:16], ["rpi"], "tCi", "pool")
            cp(tKr[F_], pwr[F_, :, 16:32], PWK, "tKr"); cp(tKi[F_], pwi[F_, :, 16:32], PWK, "tKi")
            cp(tKr[B_], rpr[B_], ["rpr"], "tKr", "pool"); cp(tKi[B_], rpi[B_], ["rpi"], "tKi", "pool")
            for tl in range(8):
                dma(dcol[tl * 16:(tl + 1) * 16, :], W["d_skip"][l, g0 * 16:(g0 + 16) * 16].rearrange("(g c) -> c g", c=16), writes=["dcol"], allow_slow_non_contiguous=True)
            GB = 2
            EBr = sb("EBr", [128, GB, 16, 16], F32); EBi = sb("EBi", [128, GB, 16, 16], F32)
            CTr = sb("CTr", [128, GB, 16, 16], F32); CTi = sb("CTi", [128, GB, 16, 16], F32)
            CKr = sb("CKr", [128, GB, 16, 16], F32); CKi = sb("CKi", [128, GB, 16, 16], F32)
            tE = sb("tE", [128, GB, 16, 16], F32)
            tsb = sb("tsbT", [128, 2, 128], F32)
            pb = 0
            for gb in range(NGR // GB):
                gs = slice(gb * GB, (gb + 1) * GB)
                pwb = lambda t: t[:, gs, :].unsqueeze(3).to_broadcast([128, GB, 16, 16])
                vb = lambda t: t[:, gs, :].unsqueeze(2).to_broadcast([128, GB, 16, 16])
                cmul(EBr[:], EBi[:], "EBr", "EBi", pwb(tBr), pwb(tBi), ["tBr"], ["tBi"], vb(Bbr), vb(Bbi), ["Bbr"], ["Bbi"], tE[:], "tE")
                cmul(CTr[:], CTi[:], "CTr", "CTi", pwb(tCr), pwb(tCi), ["tCr"], ["tCi"], vb(Cr), vb(Ci), ["Cr"], ["Ci"], tE[:], "tE")
                cmul(CKr[:], CKi[:], "CKr", "CKi", pwb(tKr), pwb(tKi), ["tKr"], ["tKi"], vb(Cr), vb(Ci), ["Cr"], ["Ci"], tE[:], "tE")
                op("dve", lambda: V.tensor_scalar(CTi[:], CTi[:], -1.0, None, op0=ALU.mult), reads=["CTi"], writes=["CTi"])
                op("dve", lambda: V.tensor_scalar(CKi[:], CKi[:], -1.0, None, op0=ALU.mult), reads=["CKi"], writes=["CKi"])
                op("act", lambda: A.copy(CK[:, gs, 0, :], CKr[:].rearrange("p g t c -> p g (t c)")), reads=["CKr"], writes=["CK"])
                op("act", lambda: A.copy(CK[:, gs, 1, :], CKi[:].rearrange("p g t c -> p g (t c)")), reads=["CKi"], writes=["CK"])
                for gl in range(GB):
                    g = gb * GB + gl
                    bF = pb % 8; pb += 1; bB = pb % 8; pb += 1
                    eb = lambda t, pr: t[pr, gl, 0:8, :].rearrange("p t c -> p (t c)")
                    ca = lambda t, pr: t[pr, gl, :, :].rearrange("p t c -> p (t c)")
                    for (bank, pr) in ((bF, F_), (bB, B_)):
                        op("pe", lambda bank=bank, pr=pr: PE.matmul(PS[bank][:, 0:256], eb(EBr, pr), ca(CTr, pr), start=True, stop=False),
                           reads=["EBr", "CTr"], writes=[psk(bank)])
                        op("pe", lambda bank=bank, pr=pr: PE.matmul(PS[bank][:, 0:256], eb(EBi, pr), ca(CTi, pr), start=False, stop=True),
                           reads=["EBi", "CTi"], writes=[psk(bank)])
                    s = g % 2
                    op("dve", lambda: V.tensor_tensor(tsb[:, s, :], PS[bF][:, 0:128], mL[:], op=ALU.mult), reads=[psk(bF), "mL"], writes=[("tsbT", s)])
                    op("dve", lambda: V.scalar_tensor_tensor(tsb[:, s, :], ident_f[:], dcol[:, g:g + 1], tsb[:, s, :], op0=ALU.mult, op1=ALU.add),
                       reads=["ident_f", "dcol", ("tsbT", s)], writes=[("tsbT", s)])
                    op("dve", lambda: V.tensor_tensor(Tm[:, g, 0, :], PS[bB][:, 0:128], mU[:], op=ALU.mult), reads=[psk(bB), "mU"], writes=["Tm"])
                    op("pool", lambda: G.tensor_tensor(Tm[:, g, 0, :], Tm[:, g, 0, :], tsb[:, s, :], op=ALU.add), reads=["Tm", ("tsbT", s)], writes=["Tm"])
                    op("act", lambda: A.copy(Tm[:, g, 1, :], PS[bF][:, 128:256]), reads=[psk(bF)], writes=["Tm"])
                    op("act", lambda: A.copy(Tm[:, g, 2, :], PS[bB][:, 128:256]), reads=[psk(bB)], writes=["Tm"])
                    bW = pb % 8; pb += 1
                    for tb in range(2):
                        for ri, EBt, ek in ((0, EBr, "EBr"), (1, EBi, "EBi")):
                            op("pe", lambda tb=tb, ri=ri, EBt=EBt: PE.transpose(PS[bW][:, (tb * 2 + ri) * 128:(tb * 2 + ri + 1) * 128],
                                                                                  EBt[:, gl, tb * 8:(tb + 1) * 8, :].rearrange("p t c -> p (t c)"), ident_f[:]),
                               reads=[ek, "ident_f"], writes=[psk(bW)])
                    evac(WB[:, g, :, :, :].rearrange("p a b m -> p (a b m)"), PS[bW][:], reads=[psk(bW)], writes=["WB"])
            release(nT)
            SX = sb("SX", [128, NCH, 2, NGR], BF16)
            ucm = sb("ucm", [128, L * 256], BF16)
            ucp = sb("ucp", [128, L * 256], BF16)
            ucpv = ucp[:].rearrange("p (g x) -> p g x", g=NGR)
            U3 = sb("U3", [128, NGR, 2, 128], BF16)
            ucv = ucm[:].rearrange("p (t g c) -> p t g c", t=L, g=NGR)
            NSEGC = NCH // 128 if NCH >= 128 else 1
            NPT = min(128, NCH)

            def load_U3(ti):
                dma(ucm[0:NPT, :].rearrange("p (t c) -> p t c", t=L), vu_d[ti * NPT * L:(ti + 1) * NPT * L, 768 + g0 * 16:768 + g0 * 16 + 256].rearrange("(n t) c -> n t c", t=L), reads=["vu_d"], writes=["ucm"])
                op("pool", lambda: G.tensor_copy(ucp[0:NPT, :].rearrange("p (g t c) -> p g t c", g=NGR, t=L), ucm[0:NPT, :].rearrange("p (t g c) -> p g t c", t=L, g=NGR)), reads=["ucm"], writes=["ucp"])
                pbl = 0
                for g in range(NGR):
                    if g % 4 == 0:
                        bq = pbl % 4; pbl += 1
                        pT = PS[bq][:].bitcast(BF16)
                    for tb in range(2):
                        o = ((g % 4) * 2 + tb) * 128
                        op("pe", lambda g=g, tb=tb, o=o, pT=pT: PE.transpose(pT[:, o:o + NPT], ucpv[0:NPT, g, tb * 128:(tb + 1) * 128], ident_b[0:NPT, 0:NPT]),
                           reads=["ucp", "ident_b"], writes=[psk(bq)])
                    if g % 4 == 3:
                        evac(U3[:, g - 3:g + 1, :, 0:NPT], pT.rearrange("p (g t n) -> p g t n", g=4, t=2)[:, :, :, 0:NPT], reads=[psk(bq)], writes=["U3"])
            for ti in range(NCH // NPT):
                load_U3(ti)
                for g in range(NGR):
                    b = 4 + (g % 4)
                    for ri in range(2):
                        for tb in range(2):
                            op("pe", lambda g=g, ri=ri, tb=tb: PE.matmul(PS[b][:, ri * 128:ri * 128 + NPT], WB[:, g, tb, ri, :], U3[:, g, tb, 0:NPT], start=(tb == 0), stop=(tb == 1)),
                               reads=["WB", "U3"], writes=[psk(b)])
                    evac(SX[:, ti * NPT:(ti + 1) * NPT, :, g].rearrange("p n r -> p r n"), PS[b][:, 0:256].rearrange("p (r n) -> p r n", r=2)[:, :, 0:NPT],
                         reads=[psk(b)], writes=["SX"])
            op("dve", lambda: V.memset(Z[:], 0.0), writes=["Z"])
            for k in range(NCH):
                kf = k; kbw = NCH - 1 - k
                if k % CPS == 0 and k > 0:
                    sgi = k // CPS
                    op("dve", lambda sgi=sgi: V.tensor_scalar(Z[:], Z[:], lflag[:, sgi:sgi + 1], None, op0=ALU.mult), reads=["Z", "lflag"], writes=["Z"])
                op("dve", lambda: V.tensor_tensor(tA[:], A1[:], Z[:], op=ALU.mult), reads=["A1", "Z"], writes=["tA"])
                op("dve", lambda: V.tensor_tensor(tB2[:, 0, :], A2[:, 0, :], Z[:, 1, :], op=ALU.mult), reads=["A2", "Z"], writes=["tB2"])
                op("dve", lambda: V.tensor_tensor(tB2[:, 1, :], A2[:, 1, :], Z[:, 0, :], op=ALU.mult), reads=["A2", "Z"], writes=["tB2"])
                op("dve", lambda: V.tensor_tensor(tA[:], tA[:], tB2[:], op=ALU.add), reads=["tA", "tB2"], writes=["tA"])
                op("pool", lambda: G.tensor_copy(Zc[F_], Z[F_]), reads=["Z"], writes=["Zc"])
                op("pool", lambda: G.tensor_copy(Zc[B_], Z[B_]), reads=["Z"], writes=["Zc"])
                op("dve", lambda: V.tensor_tensor(Z[F_], tA[F_], SX[F_, kf, :, :], op=ALU.add), reads=["tA", "SX", "Zc"], writes=["Z"])
                op("dve", lambda: V.tensor_tensor(Z[B_], tA[B_], SX[B_, kbw, :, :], op=ALU.add), reads=["tA", "SX", "Zc"], writes=["Z"])
                op("pool", lambda: G.tensor_copy(SX[F_, kf, :, :], Zc[F_]), reads=["Zc", "Z"], writes=["SX"])
                op("pool", lambda: G.tensor_copy(SX[B_, kbw, :, :], Zc[B_]), reads=["Zc", "Z"], writes=["SX"])
            ysb = sb("ysb", [128, 2, 2, 128], BF16)
            hcm = ucm
            hcv = hcm[:].rearrange("p (t g c) -> p t g c", t=L, g=NGR)
            for ti in range(NCH // NPT):
                load_U3(ti)
                for g in range(NGR):
                    s = g % 2
                    for tbo in range(2):
                        b = 4 + (g * 2 + tbo) % 4
                        mm = []
                        mm.append((Tm[:, g, 0, :], U3[:, g, tbo, 0:NPT]))
                        if tbo == 1:
                            mm.append((Tm[:, g, 1, :], U3[:, g, 0, 0:NPT]))
                        else:
                            mm.append((Tm[:, g, 2, :], U3[:, g, 1, 0:NPT]))
                        for ri in range(2):
                            mm.append((CK[F_, g, ri, tbo * 128:(tbo + 1) * 128], SX[F_, ti * NPT:(ti + 1) * NPT, ri, g]))
                            mm.append((CK[B_, g, ri, tbo * 128:(tbo + 1) * 128], SX[B_, ti * NPT:(ti + 1) * NPT, ri, g]))
                        for q, (lh, rh) in enumerate(mm):
                            op("pe", lambda lh=lh, rh=rh, q=q: PE.matmul(PS[b][:, 0:NPT], lh, rh, start=(q == 0), stop=(q == len(mm) - 1)),
                               reads=["Tm", "U3", "CK", "SX"], writes=[psk(b)])
                        op("act", lambda tbo=tbo: A.activation(ysb[:, s, tbo, 0:NPT], PS[b][:, 0:NPT], AF.Gelu), reads=[psk(b)], writes=[("ysb", s, tbo)])
                    bq = (g % 2) * 2
                    for tbo in range(2):
                        pT = PS[bq + tbo][:].bitcast(BF16)
                        op("pe", lambda tbo=tbo, pT=pT: PE.transpose(pT[0:NPT, 0:128], ysb[:, s, tbo, 0:NPT], ident_b[:]), reads=[("ysb", s, tbo), "ident_b"], writes=[psk(bq + tbo)])
                        evac(ucpv[0:NPT, g, tbo * 128:(tbo + 1) * 128], pT[0:NPT, 0:128], reads=[psk(bq + tbo)], writes=["ucp"])
                op("pool", lambda: G.tensor_copy(ucm[0:NPT, :].rearrange("p (t g c) -> p g t c", t=L, g=NGR), ucp[0:NPT, :].rearrange("p (g t c) -> p g t c", g=NGR, t=L)), reads=["ucp"], writes=["ucm"])
                dma(h_d[ti * NPT * L:(ti + 1) * NPT * L, g0 * 16:g0 * 16 + 256].rearrange("(n t) c -> n t c", t=L), ucm[0:NPT, :].rearrange("p (t c) -> p t c", t=L), reads=["ucm"], writes=["h_d"])
            release(n0)

        n0 = len(stack)
        R = 20
        wstg = sb("wstgB", [128, 2, 2048], F32)
        wqm = sb("wqm", [128, 8, 512], BF16); wkv = sb("wkv", [128, 8, 1024], BF16)
        load_w_bf16(wqm, "wqm", W["w_in"][l][:, 2816:3328], 8, 512, wstg, "wstgB")
        load_w_bf16(wkv, "wkv", W["w_mem_kv"][l], 8, 1024, wstg, "wstgB")
        KmT = sb("KmT", [128, NSEG, 4, MEM], BF16); Vm = sb("Vm", [128, NSEG, 2, 4, 129], BF16)
        memT = sb("memT", [128, NSEG, 8, MEM], BF16)
        mstg = sb("mstg", [128, 2, 1024], F32); mbf = sb("mbf", [128, 2, 1024], BF16)
        it = 0
        for sg in range(NSEG):
            for mb in range(2):
                s = it % 2; it += 1
                dma(mstg[:, s, :], mem_in[sg, mb * 128:(mb + 1) * 128, :], writes=[("mstg", s)])
                cast(mbf[:, s, :], mstg[:, s, :], reads=[("mstg", s)], writes=[("mbf", s)])
                pT = PS[s][:].bitcast(BF16)
                for kc in range(8):
                    op("pe", lambda kc=kc: PE.transpose(pT[:, kc * 128:(kc + 1) * 128], mbf[:, s, kc * 128:(kc + 1) * 128], ident_b[:]),
                       reads=[("mbf", s), "ident_b"], writes=[psk(s)])
                evac(memT[:, sg, :, mb * 128:(mb + 1) * 128], pT.rearrange("p (k m) -> p k m", k=8), reads=[psk(s)], writes=["memT"])

        op("pool", lambda: G.memset(Vm[:, :, :, :, 128:129], 1.0), writes=["Vm"])
        pb = 0
        for sg in range(NSEG):
            for h in range(4):
                b = pb % 8; pb += 1
                for kc in range(8):
                    op("pe", lambda kc=kc: PE.matmul(PS[b][:, 0:MEM], wkv[:, kc, h * 128:(h + 1) * 128], memT[:, sg, kc, :], start=(kc == 0), stop=(kc == 7)),
                       reads=["wkv", "memT"], writes=[psk(b)])
                evac(KmT[:, sg, h, :], PS[b][:, 0:MEM], reads=[psk(b)], writes=["KmT"])
            for mb in range(2):
                b = pb % 8; pb += 1
                for kc in range(8):
                    op("pe", lambda kc=kc: PE.matmul(PS[b][:], memT[:, sg, kc, mb * 128:(mb + 1) * 128], wkv[:, kc, 512:1024], start=(kc == 0), stop=(kc == 7)),
                       reads=["wkv", "memT"], writes=[psk(b)])
                evac(Vm[:, sg, mb, :, 0:128], PS[b][:].rearrange("p (h e) -> p h e", h=4), reads=[psk(b)], writes=["Vm"])
        kring = sb("kring", [128, 6, R * 128], BF16); vring = sb("vring", [128, R, 12, 65], BF16)
        op("pool", lambda: G.memset(vring[:, :, :, 64:65], 1.0), writes=[("vr", s_) for s_ in range(R)])
        qt = sb("qt", [128, 2, 6, 128], BF16); xt = sb("xtB", [128, 2, 8, 128], BF16)
        esb = sb("esb", [128, 2, 512], BF16); psb = sb("psb", [128, 2, 4, 128], BF16)
        qmT = sb("qmT", [128, 4, 128], BF16); emb = sb("emb", [128, 2, 512], BF16)
        amo = sb("amo", [128, 2, 768], BF16); rec = sb("rec", [128, 8], F32)
        zt_ = sb("zt_", [128, 260], BF16)
        op("pool", lambda: G.memset(zt_[:], 0.0), writes=["zt_"])

        def load_blk(bk):
            sl = bk % R
            dma(kring[:, :, sl * 128:(sl + 1) * 128], qk_d[768:1536, bk * 128:(bk + 1) * 128].rearrange("(c p) t -> p c t", p=128), reads=["qk_d"], writes=[("kr", sl)])
            dma(vring[:, sl, :, 0:64], vu_d[bk * 128:(bk + 1) * 128, 0:768].rearrange("t (h e) -> t h e", e=64), reads=["vu_d"], writes=[("vr", sl)])
        for bk in range(min(NT, 10)):
            load_blk(bk)
        SC_M = 1.0 / math.sqrt(128.0)
        for j in range(NT):
            if j + 10 < NT:
                load_blk(j + 10)
            js = j % 2
            dma(qt[:, js, :, :], qk_d[0:768, j * 128:(j + 1) * 128].rearrange("(c p) t -> p c t", p=128), reads=["qk_d"], writes=[("qt", js)])
            dma(xt[:, js, :, :], xT_d[:, :, j * 128:(j + 1) * 128].rearrange("k p t -> p k t"), reads=["xT_d"], writes=[("xtB", js)])
            cl = [(g, dl) for (g, dl) in COMBOS if 0 <= j + dl < NT]
            OA = 0
            op("pe", lambda: PE.matmul(PS[OA][:, 0:260], zt_[:, 0:128], zt_[:, 0:260], start=True, stop=False), reads=["zt_"], writes=[psk(OA)])
            def S_stage(ci):
                g, dl = cl[ci]; bk = j + dl; sl = bk % R; b = 2 + ci % 2
                for hs in range(4):
                    head = 4 * g + hs; c = head // 2; pr = slice((head % 2) * 64, (head % 2) * 64 + 64)
                    op("pe", lambda hs=hs, c=c, pr=pr: PE.matmul(PS[b][:, hs * 128:(hs + 1) * 128], kring[pr, c, sl * 128:(sl + 1) * 128], qt[pr, js, c, :], start=True, stop=True),
                       reads=[("kr", sl), ("qt", js)], writes=[psk(b)])

            def M_stage(ci):
                g, dl = cl[ci]; b = 2 + ci % 2; s = ci % 2
                cidx = COMBOS.index((g, dl))
                op("act", lambda: A.activation(esb[:, s, :], PS[b][:], AF.Exp, scale=0.125), reads=[psk(b)], writes=[("esb", s)])
                fi = j * 17 + dl + 8
                op("dve", lambda: V.scalar_tensor_tensor(psb[:, s, :, :], esb[:, s, :].rearrange("p (h q) -> p h q", h=4), aflag[:, fi:fi + 1],
                                                           amask[:, cidx, :].unsqueeze(1).to_broadcast([128, 4, 128]), op0=ALU.mult, op1=ALU.mult),
                   reads=[("esb", s), "aflag", "amask"], writes=[("psb", s)])

            def PV_stage(ci):
                g, dl = cl[ci]; bk = j + dl; sl = bk % R; s = ci % 2
                for hs in range(4):
                    op("pe", lambda hs=hs: PE.matmul(PS[OA][:, hs * 65:(hs + 1) * 65], psb[:, s, hs, :], vring[:, sl, 4 * g + hs, :], start=False, stop=(ci == len(cl) - 1 and hs == 3)),
                       reads=[("psb", s), ("vr", sl)], writes=[psk(OA)])
            S_stage(0)
            for ci in range(len(cl)):
                if ci + 1 < len(cl):
                    S_stage(ci + 1)
                M_stage(ci)
                PV_stage(ci)
            oav = PS[OA][:, 0:260].rearrange("p (h e) -> p h e", h=4)
            op("dve", lambda: V.reciprocal(rec[:, 0:4], oav[:, :, 64]), reads=[psk(OA)], writes=["rec"])
            op("dve", lambda: V.tensor_tensor(amo[:, js, 0:256].rearrange("p (h e) -> p h e", h=4), oav[:, :, 0:64], rec[:, 0:4].unsqueeze(2).to_broadcast([128, 4, 64]), op=ALU.mult),
               reads=[psk(OA), "rec"], writes=[("amo", js)])
            sg = (j * 128) // SEG
            bQ = 1
            for h in range(4):
                for kc in range(8):
                    op("pe", lambda h=h, kc=kc: PE.matmul(PS[bQ][:, h * 128:(h + 1) * 128], wqm[:, kc, h * 128:(h + 1) * 128], xt[:, js, kc, :], start=(kc == 0), stop=(kc == 7)),
                       reads=["wqm", ("xtB", js)], writes=[psk(bQ)])
            evac(qmT[:].rearrange("p h t -> p (h t)"), PS[bQ][:], reads=[psk(bQ)], writes=["qmT"])
            for half in range(2):
                b = 4 + half
                for hh in range(2):
                    h = 2 * half + hh
                    for mb in range(2):
                        o = (hh * 2 + mb) * 128
                        op("pe", lambda h=h, mb=mb, o=o: PE.matmul(PS[b][:, o:o + 128], KmT[:, sg, h, mb * 128:(mb + 1) * 128], qmT[:, h, :], start=True, stop=True),
                           reads=["KmT", "qmT"], writes=[psk(b)])
                op("act", lambda half=half: A.activation(emb[:, half, :], PS[b][:], AF.Exp, scale=SC_M), reads=[psk(b)], writes=[("emb", half)])
                bo = 6 + half
                for hh in range(2):
                    h = 2 * half + hh
                    for mb in range(2):
                        o = (hh * 2 + mb) * 128
                        op("pe", lambda h=h, mb=mb, o=o, hh=hh: PE.matmul(PS[bo][:, hh * 129:(hh + 1) * 129], emb[:, half, o:o + 128], Vm[:, sg, mb, h, :], start=(mb == 0), stop=(mb == 1)),
                           reads=[("emb", half), "Vm"], writes=[psk(bo)])
                omv = PS[bo][:, 0:258].rearrange("p (h e) -> p h e", h=2)
                op("dve", lambda half=half: V.reciprocal(rec[:, 4 + 2 * half:6 + 2 * half], omv[:, :, 128]), reads=[psk(bo)], writes=["rec"])
                op("dve", lambda half=half: V.tensor_tensor(amo[:, js, 256 + half * 256:512 + half * 256].rearrange("p (h e) -> p h e", h=2), omv[:, :, 0:128],
                                                             rec[:, 4 + 2 * half:6 + 2 * half].unsqueeze(2).to_broadcast([128, 2, 128]), op=ALU.mult),
                   reads=[psk(bo), "rec"], writes=[("amo", js)])
            dma(am_d[j * 128:(j + 1) * 128, :], amo[:, js, :], reads=[("amo", js)], writes=["am_d"])
        release(n0)

        n0 = len(stack)
        wstg = sb("wstgC", [128, 2, 2048], F32)
        wg = sb("wg", [128, 8, 3072], BF16); watt = sb("watt", [128, 2, 1024], BF16); wssm = sb("wssm", [128, 4, 1024], BF16)
        wmem = sb("wmem", [128, 4, 1024], BF16); wmix = sb("wmix", [128, 8, 1024], BF16); wglu = sb("wglu", [128, 4, 512], BF16)
        load_w_bf16(wg, "wg", W["w_in"][l][:, 3328:6400], 8, 3072, wstg, "wstgC")
        load_w_bf16(watt, "watt", W["w_att_out"][l], 2, 1024, wstg, "wstgC")
        load_w_bf16(wssm, "wssm", W["w_ssm_out"][l], 4, 1024, wstg, "wstgC")
        load_w_bf16(wmem, "wmem", W["w_mem_out"][l], 4, 1024, wstg, "wstgC")
        load_w_bf16(wmix, "wmix", W["w_mix_out"][l], 8, 1024, wstg, "wstgC")
        load_w_bf16(wglu, "wglu", W["w_glu"][l], 4, 512, wstg, "wstgC")
        bgate = sb("bgate", [128, 24], F32); bglu = sb("bglu", [128, 4], F32)
        dma(bgate[:], W["b_gate"][l].rearrange("(c p) -> p c", p=128), writes=["bgate"], allow_slow_non_contiguous=True)
        dma(bglu[:], W["b_glu"][l].rearrange("(c p) -> p c", p=128), writes=["bglu"], allow_slow_non_contiguous=True)
        g1 = sb("g1", [128, 1024], F32); b1 = sb("b1", [128, 1024], F32)
        bcast_row(g1[:], "g1", W["ln1_g"][l], 1024); bcast_row(b1[:], "b1", W["ln1_b"][l], 1024)
        xTb = sb("xTb", [128, 8, 512], BF16); amt = sb("amt", [128, 2, 768], BF16); ht = sb("htk", [128, 2, 512], BF16)
        amT = sb("amT", [128, 6, 512], BF16); hT = sb("hT", [128, 4, 512], BF16); hg = sb("hg", [128, 4, 512], BF16)
        sig = sb("sig", [128, 2, 512], BF16); mg = sb("mg", [128, 2, 512], F32); mrg = sb("mrg", [128, 8, 512], BF16)
        xfC = sb("xfC", [128, 2, 1024], F32); zs = sb("zs", [128, 2, 1024], F32); st = sb("st", [128, 2, 6], F32); mv = sb("mv", [128, 2], F32)
        x1b = sb("x1b", [128, 2, 1024], BF16); x1T = sb("x1T", [128, 2, 8, 128], BF16)

        def layer_norm(zt, zk, gt, bt, gk, bk_, out_f32, outk):
            op("dve", lambda: V.bn_stats(st[:, 0, :], zt[:, 0:512]), reads=[zk], writes=["st"])
            op("dve", lambda: V.bn_stats(st[:, 1, :], zt[:, 512:1024]), reads=[zk], writes=["st"])
            op("dve", lambda: V.bn_aggr(mv[:], st[:]), reads=["st"], writes=["mv"])
            op("dve", lambda: V.tensor_scalar(mv[:, 1:2], mv[:, 1:2], EPS, None, op0=ALU.add), reads=["mv"], writes=["mv"])
            op("act", lambda: A.activation(mv[:, 1:2], mv[:, 1:2], AF.Sqrt), reads=["mv"], writes=["mv"])
            op("dve", lambda: V.reciprocal(mv[:, 1:2], mv[:, 1:2]), reads=["mv"], writes=["mv"])
            op("dve", lambda: V.tensor_scalar(zt, zt, mv[:, 0:1], mv[:, 1:2], op0=ALU.subtract, op1=ALU.mult), reads=[zk, "mv"], writes=[zk])
            op("dve", lambda: V.tensor_tensor(zt, zt, gt, op=ALU.mult), reads=[zk, gk], writes=[zk])
            op("dve", lambda: V.tensor_tensor(out_f32, zt, bt, op=ALU.add), reads=[zk, bk_], writes=[outk])
        pb = 0
        for bi in range(NB):
            t0 = bi * 512
            dma(xTb[:], xT_d[:, :, t0:t0 + 512].rearrange("k p t -> p k t"), reads=["xT_d"], writes=["xTb"])
            for tt in range(4):
                s = tt % 2
                dma(amt[:, s, :], am_d[t0 + tt * 128:t0 + (tt + 1) * 128, :], reads=["am_d"], writes=[("amt", s)])
                dma(ht[:, s, :], h_d[t0 + tt * 128:t0 + (tt + 1) * 128, :], reads=["h_d"], writes=[("htk", s)])
                b = pb % 8; pb += 1
                pT = PS[b][:].bitcast(BF16)
                for c in range(6):
                    op("pe", lambda c=c: PE.transpose(pT[:, c * 128:(c + 1) * 128], amt[:, s, c * 128:(c + 1) * 128], ident_b[:]), reads=[("amt", s), "ident_b"], writes=[psk(b)])
                evac(amT[:, :, tt * 128:(tt + 1) * 128], pT[:, 0:768].rearrange("p (c t) -> p c t", c=6), reads=[psk(b)], writes=["amT"])
                b = pb % 8; pb += 1
                pT = PS[b][:].bitcast(BF16)
                for c in range(4):
                    op("pe", lambda c=c: PE.transpose(pT[:, c * 128:(c + 1) * 128], ht[:, s, c * 128:(c + 1) * 128], ident_b[:]), reads=[("htk", s), "ident_b"], writes=[psk(b)])
                evac(hT[:, :, tt * 128:(tt + 1) * 128], pT[:, 0:512].rearrange("p (c t) -> p c t", c=4), reads=[psk(b)], writes=["hT"])
            for c in range(4):
                b = pb % 8; pb += 1; s = c % 2
                for kc in range(4):
                    op("pe", lambda kc=kc: PE.matmul(PS[b][:], wglu[:, kc, c * 128:(c + 1) * 128], hT[:, kc, :], start=(kc == 0), stop=(kc == 3)), reads=["wglu", "hT"], writes=[psk(b)])
                op("act", lambda: A.activation(sig[:, s, :], PS[b][:], AF.Sigmoid, bias=bglu[:, c:c + 1]), reads=[psk(b), "bglu"], writes=[("sig", s)])
                op("dve", lambda: V.tensor_tensor(hg[:, c, :], hT[:, c, :], sig[:, s, :], op=ALU.mult), reads=["hT", ("sig", s)], writes=["hg"])
            for dm in range(8):
                ms = dm % 2
                for br, (wt, wk_, nk, rhs_of, rk) in enumerate(((watt, "watt", 2, lambda kc: amT[:, kc, :], "amT"),
                                                                  (wssm, "wssm", 4, lambda kc: hg[:, kc, :], "hg"),
                                                                  (wmem, "wmem", 4, lambda kc: amT[:, 2 + kc, :], "amT"))):
                    b = pb % 8; pb += 1; b2 = pb % 8; pb += 1; s = br % 2
                    for kc in range(nk):
                        op("pe", lambda kc=kc: PE.matmul(PS[b][:], wt[:, kc, dm * 128:(dm + 1) * 128], rhs_of(kc), start=(kc == 0), stop=(kc == nk - 1)), reads=[wk_, rk], writes=[psk(b)])
                    for kc in range(8):
                        op("pe", lambda kc=kc: PE.matmul(PS[b2][:], wg[:, kc, br * 1024 + dm * 128:br * 1024 + (dm + 1) * 128], xTb[:, kc, :], start=(kc == 0), stop=(kc == 7)),
                           reads=["wg", "xTb"], writes=[psk(b2)])
                    op("act", lambda: A.activation(sig[:, s, :], PS[b2][:], AF.Sigmoid, bias=bgate[:, br * 8 + dm:br * 8 + dm + 1]), reads=[psk(b2), "bgate"], writes=[("sig", s)])
                    if br == 0:
                        op("dve", lambda: V.tensor_tensor(mg[:, ms, :], PS[b][:], sig[:, s, :], op=ALU.mult), reads=[psk(b), ("sig", s)], writes=[("mg", ms)])
                    else:
                        op("dve", lambda: V.tensor_tensor(sig[:, s, :], PS[b][:], sig[:, s, :], op=ALU.mult), reads=[psk(b), ("sig", s)], writes=[("sig", s)])
                        if br == 1:
                            op("pool", lambda: G.tensor_tensor(mg[:, ms, :], mg[:, ms, :], sig[:, s, :], op=ALU.add), reads=[("mg", ms), ("sig", s)], writes=[("mg", ms)])
                        else:
                            op("pool", lambda: G.tensor_tensor(mrg[:, dm, :], mg[:, ms, :], sig[:, s, :], op=ALU.add), reads=[("mg", ms), ("sig", s)], writes=["mrg"])
            for tt in range(4):
                s = tt % 2
                dma(xfC[:, s, :], x_src[t0 + tt * 128:t0 + (tt + 1) * 128, :], reads=[xsk], writes=[("xfC", s)])
                bb = []
                for half in range(2):
                    b = pb % 8; pb += 1; bb.append(b)
                    for kc in range(8):
                        op("pe", lambda kc=kc: PE.matmul(PS[b][:], mrg[:, kc, tt * 128:(tt + 1) * 128], wmix[:, kc, half * 512:(half + 1) * 512], start=(kc == 0), stop=(kc == 7)),
                           reads=["mrg", "wmix"], writes=[psk(b)])
                    op("dve", lambda half=half: V.scalar_tensor_tensor(zs[:, s, half * 512:(half + 1) * 512], xfC[:, s, half * 512:(half + 1) * 512], ALPHA, PS[b][:], op0=ALU.mult, op1=ALU.add),
                       reads=[("xfC", s), psk(b)], writes=[("zs", s)])
                layer_norm(zs[:, s, :], ("zs", s), g1[:], b1[:], "g1", "b1", zs[:, s, :], ("zs", s))
                dma(x1_d[t0 + tt * 128:t0 + (tt + 1) * 128, :], zs[:, s, :], reads=[("zs", s)], writes=["x1_d"])
                cast(x1b[:, s, :], zs[:, s, :], reads=[("zs", s)], writes=[("x1b", s)])
                b = pb % 8; pb += 1
                pT = PS[b][:].bitcast(BF16)
                for kc in range(8):
                    op("pe", lambda kc=kc: PE.transpose(pT[:, kc * 128:(kc + 1) * 128], x1b[:, s, kc * 128:(kc + 1) * 128], ident_b[:]), reads=[("x1b", s), "ident_b"], writes=[psk(b)])
                evac(x1T[:, s, :, :], pT.rearrange("p (k t) -> p k t", k=8), reads=[psk(b)], writes=[("x1T", s)])
                dma(x1T_d[:, :, t0 + tt * 128:t0 + (tt + 1) * 128].rearrange("k p t -> p k t"), x1T[:, s, :, :], reads=[("x1T", s)], writes=["x1T_d"])
        release(n0)

        n0 = len(stack)
        ST = 256; NST = T // ST
        wq = sb("wq", [128, 8, 2048], BF16)
        keysT = sb("keysT", [128, 2, 128], BF16)
        nW = len(stack)
        wstg = sb("wstgP", [128, 2, 2048], F32)
        load_w_bf16(wq, "wq", W["w_query"][l], 8, 2048, wstg, "wstgP")
        kst = sb("kst", [128, 128], F32); ksb = sb("ksb", [128, 128], BF16)
        for hf in range(2):
            dma(kst[:], W["sub_keys"][l, hf], writes=["kst"])
            op("dve", lambda: V.tensor_copy(ksb[:], kst[:]), reads=["kst"], writes=["ksb"])
            pT = PS[hf][:].bitcast(BF16)
            op("pe", lambda: PE.transpose(pT[:, 0:128], ksb[:], ident_b[:]), reads=["ksb", "ident_b"], writes=[psk(hf)])
            evac(keysT[:, hf, :], pT[:, 0:128], reads=[psk(hf)], writes=["keysT"])
        release(nW)
        g2 = sb("g2", [128, 1024], F32); b2_ = sb("b2", [128, 1024], F32)
        bcast_row(g2[:], "g2", W["ln2_g"][l], 1024); bcast_row(b2_[:], "b2", W["ln2_b"][l], 1024)
        x1Tp = sb("x1Tp", [128, 8, ST], BF16); qpT = sb("qpT", [128, 16, ST], BF16)
        sc = sb("sc", [128, 16, 128], F32)
        mxa = sb("mxa", [128, 2, 8], F32); mxb_ = sb("mxb", [128, 2, 8], F32); mia = sb("mia", [128, 2, 8], U32); mib = sb("mib", [128, 2, 8], U32)
        tmpk = sb("tmpk", [128, 2, 256], F32)
        V12 = sb("V12", [128, 16, 16], F32); I12 = sb("I12", [128, 16, 16], F32)
        cand = sb("cand", [128, 8, 256], F32); best = sb("best", [128, 8, 16], F32); posu = sb("posu", [128, 8, 16], U32)
        pa = sb("pa", [128, 8, 16], U32); pbb = sb("pbb", [128, 8, 16], U32); paf = sb("paf", [128, 8, 16], F32); pbf = sb("pbf", [128, 8, 16], F32)
        eq = sb("eq", [128, 8, 16, 16], F32); gate = sb("gate", [128, 8, 16], F32); zsum = sb("zsum", [128, 8], F32)
        isel = sb("isel", [128, 3, 128], F32)
        iT = sb("iT", [128, 3, ST], BF16)
        Ab = sb("Ab", [128, 2, 16, 128], BF16); Bb = sb("Bb", [128, 2, 16, 128], BF16)
        Wsb = sb("Wsb", [128, ST, 128], BF16)
        UTi = sb("UTi", [128, 2, 1024], BF16); Vi = sb("Vi", [128, 2, 1024], BF16)
        gl_ = sb("gl", [128, 2, ST], BF16); Gt = sb("Gt", [128, 2, ST], BF16)
        xfP = sb("xfP", [128, 2, 1024], F32); zs = sb("zsP", [128, 2, 1024], F32); st = sb("stP", [128, 2, 6], F32); mv = sb("mvP", [128, 2], F32)

        def topk16(src_ap, srck, n, vout, iout, outk, s):
            op("dve", lambda: V.max(mxa[:, s, :], src_ap), reads=[srck], writes=[("mxa", s)])
            op("dve", lambda: V.max_index(mia[:, s, :], mxa[:, s, :], src_ap), reads=[srck, ("mxa", s)], writes=[("mia", s)], inline=False)
            op("dve", lambda: V.match_replace(tmpk[:, s, 0:n], mxa[:, s, :], src_ap, -1e30), reads=[srck, ("mxa", s)], writes=[("tmpk", s)], inline=False)
            op("dve", lambda: V.max(mxb_[:, s, :], tmpk[:, s, 0:n]), reads=[("tmpk", s)], writes=[("mxb", s)])
            op("dve", lambda: V.max_index(mib[:, s, :], mxb_[:, s, :], tmpk[:, s, 0:n]), reads=[("tmpk", s), ("mxb", s)], writes=[("mib", s)], inline=False)
            op("pool", lambda: G.tensor_copy(vout[:, 0:8], mxa[:, s, :]), reads=[("mxa", s)], writes=[outk])
            op("pool", lambda: G.tensor_copy(vout[:, 8:16], mxb_[:, s, :]), reads=[("mxb", s)], writes=[outk])
            op("pool", lambda: G.tensor_copy(iout[:, 0:8], mia[:, s, :]), reads=[("mia", s)], writes=[outk])
            op("pool", lambda: G.tensor_copy(iout[:, 8:16], mib[:, s, :]), reads=[("mib", s)], writes=[outk])

        def layer_norm2(zt, zk, gt, bt, gk, bk_, out_f32, outk):
            op("dve", lambda: V.bn_stats(st[:, 0, :], zt[:, 0:512]), reads=[zk], writes=["stP"])
            op("dve", lambda: V.bn_stats(st[:, 1, :], zt[:, 512:1024]), reads=[zk], writes=["stP"])
            op("dve", lambda: V.bn_aggr(mv[:], st[:]), reads=["stP"], writes=["mvP"])
            op("dve", lambda: V.tensor_scalar(mv[:, 1:2], mv[:, 1:2], EPS, None, op0=ALU.add), reads=["mvP"], writes=["mvP"])
            op("act", lambda: A.activation(mv[:, 1:2], mv[:, 1:2], AF.Sqrt), reads=["mvP"], writes=["mvP"])
            op("dve", lambda: V.reciprocal(mv[:, 1:2], mv[:, 1:2]), reads=["mvP"], writes=["mvP"])
            op("dve", lambda: V.tensor_scalar(zt, zt, mv[:, 0:1], mv[:, 1:2], op0=ALU.subtract, op1=ALU.mult), reads=[zk, "mvP"], writes=[zk])
            op("dve", lambda: V.tensor_tensor(zt, zt, gt, op=ALU.mult), reads=[zk, gk], writes=[zk])
            op("dve", lambda: V.tensor_tensor(out_f32, zt, bt, op=ALU.add), reads=[zk, bk_], writes=[outk])

        for si in range(NST):
            t0 = si * ST
            dma(x1Tp[:], x1T_d[:, :, t0:t0 + ST].rearrange("k p t -> p k t"), reads=["x1T_d"], writes=["x1Tp"])
            for c in range(16):
                b = c % 2
                for kc in range(8):
                    op("pe", lambda kc=kc: PE.matmul(PS[b][:, 0:ST], wq[:, kc, c * 128:(c + 1) * 128], x1Tp[:, kc, :], start=(kc == 0), stop=(kc == 7)), reads=["wq", "x1Tp"], writes=[psk(b)])
                evac(qpT[:, c, :], PS[b][:, 0:ST], reads=[psk(b)], writes=["qpT"])
            for tt in range(ST // 128):
                for cq in range(4):
                    b = 2 + cq % 2
                    for cc in range(4):
                        c = cq * 4 + cc
                        op("pe", lambda c=c, cc=cc: PE.matmul(PS[b][:, cc * 128:(cc + 1) * 128], qpT[:, c, tt * 128:(tt + 1) * 128], keysT[:, c % 2, :], start=True, stop=True),
                           reads=["qpT", "keysT"], writes=[psk(b)])
                    evac(sc[:, cq * 4:(cq + 1) * 4, :].rearrange("p c k -> p (c k)"), PS[b][:], reads=[psk(b)], writes=["sc"])
                for c in range(16):
                    topk16(sc[:, c, :], "sc", 128, V12[:, c, :], I12[:, c, :], "V12", c % 2)
                v12 = V12[:].rearrange("p (h f) a -> p h f a", f=2); i12 = I12[:].rearrange("p (h f) a -> p h f a", f=2)
                op("dve", lambda: V.tensor_tensor(cand[:].rearrange("p h (a b) -> p h a b", a=16), v12[:, :, 0, :].unsqueeze(3).to_broadcast([128, 8, 16, 16]),
                                                  v12[:, :, 1, :].unsqueeze(2).to_broadcast([128, 8, 16, 16]), op=ALU.add), reads=["V12"], writes=["cand"])
                for h in range(8):
                    topk16(cand[:, h, :], "cand", 256, best[:, h, :], posu[:, h, :], "best", h % 2)
                op("dve", lambda: V.tensor_tensor(gate[:], best[:], best[:, :, 0:1].to_broadcast([128, 8, 16]), op=ALU.subtract), reads=["best"], writes=["gate"])
                op("act", lambda: A.activation(gate[:], gate[:], AF.Exp), reads=["gate"], writes=["gate"])
                op("dve", lambda: V.tensor_reduce(zsum[:], gate[:], op=ALU.add, axis=AX.X), reads=["gate"], writes=["zsum"])
                op("dve", lambda: V.reciprocal(zsum[:], zsum[:]), reads=["zsum"], writes=["zsum"])
                op("dve", lambda: V.tensor_tensor(isel[:, 2, :].rearrange("p (h k) -> p h k", h=8), gate[:], zsum[:].unsqueeze(2).to_broadcast([128, 8, 16]), op=ALU.mult),
                   reads=["gate", "zsum"], writes=["isel"])
                op("dve", lambda: V.tensor_scalar(pa[:], posu[:], 4, None, op0=ALU.logical_shift_right), reads=["best"], writes=["pa"])
                op("dve", lambda: V.tensor_scalar(pbb[:], posu[:], 15, None, op0=ALU.bitwise_and), reads=["best"], writes=["pbb"])
                op("pool", lambda: G.tensor_copy(paf[:], pa[:]), reads=["pa"], writes=["paf"])
                op("pool", lambda: G.tensor_copy(pbf[:], pbb[:]), reads=["pbb"], writes=["pbf"])
                io16 = iota_f[:, 0:16].unsqueeze(1).unsqueeze(1).to_broadcast([128, 8, 16, 16])
                for which, pf, pfk in ((0, paf, "paf"), (1, pbf, "pbf")):
                    op("dve", lambda pf=pf: V.tensor_tensor(eq[:], pf[:].unsqueeze(3).to_broadcast([128, 8, 16, 16]), io16, op=ALU.is_equal), reads=[pfk, "iota_f"], writes=["eq"])
                    op("dve", lambda which=which: V.tensor_tensor(eq[:], eq[:], i12[:, :, which, :].unsqueeze(2).to_broadcast([128, 8, 16, 16]), op=ALU.mult), reads=["eq", "V12"], writes=["eq"])
                    op("dve", lambda which=which: V.tensor_reduce(isel[:, which, :].rearrange("p (h k) -> p h k", h=8), eq[:], op=ALU.add, axis=AX.X), reads=["eq"], writes=["isel"])
                for w3 in range(3):
                    b = 2 + w3 % 2
                    op("pe", lambda w3=w3: PE.transpose(PS[b][:, 0:128], isel[:, w3, :], ident_f[:]), reads=["isel", "ident_f"], writes=[psk(b)])
                    evac(iT[:, w3, tt * 128:(tt + 1) * 128], PS[b][:, 0:128], reads=[psk(b)], writes=["iT"])
            for sbk in range(ST // 16):
                s = sbk % 2; ts = slice(sbk * 16, (sbk + 1) * 16)
                iob = iota_b[:].unsqueeze(1).to_broadcast([128, 16, 128])
                op("dve", lambda: V.tensor_tensor(Ab[:, s, :, :], iob, iT[:, 0, ts].unsqueeze(2).to_broadcast([128, 16, 128]), op=ALU.is_equal), reads=["iota_b", "iT"], writes=[("Ab", s)])
                op("pool", lambda: G.tensor_tensor(Ab[:, s, :, :], Ab[:, s, :, :], iT[:, 2, ts].unsqueeze(2).to_broadcast([128, 16, 128]), op=ALU.mult), reads=[("Ab", s), "iT"], writes=[("Ab", s)])
                op("dve", lambda: V.tensor_tensor(Bb[:, s, :, :], iob, iT[:, 1, ts].unsqueeze(2).to_broadcast([128, 16, 128]), op=ALU.is_equal), reads=["iota_b", "iT"], writes=[("Bb", s)])
                for q4 in range(4):
                    b = 2 + q4 % 2
                    for u in range(4):
                        tl = q4 * 4 + u
                        op("pe", lambda tl=tl, u=u: PE.matmul(PS[b][:, u * 128:(u + 1) * 128], Bb[:, s, tl, :], Ab[:, s, tl, :], start=True, stop=True),
                           reads=[("Ab", s), ("Bb", s)], writes=[psk(b)])
                    tg = sbk * 16 + q4 * 4
                    evac(Wsb[:, tg:tg + 4, :].rearrange("p t i -> p (t i)"), PS[b][:], reads=[psk(b)], writes=["Wsb"])
            OB = [4, 5, 6, 7]

            def U_stage(i):
                s3 = i % 2; b = i % 2
                dma(UTi[:, s3, :], UT_d[l, i], reads=["UT_d"], writes=[("UTi", s3)])
                dma(Vi[:, s3, :], VS_d[l, i], reads=["VS_d"], writes=[("Vi", s3)])
                for kc in range(8):
                    op("pe", lambda kc=kc: PE.matmul(PS[b][:, 0:ST], UTi[:, s3, kc * 128:(kc + 1) * 128], x1Tp[:, kc, :], start=(kc == 0), stop=(kc == 7)),
                       reads=[("UTi", s3), "x1Tp"], writes=[psk(b)])

            def M_stage(i):
                s = i % 2; b = i % 2
                op("act", lambda: A.activation(gl_[:, s, :], PS[b][:, 0:ST], AF.Gelu), reads=[psk(b)], writes=[("gl", s)])
                op("dve", lambda: V.tensor_tensor(Gt[:, s, :], gl_[:, s, :], Wsb[:, :, i], op=ALU.mult), reads=[("gl", s), "Wsb"], writes=[("Gt", s)])

            def V_stage(i):
                s3 = i % 2; s = i % 2
                for tt in range(2):
                    for half in range(2):
                        op("pe", lambda tt=tt, half=half: PE.matmul(PS[OB[tt * 2 + half]][:], Gt[:, s, tt * 128:(tt + 1) * 128], Vi[:, s3, half * 512:(half + 1) * 512], start=(i == 0), stop=(i == 127)),
                           reads=[("Gt", s), ("Vi", s3)], writes=[psk(OB[tt * 2 + half])])
            U_stage(0)
            for i in range(128):
                if i + 1 < 128:
                    U_stage(i + 1)
                M_stage(i)
                V_stage(i)
            for tt in range(2):
                s = tt
                dma(xfP[:, s, :], x1_d[t0 + tt * 128:t0 + (tt + 1) * 128, :], reads=["x1_d"], writes=[("xfP", s)])
                for half in range(2):
                    op("dve", lambda half=half: V.scalar_tensor_tensor(zs[:, s, half * 512:(half + 1) * 512], xfP[:, s, half * 512:(half + 1) * 512], ALPHA, PS[OB[tt * 2 + half]][:], op0=ALU.mult, op1=ALU.add),
                       reads=[("xfP", s), psk(OB[tt * 2 + half])], writes=[("zsP", s)])
                layer_norm2(zs[:, s, :], ("zsP", s), g2[:], b2_[:], "g2", "b2", zs[:, s, :], ("zsP", s))
                dma(x_dst[t0 + tt * 128:t0 + (tt + 1) * 128, :], zs[:, s, :], reads=[("zsP", s)], writes=[xdk])
        release(n0)
    kb.finish(["xsrc%d" % DEPTH])
    return nc, kb


def host_consts(SEG, seq_of_seg, n_valid_seg):
    T = NSEG * SEG; NT = T // 128
    c = {}
    c["c_ident"] = np.eye(128, dtype=np.float32)
    pm = np.zeros((128, 128), np.float32)
    for hb in (0, 64):
        for e in range(8):
            pm[hb + e + 8, hb + e] = 1.0; pm[hb + e, hb + e + 8] = 1.0
    c["c_pm"] = pm
    pos = np.zeros(T, np.float64); segstart = np.zeros(T, np.int64); segend = np.zeros(T, np.int64)
    s = 0
    while s < NSEG:
        e = s
        while e + 1 < NSEG and seq_of_seg[e + 1] == seq_of_seg[s]:
            e += 1
        n = (e - s + 1) * SEG
        pos[s * SEG:s * SEG + n] = np.arange(n)
        segstart[s * SEG:s * SEG + n] = s * SEG; segend[s * SEG:s * SEG + n] = s * SEG + n
        s = e + 1
    inv = (500000.0 ** (-np.arange(0, 16, 2, dtype=np.float32) / 16)).astype(np.float32)
    ang = (pos.astype(np.float32)[None, :] * inv[:, None]).astype(np.float32)
    cosT = np.ones((128, T), np.float32); sinT = np.zeros((128, T), np.float32)
    for hb in (0, 64):
        cosT[hb:hb + 8] = np.cos(ang); cosT[hb + 8:hb + 16] = np.cos(ang)
        sinT[hb:hb + 8] = -np.sin(ang); sinT[hb + 8:hb + 16] = np.sin(ang)
    c["c_cos"] = cosT; c["c_sin"] = sinT
    am = np.zeros((128, 25, 128), np.float32)
    kk = np.arange(128)[:, None]; qq = np.arange(128)[None, :]
    for ci, (g, dl) in enumerate(COMBOS):
        win, dil = PATTERNS[g]
        diff = 128 * dl + kk - qq
        am[:, ci, :] = ((diff % dil == 0) & (np.abs(diff) <= (win // 2 // dil) * dil)).astype(np.float32)
    c["c_amask"] = am.reshape(128, 25 * 128)
    fl = np.zeros((NT, 17), np.float32)
    for j in range(NT):
        for dl in range(-8, 9):
            b = j + dl
            if 0 <= b < NT and segstart[j * 128] <= b * 128 < segend[j * 128]:
                fl[j, dl + 8] = 1.0
    c["c_aflag"] = np.broadcast_to(fl.reshape(1, -1), (128, NT * 17)).copy()
    lf = np.zeros((128, NSEG), np.float32)
    for sg in range(1, NSEG):
        lf[0:64, sg] = 1.0 if seq_of_seg[sg] == seq_of_seg[sg - 1] else 0.0
        bs = NSEG - 1 - sg
        lf[64:128, sg] = 1.0 if seq_of_seg[bs] == seq_of_seg[bs + 1] else 0.0
    c["c_lflag"] = lf
    c["c_iota"] = np.broadcast_to(np.arange(128, dtype=np.float32)[None, :], (128, 128)).copy()
    ti = np.arange(128) // 16
    c["c_mL"] = (ti[None, :] >= ti[:, None]).astype(np.float32)
    c["c_mU"] = (ti[None, :] <= ti[:, None]).astype(np.float32)
    return c


def kernel(**inputs):
    SEG = 2048; DEPTH = 4
    wshapes = {n: inputs[n].shape for n in WNAMES}
    nc, kb = build(SEG, DEPTH, wshapes)
    xp = np.asarray(inputs["x_prompt"], np.float32); xsm = np.asarray(inputs["x_sample"], np.float32)
    mp = np.asarray(inputs["mem_prompt"], np.float32); msm = np.asarray(inputs["mem_sample"], np.float32)
    wd = {n: np.ascontiguousarray(np.asarray(inputs[n], np.float32)) for n in WNAMES}
    in_maps = []
    for core in range(8):
        if core < 4:
            x = xp[core]; mem = np.broadcast_to(mp[core][None], (NSEG, MEM, D)); seqs = [0, 0, 0, 0]
        elif core < 6:
            sl = slice((core - 4) * 4, (core - 4) * 4 + 4)
            x = xsm[sl].reshape(NSEG * SEG, D); mem = msm[sl]; seqs = [0, 1, 2, 3]
        else:
            x = np.zeros((NSEG * SEG, D), np.float32); mem = np.zeros((NSEG, MEM, D), np.float32); seqs = [0, 1, 2, 3]
        m = {"x": np.ascontiguousarray(x), "mem": np.ascontiguousarray(mem)}
        m.update(wd); m.update(host_consts(SEG, seqs, NSEG))
        in_maps.append(m)
    res = run_bass_kernel_spmd(nc, in_maps, core_ids=list(range(8)))
    yp = np.stack([res.results[c]["y"] for c in range(4)], 0).astype(np.float32)
    ys = np.concatenate([res.results[c]["y"].reshape(4, SEG, D) for c in (4, 5)], 0).astype(np.float32)
    return (yp, ys)
```

```python
import math
import numpy as np
import concourse.bass as bass
import concourse.mybir as mybir
from concourse.bass_utils import run_bass_kernel_spmd

F32 = mybir.dt.float32; BF16 = mybir.dt.bfloat16; U32 = mybir.dt.uint32
AF = mybir.ActivationFunctionType; ALU = mybir.AluOpType; AX = mybir.AxisListType

D = 1024; NSEG = 4; MEM = 256
ALPHA = 8.0 ** 0.25; EPS = 1e-5
PATTERNS = ((128, 1), (512, 4), (2048, 16))
GD = {0: (-1, 1), 1: (-2, 2), 2: (-8, 8)}
COMBOS = [(g, dl) for g in range(3) for dl in range(GD[g][0], GD[g][1] + 1)]
L = 16


class KB:
    NDMA = 24

    def __init__(self, nc):
        self.nc = nc
        self.eng = {"pe": nc.tensor, "act": nc.scalar, "dve": nc.vector, "pool": nc.gpsimd, "sp": nc.sync}
        self.sem = {}; self.cnt = {}
        for e in ["pe", "act", "dve", "pool"]:
            self.sem[e] = nc.semaphore("sem_" + e).__enter__(); self.cnt[e] = 0
        for i in range(self.NDMA):
            self.sem[("dma", i)] = nc.semaphore("sem_dma%d" % i).__enter__(); self.cnt[("dma", i)] = 0
        self.dma_rr = 0
        self.waited = {e: {} for e in ["pe", "act", "dve", "pool", "sp"]}
        self.last_w = {}; self.readers = {}; self.nops = 0
        self.snap = {}; self.dirty = {}

    def _deps(self, reads, writes):
        deps = []
        for b in list(reads) + list(writes):
            if b in self.last_w:
                deps.append(self.last_w[b])
        for b in writes:
            deps.extend(self.readers.get(b, []))
        return deps

    def _know(self, s, v):
        h = self.snap.get(s)
        if not h:
            return None
        lo, hi = 0, len(h) - 1
        if h[0][0] > v:
            return None
        while lo < hi:
            mid = (lo + hi + 1) // 2
            if h[mid][0] <= v:
                lo = mid
            else:
                hi = mid - 1
        return h[lo][1]

    def _wait(self, e, deps, keep_last=False):
        w = self.waited[e]; need = {}
        for (s, v) in deps:
            if v > w.get(s, 0) and v > need.get(s, 0):
                need[s] = v
        todo = []
        for s, v in sorted(need.items(), key=lambda kv: (isinstance(kv[0], tuple), -kv[1])):
            if v <= w.get(s, 0):
                continue
            todo.append((s, v)); w[s] = v
            k = self._know(s, v)
            if k:
                for s2, v2 in k.items():
                    if v2 > w.get(s2, 0):
                        w[s2] = v2
        last = None
        if keep_last and todo:
            last = todo.pop()
        for s, v in todo:
            self.eng[e].wait_ge(self.sem[s], v)
        if todo or last is not None:
            self.dirty[e] = True
        return last

    def _record(self, tok, reads, writes):
        for b in writes:
            self.last_w[b] = tok; self.readers[b] = []
        for b in reads:
            self.readers.setdefault(b, []).append(tok)

    def op(self, e, fn, reads=(), writes=(), inline=True):
        last = self._wait(e, self._deps(reads, writes), keep_last=inline)
        ins = fn()
        if last is not None:
            ins._wait_ge(self.sem[last[0]], last[1])
        if self.dirty.get(e, True):
            self.snap.setdefault(e, []).append((self.cnt[e] + 1, dict(self.waited[e]))); self.dirty[e] = False
        self.cnt[e] += 1
        ins.then_inc(self.sem[e], 1)
        self._record((e, self.cnt[e]), reads, writes)
        self.nops += 1
        return ins

    def dma(self, out, in_, reads=(), writes=(), **kw):
        s = ("dma", self.dma_rr); self.dma_rr = (self.dma_rr + 1) % self.NDMA
        deps = self._deps(reads, writes); deps.append((s, self.cnt[s]))
        last = self._wait("sp", deps, keep_last=True)
        ins = self.nc.sync.dma_start(out=out, in_=in_, **kw)
        if last is not None:
            ins._wait_ge(self.sem[last[0]], last[1])
        self.snap.setdefault(s, []).append((self.cnt[s] + 16, dict(self.waited["sp"])))
        self.cnt[s] += 16
        ins.then_inc(self.sem[s], 16)
        self._record((s, self.cnt[s]), reads, writes)
        self.nops += 1

    def barrier(self):
        allc = [(k, v) for k, v in self.cnt.items() if v > 0]
        for e in ["pe", "act", "dve", "pool", "sp"]:
            self._wait(e, allc)

    def finish(self, bufs):
        self._wait("sp", [self.last_w[b] for b in bufs if b in self.last_w])


WNAMES = ["w_in", "b_gate", "w_att_out", "w_ssm_out", "w_mem_out", "w_mix_out", "w_mem_kv", "lam_re", "lam_im",
          "log_step", "b_re", "b_im", "c_re", "c_im", "d_skip", "w_glu", "b_glu", "ln1_g", "ln1_b", "w_query",
          "sub_keys", "expert_u", "expert_v", "ln2_g", "ln2_b"]


def build(SEG, DEPTH, wshapes, debug=()):
    T = NSEG * SEG; NT = T // 128; NB = T // 512; NCH = T // L; CPS = SEG // L
    assert CPS == 128 or SEG < 2048
    nc = bass.Bass("TRN2", target_bir_lowering=False)
    kb = KB(nc)
    op = kb.op; dma = kb.dma
    V = nc.vector; A = nc.scalar; G = nc.gpsimd; PE = nc.tensor

    def din(name, shape, dt=F32):
        return nc.dram_tensor(name, list(shape), dt, kind="ExternalInput").ap()

    def dscr(name, shape, dt):
        kind = "ExternalOutput" if name in debug else "Internal"
        return nc.dram_tensor(name, list(shape), dt, kind=kind).ap()

    x_in = din("x", [T, D]); mem_in = din("mem", [NSEG, MEM, D])
    W = {n: din(n, wshapes[n]) for n in WNAMES}
    c_ident = din("c_ident", [128, 128]); c_pm = din("c_pm", [128, 128])
    c_cos = din("c_cos", [128, T]); c_sin = din("c_sin", [128, T])
    c_amask = din("c_amask", [128, 25 * 128]); c_aflag = din("c_aflag", [128, NT * 17])
    c_lflag = din("c_lflag", [128, NSEG]); c_iota = din("c_iota", [128, 128])
    c_mL = din("c_mL", [128, 128]); c_mU = din("c_mU", [128, 128])
    y_out = nc.dram_tensor("y", [T, D], F32, kind="ExternalOutput").ap()

    xs_d = [dscr("xs0", [T, D], F32), dscr("xs1", [T, D], F32)]
    x1_d = dscr("x1_d", [T, D], F32)
    xT_d = dscr("xT_d", [8, 128, T], BF16); x1T_d = dscr("x1T_d", [8, 128, T], BF16)
    qk_d = dscr("qk_d", [1536, T], BF16); vu_d = dscr("vu_d", [T, 1280], BF16)
    h_d = dscr("h_d", [T, 512], BF16); am_d = dscr("am_d", [T, 768], BF16)
    UT_d = dscr("UT_d", [DEPTH, 128, 128, 1024], BF16); VS_d = dscr("VS_d", [DEPTH, 128, 128, 1024], BF16)

    stack = []

    uid = [0]

    def sb(name, shape, dt):
        uid[0] += 1
        cm = nc.sbuf_tensor("%s_%d" % (name, uid[0]), list(shape), dt); t = cm.__enter__(); stack.append(cm); return t

    def release(n0):
        kb.barrier()
        while len(stack) > n0:
            stack.pop().__exit__(None, None, None)

    PS = [nc.psum_tensor("ps%d" % i, [128, 512], F32).__enter__() for i in range(8)]
    psk = lambda b: ("ps", b)

    ident_f = sb("ident_f", [128, 128], F32); ident_b = sb("ident_b", [128, 128], BF16)
    pm_f = sb("pm_f", [128, 128], F32)
    amask = sb("amask", [128, 25, 128], BF16); aflag = sb("aflag", [128, NT * 17], F32)
    lflag = sb("lflag", [128, NSEG], F32); iota_f = sb("iota_f", [128, 128], F32); iota_b = sb("iota_b", [128, 128], BF16)
    mL = sb("mL", [128, 128], F32); mU = sb("mU", [128, 128], F32)
    nstg = len(stack)
    stg = sb("stg_c", [128, 25 * 128], F32)
    dma(ident_f[:], c_ident[:, :], writes=["ident_f"]); dma(pm_f[:], c_pm[:, :], writes=["pm_f"])
    dma(aflag[:], c_aflag[:, :], writes=["aflag"]); dma(lflag[:], c_lflag[:, :], writes=["lflag"])
    dma(iota_f[:], c_iota[:, :], writes=["iota_f"]); dma(mL[:], c_mL[:, :], writes=["mL"]); dma(mU[:], c_mU[:, :], writes=["mU"])
    dma(stg[:], c_amask[:, :], writes=["stg_c"])
    op("dve", lambda: V.tensor_copy(amask[:].rearrange("p a b -> p (a b)"), stg[:]), reads=["stg_c"], writes=["amask"])
    op("dve", lambda: V.tensor_copy(ident_b[:], ident_f[:]), reads=["ident_f"], writes=["ident_b"])
    op("dve", lambda: V.tensor_copy(iota_b[:], iota_f[:]), reads=["iota_f"], writes=["iota_b"])
    release(nstg)
    NG = len(stack)

    rr = {"ev": 0, "cast": 0}

    def evac(out, in_, reads, writes, eng=None):
        if eng is None:
            eng = ("act", "dve")[rr["ev"] % 2]; rr["ev"] += 1
        if eng == "act":
            op("act", lambda: A.copy(out, in_), reads=reads, writes=writes)
        else:
            op("dve", lambda: V.tensor_copy(out, in_), reads=reads, writes=writes)

    def cast(out, in_, reads, writes, eng=None):
        if eng is None:
            eng = ("pool", "dve", "act")[rr["cast"] % 3]; rr["cast"] += 1
        if eng == "pool":
            op("pool", lambda: G.tensor_copy(out, in_), reads=reads, writes=writes)
        elif eng == "dve":
            op("dve", lambda: V.tensor_copy(out, in_), reads=reads, writes=writes)
        else:
            op("act", lambda: A.copy(out, in_), reads=reads, writes=writes)

    def load_w_bf16(dst, dkey, src_ap, rows_chunks, ncols, stgt, skey):
        cw = min(ncols, 2048)
        i = 0
        for kc in range(rows_chunks):
            for c0 in range(0, ncols, cw):
                s = i % 2; i += 1; w_ = min(cw, ncols - c0)
                dma(stgt[:, s, 0:w_], src_ap[kc * 128:(kc + 1) * 128, c0:c0 + w_], writes=[(skey, s)])
                cast(dst[:, kc, c0:c0 + w_], stgt[:, s, 0:w_], reads=[(skey, s)], writes=[dkey])

    def bcast_row(dst, dkey, src_row_ap, n):
        dma(dst, src_row_ap.partition_broadcast(128), writes=[dkey])

    n0 = len(stack)
    ustg = sb("ustg", [128, 2, 1024], F32); ubf = sb("ubf", [128, 2, 1024], BF16)
    utsb = sb("utsb", [128, 2, 1024], BF16); vbf = sb("vbf", [128, 2, 1024], BF16); vstg = sb("vstg", [128, 2, 1024], F32)
    items = [(l, i) for l in range(DEPTH) for i in range(128)]

    def prep_load(it):
        l, i = items[it]; s = it % 2
        dma(ustg[:, s, :], W["expert_u"][l, i * 128:(i + 1) * 128, :], writes=[("ustg", s)])
        dma(vstg[:, s, :], W["expert_v"][l, i * 128:(i + 1) * 128, :], writes=[("vstg", s)])

    def prep_work(it):
        l, i = items[it]; s = it % 2; b = it % 2
        cast(ubf[:, s, :], ustg[:, s, :], reads=[("ustg", s)], writes=[("ubf", s)], eng="pool")
        pT = PS[b][:].bitcast(BF16)
        for kc in range(8):
            op("pe", lambda kc=kc: PE.transpose(pT[:, kc * 128:(kc + 1) * 128], ubf[:, s, kc * 128:(kc + 1) * 128], ident_b[:]),
               reads=[("ubf", s), "ident_b"], writes=[psk(b)])
        evac(utsb[:, s, :], pT, reads=[psk(b)], writes=[("utsb", s)])
        cast(vbf[:, s, :], vstg[:, s, :], reads=[("vstg", s)], writes=[("vbf", s)], eng="dve" if i % 2 else "act")
        dma(UT_d[l, i], utsb[:, s, :], reads=[("utsb", s)], writes=["UT_d"])
        dma(VS_d[l, i], vbf[:, s, :], reads=[("vbf", s)], writes=["VS_d"])
    prep_load(0)
    for it in range(len(items)):
        if it + 1 < len(items):
            prep_load(it + 1)
        prep_work(it)
    release(n0)

    NG2 = len(stack)

    for l in range(DEPTH):
        x_src = x_in if l == 0 else xs_d[(l - 1) % 2]
        x_dst = y_out if l == DEPTH - 1 else xs_d[l % 2]
        xsk = "xsrc%d" % l; xdk = "xsrc%d" % (l + 1)

        n0 = len(stack)
        wA = sb("wA", [128, 8, 2816], BF16); wstg = sb("wstgA", [128, 2, 2048], F32)
        load_w_bf16(wA, "wA", W["w_in"][l][:, 0:2816], 8, 2816, wstg, "wstgA")
        xf = sb("xfA", [128, 2, 1024], F32); xb = sb("xbA", [128, 2, 1024], BF16)
        xT = sb("xTA", [128, 2, 8, 512], BF16)
        cs = sb("csA", [128, 2, 512], F32); sn = sb("snA", [128, 2, 512], F32)
        qs = sb("qsA", [128, 2, 512], F32); t1 = sb("t1A", [128, 2, 512], F32); t2 = sb("t2A", [128, 2, 512], F32)
        qr = sb("qrA", [128, 2, 512], BF16); vu = sb("vuA", [128, 2, 1280], BF16)
        pb = 0
        for bi in range(NB):
            t0 = bi * 512; xs_ = bi % 2
            dma(cs[:, xs_, :], c_cos[:, t0:t0 + 512], writes=[("csA", xs_)])
            dma(sn[:, xs_, :], c_sin[:, t0:t0 + 512], writes=[("snA", xs_)])
            for tt in range(4):
                s = tt % 2
                dma(xf[:, s, :], x_src[t0 + tt * 128:t0 + (tt + 1) * 128, :], reads=[xsk], writes=[("xfA", s)])
                cast(xb[:, s, :], xf[:, s, :], reads=[("xfA", s)], writes=[("xbA", s)])
                b = pb % 8; pb += 1
                pT = PS[b][:].bitcast(BF16)
                for kc in range(8):
                    op("pe", lambda kc=kc: PE.transpose(pT[:, kc * 128:(kc + 1) * 128], xb[:, s, kc * 128:(kc + 1) * 128], ident_b[:]),
                       reads=[("xbA", s), "ident_b"], writes=[psk(b)])
                evac(xT[:, xs_, :, tt * 128:(tt + 1) * 128], pT.rearrange("p (k t) -> p k t", k=8), reads=[psk(b)], writes=[("xTA", xs_)])
            dma(xT_d[:, :, t0:t0 + 512].rearrange("k p t -> p k t"), xT[:, xs_, :, :], reads=[("xTA", xs_)], writes=["xT_d"])
            for c in range(12):
                b = pb % 8; pb += 1; s = c % 2
                for kc in range(8):
                    op("pe", lambda kc=kc: PE.matmul(PS[b][:], wA[:, kc, c * 128:(c + 1) * 128], xT[:, xs_, kc, :], start=(kc == 0), stop=(kc == 7)),
                       reads=["wA", ("xTA", xs_)], writes=[psk(b)])
                op("act", lambda: A.copy(qs[:, s, :], PS[b][:]), reads=[psk(b)], writes=[("qsA", s)])
                b2 = pb % 8; pb += 1
                op("pe", lambda: PE.matmul(PS[b2][:], pm_f[:], qs[:, s, :], start=True, stop=True), reads=["pm_f", ("qsA", s)], writes=[psk(b2)])
                op("dve", lambda: V.tensor_tensor(t1[:, s, :], qs[:, s, :], cs[:, xs_, :], op=ALU.mult), reads=[("qsA", s), ("csA", xs_)], writes=[("t1A", s)])
                op("dve", lambda: V.tensor_tensor(t2[:, s, :], PS[b2][:], sn[:, xs_, :], op=ALU.mult), reads=[psk(b2), ("snA", xs_)], writes=[("t2A", s)])
                op("pool", lambda: G.tensor_tensor(qr[:, s, :], t1[:, s, :], t2[:, s, :], op=ALU.add), reads=[("t1A", s), ("t2A", s)], writes=[("qrA", s)])
                dma(qk_d[c * 128:(c + 1) * 128, t0:t0 + 512], qr[:, s, :], reads=[("qrA", s)], writes=["qk_d"])
            for tt in range(4):
                s = tt % 2
                for (c0, c1, o0) in ((1536, 2048, 0), (2048, 2304, 512), (2304, 2816, 768)):
                    b = pb % 8; pb += 1
                    for kc in range(8):
                        op("pe", lambda kc=kc: PE.matmul(PS[b][:, 0:c1 - c0], xT[:, xs_, kc, tt * 128:(tt + 1) * 128], wA[:, kc, c0:c1], start=(kc == 0), stop=(kc == 7)),
                           reads=["wA", ("xTA", xs_)], writes=[psk(b)])
                    evac(vu[:, s, o0:o0 + c1 - c0], PS[b][:, 0:c1 - c0], reads=[psk(b)], writes=[("vuA", s)])
                dma(vu_d[t0 + tt * 128:t0 + (tt + 1) * 128, :], vu[:, s, :], reads=[("vuA", s)], writes=["vu_d"])
        release(n0)

        for gh in range(2):
            n0 = len(stack)
            NGR = 16; g0 = gh * 16
            Tm = sb("Tm", [128, NGR, 3, 128], BF16)
            WB = sb("WB", [128, NGR, 2, 2, 128], BF16)
            CK = sb("CK", [128, NGR, 2, 256], BF16)
            dcol = sb("dcol", [128, NGR], F32)
            Z = sb("Z", [128, 2, NGR], F32); A1 = sb("A1", [128, 2, NGR], F32); A2 = sb("A2", [128, 2, NGR], F32)
            tA = sb("tA", [128, 2, NGR], F32); tB2 = sb("tB2", [128, 2, NGR], F32); Zc = sb("Zc", [128, 2, NGR], F32)
            nT = len(stack)
            lre = sb("lre", [128, NGR], F32); lim = sb("lim", [128, NGR], F32); dt_ = sb("dt", [128, NGR], F32)
            for d_ in range(2):
                dma(lre[d_ * 64:(d_ + 1) * 64, :], W["lam_re"][l, d_, g0:g0 + 16].rearrange("g p -> p g"), writes=["lre"], allow_slow_non_contiguous=True)
                dma(lim[d_ * 64:(d_ + 1) * 64, :], W["lam_im"][l, d_, g0:g0 + 16].rearrange("g p -> p g"), writes=["lim"], allow_slow_non_contiguous=True)
                dma(dt_[d_ * 64:(d_ + 1) * 64, :], W["log_step"][l, d_, g0:g0 + 16].partition_broadcast(64), writes=["dt"])
            Br = sb("Br", [128, NGR, 16], F32); Bi = sb("Bi", [128, NGR, 16], F32)
            for d_ in range(2):
                dma(Br[d_ * 64:(d_ + 1) * 64, :, :], W["b_re"][l, d_, g0:g0 + 16].rearrange("g p c -> p g c"), writes=["Br"])
                dma(Bi[d_ * 64:(d_ + 1) * 64, :, :], W["b_im"][l, d_, g0:g0 + 16].rearrange("g p c -> p g c"), writes=["Bi"])
            Cr = sb("Cr", [128, NGR, 16], F32); Ci = sb("Ci", [128, NGR, 16], F32)
            cstg = sb("cstg", [128, 2, 128], F32)
            it = 0
            for (src, dstt, dk) in ((W["c_re"], Cr, "Cr"), (W["c_im"], Ci, "Ci")):
                for gq in range(2):
                    b = it % 8; s = it % 2; it += 1
                    for d_ in range(2):
                        dma(cstg[:, s, d_ * 64:(d_ + 1) * 64], src[l, d_, g0 + gq * 8:g0 + (gq + 1) * 8].rearrange("g c p -> (g c) p"), writes=[("cstg", s)])
                    op("pe", lambda: PE.transpose(PS[b][:, 0:128], cstg[:, s, :], ident_f[:]), reads=[("cstg", s), "ident_f"], writes=[psk(b)])
                    evac(dstt[:, gq * 8:(gq + 1) * 8, :].rearrange("p g c -> p (g c)"), PS[b][:, 0:128], reads=[psk(b)], writes=[dk])
            dsk = sb("dsk", [128, 4], F32)
            sA = lambda nm: sb(nm, [128, NGR], F32)
            xr_ = sA("xr_"); xi_ = sA("xi_"); mag = sA("mag"); are = sA("are"); aim = sA("aim"); kk = sA("kk"); tq = sA("tq"); yy = sA("yy")
            op("act", lambda: A.activation(dt_[:], dt_[:], AF.Exp), reads=["dt"], writes=["dt"])
            op("dve", lambda: V.tensor_scalar(lre[:], lre[:], -1e-4, None, op0=ALU.min), reads=["lre"], writes=["lre"])
            op("dve", lambda: V.tensor_tensor(xr_[:], dt_[:], lre[:], op=ALU.mult), reads=["dt", "lre"], writes=["xr_"])
            op("dve", lambda: V.tensor_tensor(xi_[:], dt_[:], lim[:], op=ALU.mult), reads=["dt", "lim"], writes=["xi_"])
            op("act", lambda: A.activation(mag[:], xr_[:], AF.Exp), reads=["xr_"], writes=["mag"])

            def sin_of(dst, dkey, shift):
                op("dve", lambda: V.tensor_scalar(yy[:], xi_[:], float(shift), None, op0=ALU.add), reads=["xi_"], writes=["yy"])
                op("dve", lambda: V.memset(kk[:], 0.0), writes=["kk"])
                for j in range(1, 6):
                    op("dve", lambda j=j: V.tensor_scalar(tq[:], yy[:], float((2 * j - 1) * math.pi), None, op0=ALU.is_ge), reads=["yy"], writes=["tq"])
                    op("dve", lambda: V.tensor_tensor(kk[:], kk[:], tq[:], op=ALU.add), reads=["kk", "tq"], writes=["kk"])
                op("dve", lambda: V.scalar_tensor_tensor(yy[:], kk[:], float(-2 * math.pi), yy[:], op0=ALU.mult, op1=ALU.add), reads=["kk", "yy"], writes=["yy"])
                op("act", lambda: A.activation(dst[:], yy[:], AF.Sin), reads=["yy"], writes=[dkey])
            sin_of(aim, "aim", 0.0); sin_of(are, "are", math.pi / 2)
            op("dve", lambda: V.tensor_tensor(are[:], are[:], mag[:], op=ALU.mult), reads=["are", "mag"], writes=["are"])
            op("dve", lambda: V.tensor_tensor(aim[:], aim[:], mag[:], op=ALU.mult), reads=["aim", "mag"], writes=["aim"])
            den = sA("den"); zr = sA("zr"); zi = sA("zi"); am1 = sA("am1"); tz = sA("tz")
            op("dve", lambda: V.tensor_tensor(den[:], lre[:], lre[:], op=ALU.mult), reads=["lre"], writes=["den"])
            op("dve", lambda: V.tensor_tensor(tz[:], lim[:], lim[:], op=ALU.mult), reads=["lim"], writes=["tz"])
            op("dve", lambda: V.tensor_tensor(den[:], den[:], tz[:], op=ALU.add), reads=["den", "tz"], writes=["den"])
            op("dve", lambda: V.reciprocal(den[:], den[:]), reads=["den"], writes=["den"])
            op("dve", lambda: V.tensor_scalar(am1[:], are[:], -1.0, None, op0=ALU.add), reads=["are"], writes=["am1"])
            op("dve", lambda: V.tensor_tensor(zr[:], am1[:], lre[:], op=ALU.mult), reads=["am1", "lre"], writes=["zr"])
            op("dve", lambda: V.tensor_tensor(tz[:], aim[:], lim[:], op=ALU.mult), reads=["aim", "lim"], writes=["tz"])
            op("dve", lambda: V.tensor_tensor(zr[:], zr[:], tz[:], op=ALU.add), reads=["zr", "tz"], writes=["zr"])
            op("dve", lambda: V.tensor_tensor(zr[:], zr[:], den[:], op=ALU.mult), reads=["zr", "den"], writes=["zr"])
            op("dve", lambda: V.tensor_tensor(zi[:], aim[:], lre[:], op=ALU.mult), reads=["aim", "lre"], writes=["zi"])
            op("dve", lambda: V.tensor_tensor(tz[:], am1[:], lim[:], op=ALU.mult), reads=["am1", "lim"], writes=["tz"])
            op("dve", lambda: V.tensor_tensor(zi[:], zi[:], tz[:], op=ALU.subtract), reads=["zi", "tz"], writes=["zi"])
            op("dve", lambda: V.tensor_tensor(zi[:], zi[:], den[:], op=ALU.mult), reads=["zi", "den"], writes=["zi"])

            def cmul(orr, oi, ork, oik, ar, ai, ark, aik, br, bi, brk, bik, tmpa, tmpak, eng="dve"):
                E = V if eng == "dve" else G
                op(eng, lambda: E.tensor_tensor(tmpa, ai, bi, op=ALU.mult), reads=aik + bik, writes=[tmpak])
                op(eng, lambda: E.tensor_tensor(orr, ar, br, op=ALU.mult), reads=ark + brk, writes=[ork])
                op(eng, lambda: E.tensor_tensor(orr, orr, tmpa, op=ALU.subtract), reads=[ork, tmpak], writes=[ork])
                op(eng, lambda: E.tensor_tensor(tmpa, ai, br, op=ALU.mult), reads=aik + brk, writes=[tmpak])
                op(eng, lambda: E.tensor_tensor(oi, ar, bi, op=ALU.mult), reads=ark + bik, writes=[oik])
                op(eng, lambda: E.tensor_tensor(oi, oi, tmpa, op=ALU.add), reads=[oik, tmpak], writes=[oik])
            Bbr = sb("Bbr", [128, NGR, 16], F32); Bbi = sb("Bbi", [128, NGR, 16], F32); tB = sb("tB", [128, NGR, 16], F32)
            zb = lambda t: t[:].unsqueeze(2).to_broadcast([128, NGR, 16])
            cmul(Bbr[:], Bbi[:], "Bbr", "Bbi", zb(zr), zb(zi), ["zr"], ["zi"], Br[:], Bi[:], ["Br"], ["Bi"], tB[:], "tB")
            pwr = sb("pwr", [128, NGR, 33], F32); pwi = sb("pwi", [128, NGR, 33], F32)
            ipr = sb("ipr", [128, NGR, 17], F32); ipi = sb("ipi", [128, NGR, 17], F32)
            air = sA("air"); aii = sA("aii"); tp = sA("tp")
            op("dve", lambda: V.tensor_tensor(tz[:], mag[:], mag[:], op=ALU.mult), reads=["mag"], writes=["tz"])
            op("dve", lambda: V.reciprocal(tz[:], tz[:]), reads=["tz"], writes=["tz"])
            op("dve", lambda: V.tensor_tensor(air[:], are[:], tz[:], op=ALU.mult), reads=["are", "tz"], writes=["air"])
            op("dve", lambda: V.scalar_tensor_tensor(aii[:], aim[:], -1.0, tz[:], op0=ALU.mult, op1=ALU.mult), reads=["aim", "tz"], writes=["aii"])
            op("dve", lambda: V.memset(pwr[:, :, 0:1], 1.0), writes=[("pw", 0)]); op("dve", lambda: V.memset(pwi[:, :, 0:1], 0.0), writes=[("pwi_", 0)])
            op("pool", lambda: G.memset(ipr[:, :, 0:1], 1.0), writes=[("ip", 0)]); op("pool", lambda: G.memset(ipi[:, :, 0:1], 0.0), writes=[("ipi_", 0)])
            tp2 = sA("tp2")
            for m in range(32):
                cmul(pwr[:, :, m + 1], pwi[:, :, m + 1], ("pw", m + 1), ("pwi_", m + 1), pwr[:, :, m], pwi[:, :, m], [("pw", m)], [("pwi_", m)],
                     are[:], aim[:], ["are"], ["aim"], tp[:], "tp")
            for m in range(16):
                cmul(ipr[:, :, m + 1], ipi[:, :, m + 1], ("ip", m + 1), ("ipi_", m + 1), ipr[:, :, m], ipi[:, :, m], [("ip", m)], [("ipi_", m)],
                     air[:], aii[:], ["air"], ["aii"], tp2[:], "tp2", eng="pool")
            PWK = [("pw", m) for m in range(33)] + [("pwi_", m) for m in range(33)]
            IPK = [("ip", m) for m in range(17)] + [("ipi_", m) for m in range(17)]
            rpr = sb("rpr", [128, NGR, 16], F32); rpi = sb("rpi", [128, NGR, 16], F32); tR = sb("tR", [128, NGR, 16], F32)
            a16r = pwr[:, :, 16:17].to_broadcast([128, NGR, 16]); a16i = pwi[:, :, 16:17].to_broadcast([128, NGR, 16])
            cmul(rpr[:], rpi[:], "rpr", "rpi", a16r, a16i, PWK, PWK, ipr[:, :, 0:16], ipi[:, :, 0:16], IPK, IPK, tR[:], "tR")
            op("dve", lambda: V.tensor_copy(A1[:, 0, :], pwr[:, :, 16]), reads=PWK, writes=["A1"])
            op("dve", lambda: V.tensor_copy(A1[:, 1, :], pwr[:, :, 16]), reads=PWK, writes=["A1"])
            op("dve", lambda: V.tensor_scalar(A2[:, 0, :], pwi[:, :, 16], -1.0, None, op0=ALU.mult), reads=PWK, writes=["A2"])
            op("dve", lambda: V.tensor_copy(A2[:, 1, :], pwi[:, :, 16]), reads=PWK, writes=["A2"])
            tBr = sb("tBr", [128, NGR, 16], F32); tBi = sb("tBi", [128, NGR, 16], F32)
            tCr = sb("tCr", [128, NGR, 16], F32); tCi = sb("tCi", [128, NGR, 16], F32)
            tKr = sb("tKr", [128, NGR, 16], F32); tKi = sb("tKi", [128, NGR, 16], F32)
            F_ = slice(0, 64); B_ = slice(64, 128)
            cp = lambda dst, src, rk, wk, e="dve": op(e, (lambda: V.tensor_copy(dst, src)) if e == "dve" else (lambda: G.tensor_copy(dst, src)), reads=rk, writes=[wk])
            cp(tBr[F_], ipr[F_, :, 0:16], IPK, "tBr"); cp(tBi[F_], ipi[F_, :, 0:16], IPK, "tBi")
            cp(tBr[B_], pwr[B_, :, 0:16], PWK, "tBr", "pool"); cp(tBi[B_], pwi[B_, :, 0:16], PWK, "tBi", "pool")
            cp(tCr[F_], pwr[F_, :, 0:16], PWK, "tCr"); cp(tCi[F_], pwi[F_, :, 0:16], PWK, "tCi")
            cp(tCr[B_, :, 0:8], ipr[B_, :, 0:8], IPK, "tCr", "pool"); cp(tCi[B_, :, 0:8], ipi[B_, :, 0:8], IPK, "tCi", "pool")
            cp(tCr[B_, :, 8:16], rpr[B_, :, 8:16], ["rpr"], "tCr", "pool"); cp(tCi[B_, :, 8:16], rpi[B_, :, 8:16], ["rpi"], "tCi", "pool")
            cp(tKr[F_], pwr[F_, :, 16:32], PWK, "tKr"); cp(tKi[F_], pwi[F_, :, 16:32], PWK, "tKi")
            cp(tKr[B_], rpr[B_], ["rpr"], "tKr", "pool"); cp(tKi[B_], rpi[B_], ["rpi"], "tKi", "pool")
            for tl in range(8):
                dma(dcol[tl * 16:(tl + 1) * 16, :], W["d_skip"][l, g0 * 16:(g0 + 16) * 16].rearrange("(g c) -> c g", c=16), writes=["dcol"], allow_slow_non_contiguous=True)
            GB = 2
            EBr = sb("EBr", [128, GB, 16, 16], F32); EBi = sb("EBi", [128, GB, 16, 16], F32)
            CTr = sb("CTr", [128, GB, 16, 16], F32); CTi = sb("CTi", [128, GB, 16, 16], F32)
            CKr = sb("CKr", [128, GB, 16, 16], F32); CKi = sb("CKi", [128, GB, 16, 16], F32)
            tE = sb("tE", [128, GB, 16, 16], F32)
            tsb = sb("tsbT", [128, 2, 128], F32)
            pb = 0
            for gb in range(NGR // GB):
                gs = slice(gb * GB, (gb + 1) * GB)
                pwb = lambda t: t[:, gs, :].unsqueeze(3).to_broadcast([128, GB, 16, 16])
                vb = lambda t: t[:, gs, :].unsqueeze(2).to_broadcast([128, GB, 16, 16])
                cmul(EBr[:], EBi[:], "EBr", "EBi", pwb(tBr), pwb(tBi), ["tBr"], ["tBi"], vb(Bbr), vb(Bbi), ["Bbr"], ["Bbi"], tE[:], "tE")
                cmul(CTr[:], CTi[:], "CTr", "CTi", pwb(tCr), pwb(tCi), ["tCr"], ["tCi"], vb(Cr), vb(Ci), ["Cr"], ["Ci"], tE[:], "tE")
                cmul(CKr[:], CKi[:], "CKr", "CKi", pwb(tKr), pwb(tKi), ["tKr"], ["tKi"], vb(Cr), vb(Ci), ["Cr"], ["Ci"], tE[:], "tE")
                op("dve", lambda: V.tensor_scalar(CTi[:], CTi[:], -1.0, None, op0=ALU.mult), reads=["CTi"], writes=["CTi"])
                op("dve", lambda: V.tensor_scalar(CKi[:], CKi[:], -1.0, None, op0=ALU.mult), reads=["CKi"], writes=["CKi"])
                op("act", lambda: A.copy(CK[:, gs, 0, :], CKr[:].rearrange("p g t c -> p g (t c)")), reads=["CKr"], writes=["CK"])
                op("act", lambda: A.copy(CK[:, gs, 1, :], CKi[:].rearrange("p g t c -> p g (t c)")), reads=["CKi"], writes=["CK"])
                for gl in range(GB):
                    g = gb * GB + gl
                    bF = pb % 8; pb += 1; bB = pb % 8; pb += 1
                    eb = lambda t, pr: t[pr, gl, 0:8, :].rearrange("p t c -> p (t c)")
                    ca = lambda t, pr: t[pr, gl, :, :].rearrange("p t c -> p (t c)")
                    for (bank, pr) in ((bF, F_), (bB, B_)):
                        op("pe", lambda bank=bank, pr=pr: PE.matmul(PS[bank][:, 0:256], eb(EBr, pr), ca(CTr, pr), start=True, stop=False),
                           reads=["EBr", "CTr"], writes=[psk(bank)])
                        op("pe", lambda bank=bank, pr=pr: PE.matmul(PS[bank][:, 0:256], eb(EBi, pr), ca(CTi, pr), start=False, stop=True),
                           reads=["EBi", "CTi"], writes=[psk(bank)])
                    s = g % 2
                    op("dve", lambda: V.tensor_tensor(tsb[:, s, :], PS[bF][:, 0:128], mL[:], op=ALU.mult), reads=[psk(bF), "mL"], writes=[("tsbT", s)])
                    op("dve", lambda: V.scalar_tensor_tensor(tsb[:, s, :], ident_f[:], dcol[:, g:g + 1], tsb[:, s, :], op0=ALU.mult, op1=ALU.add),
                       reads=["ident_f", "dcol", ("tsbT", s)], writes=[("tsbT", s)])
                    op("dve", lambda: V.tensor_tensor(Tm[:, g, 0, :], PS[bB][:, 0:128], mU[:], op=ALU.mult), reads=[psk(bB), "mU"], writes=["Tm"])
                    op("pool", lambda: G.tensor_tensor(Tm[:, g, 0, :], Tm[:, g, 0, :], tsb[:, s, :], op=ALU.add), reads=["Tm", ("tsbT", s)], writes=["Tm"])
                    op("act", lambda: A.copy(Tm[:, g, 1, :], PS[bF][:, 128:256]), reads=[psk(bF)], writes=["Tm"])
                    op("act", lambda: A.copy(Tm[:, g, 2, :], PS[bB][:, 128:256]), reads=[psk(bB)], writes=["Tm"])
                    bW = pb % 8; pb += 1
                    for tb in range(2):
                        for ri, EBt, ek in ((0, EBr, "EBr"), (1, EBi, "EBi")):
                            op("pe", lambda tb=tb, ri=ri, EBt=EBt: PE.transpose(PS[bW][:, (tb * 2 + ri) * 128:(tb * 2 + ri + 1) * 128],
                                                                                  EBt[:, gl, tb * 8:(tb + 1) * 8, :].rearrange("p t c -> p (t c)"), ident_f[:]),
                               reads=[ek, "ident_f"], writes=[psk(bW)])
                    evac(WB[:, g, :, :, :].rearrange("p a b m -> p (a b m)"), PS[bW][:], reads=[psk(bW)], writes=["WB"])
            release(nT)
            SX = sb("SX", [128, NCH, 2, NGR], BF16)
            ucm = sb("ucm", [128, L * 256], BF16)
            ucp = sb("ucp", [128, L * 256], BF16)
            ucpv = ucp[:].rearrange("p (g x) -> p g x", g=NGR)
            U3 = sb("U3", [128, NGR, 2, 128], BF16)
            ucv = ucm[:].rearrange("p (t g c) -> p t g c", t=L, g=NGR)
            NSEGC = NCH // 128 if NCH >= 128 else 1
            NPT = min(128, NCH)

            def load_U3(ti):
                dma(ucm[0:NPT, :].rearrange("p (t c) -> p t c", t=L), vu_d[ti * NPT * L:(ti + 1) * NPT * L, 768 + g0 * 16:768 + g0 * 16 + 256].rearrange("(n t) c -> n t c", t=L), reads=["vu_d"], writes=["ucm"])
                op("pool", lambda: G.tensor_copy(ucp[0:NPT, :].rearrange("p (g t c) -> p g t c", g=NGR, t=L), ucm[0:NPT, :].rearrange("p (t g c) -> p g t c", t=L, g=NGR)), reads=["ucm"], writes=["ucp"])
                pbl = 0
                for g in range(NGR):
                    if g % 4 == 0:
                        bq = pbl % 4; pbl += 1
                        pT = PS[bq][:].bitcast(BF16)
                    for tb in range(2):
                        o = ((g % 4) * 2 + tb) * 128
                        op("pe", lambda g=g, tb=tb, o=o, pT=pT: PE.transpose(pT[:, o:o + NPT], ucpv[0:NPT, g, tb * 128:(tb + 1) * 128], ident_b[0:NPT, 0:NPT]),
                           reads=["ucp", "ident_b"], writes=[psk(bq)])
                    if g % 4 == 3:
                        evac(U3[:, g - 3:g + 1, :, 0:NPT], pT.rearrange("p (g t n) -> p g t n", g=4, t=2)[:, :, :, 0:NPT], reads=[psk(bq)], writes=["U3"])
            for ti in range(NCH // NPT):
                load_U3(ti)
                for g in range(NGR):
                    b = 4 + (g % 4)
                    for ri in range(2):
                        for tb in range(2):
                            op("pe", lambda g=g, ri=ri, tb=tb: PE.matmul(PS[b][:, ri * 128:ri * 128 + NPT], WB[:, g, tb, ri, :], U3[:, g, tb, 0:NPT], start=(tb == 0), stop=(tb == 1)),
                               reads=["WB", "U3"], writes=[psk(b)])
                    evac(SX[:, ti * NPT:(ti + 1) * NPT, :, g].rearrange("p n r -> p r n"), PS[b][:, 0:256].rearrange("p (r n) -> p r n", r=2)[:, :, 0:NPT],
                         reads=[psk(b)], writes=["SX"])
            op("dve", lambda: V.memset(Z[:], 0.0), writes=["Z"])
            for k in range(NCH):
                kf = k; kbw = NCH - 1 - k
                if k % CPS == 0 and k > 0:
                    sgi = k // CPS
                    op("dve", lambda sgi=sgi: V.tensor_scalar(Z[:], Z[:], lflag[:, sgi:sgi + 1], None, op0=ALU.mult), reads=["Z", "lflag"], writes=["Z"])
                op("dve", lambda: V.tensor_tensor(tA[:], A1[:], Z[:], op=ALU.mult), reads=["A1", "Z"], writes=["tA"])
                op("dve", lambda: V.tensor_tensor(tB2[:, 0, :], A2[:, 0, :], Z[:, 1, :], op=ALU.mult), reads=["A2", "Z"], writes=["tB2"])
                op("dve", lambda: V.tensor_tensor(tB2[:, 1, :], A2[:, 1, :], Z[:, 0, :], op=ALU.mult), reads=["A2", "Z"], writes=["tB2"])
                op("dve", lambda: V.tensor_tensor(tA[:], tA[:], tB2[:], op=ALU.add), reads=["tA", "tB2"], writes=["tA"])
                op("pool", lambda: G.tensor_copy(Zc[F_], Z[F_]), reads=["Z"], writes=["Zc"])
                op("pool", lambda: G.tensor_copy(Zc[B_], Z[B_]), reads=["Z"], writes=["Zc"])
                op("dve", lambda: V.tensor_tensor(Z[F_], tA[F_], SX[F_, kf, :, :], op=ALU.add), reads=["tA", "SX", "Zc"], writes=["Z"])
                op("dve", lambda: V.tensor_tensor(Z[B_], tA[B_], SX[B_, kbw, :, :], op=ALU.add), reads=["tA", "SX", "Zc"], writes=["Z"])
                op("pool", lambda: G.tensor_copy(SX[F_, kf, :, :], Zc[F_]), reads=["Zc", "Z"], writes=["SX"])
                op("pool", lambda: G.tensor_copy(SX[B_, kbw, :, :], Zc[B_]), reads=["Zc", "Z"], writes=["SX"])
            ysb = sb("ysb", [128, 2, 2, 128], BF16)
            hcm = ucm
            hcv = hcm[:].rearrange("p (t g c) -> p t g c", t=L, g=NGR)
            for ti in range(NCH // NPT):
                load_U3(ti)
                for g in range(NGR):
                    s = g % 2
                    for tbo in range(2):
                        b = 4 + (g * 2 + tbo) % 4
                        mm = []
                        mm.append((Tm[:, g, 0, :], U3[:, g, tbo, 0:NPT]))
                        if tbo == 1:
                            mm.append((Tm[:, g, 1, :], U3[:, g, 0, 0:NPT]))
                        else:
                            mm.append((Tm[:, g, 2, :], U3[:, g, 1, 0:NPT]))
                        for ri in range(2):
                            mm.append((CK[F_, g, ri, tbo * 128:(tbo + 1) * 128], SX[F_, ti * NPT:(ti + 1) * NPT, ri, g]))
                            mm.append((CK[B_, g, ri, tbo * 128:(tbo + 1) * 128], SX[B_, ti * NPT:(ti + 1) * NPT, ri, g]))
                        for q, (lh, rh) in enumerate(mm):
                            op("pe", lambda lh=lh, rh=rh, q=q: PE.matmul(PS[b][:, 0:NPT], lh, rh, start=(q == 0), stop=(q == len(mm) - 1)),
                               reads=["Tm", "U3", "CK", "SX"], writes=[psk(b)])
                        op("act", lambda tbo=tbo: A.activation(ysb[:, s, tbo, 0:NPT], PS[b][:, 0:NPT], AF.Gelu), reads=[psk(b)], writes=[("ysb", s, tbo)])
                    bq = (g % 2) * 2
                    for tbo in range(2):
                        pT = PS[bq + tbo][:].bitcast(BF16)
                        op("pe", lambda tbo=tbo, pT=pT: PE.transpose(pT[0:NPT, 0:128], ysb[:, s, tbo, 0:NPT], ident_b[:]), reads=[("ysb", s, tbo), "ident_b"], writes=[psk(bq + tbo)])
                        evac(ucpv[0:NPT, g, tbo * 128:(tbo + 1) * 128], pT[0:NPT, 0:128], reads=[psk(bq + tbo)], writes=["ucp"])
                op("pool", lambda: G.tensor_copy(ucm[0:NPT, :].rearrange("p (t g c) -> p g t c", t=L, g=NGR), ucp[0:NPT, :].rearrange("p (g t c) -> p g t c", g=NGR, t=L)), reads=["ucp"], writes=["ucm"])
                dma(h_d[ti * NPT * L:(ti + 1) * NPT * L, g0 * 16:g0 * 16 + 256].rearrange("(n t) c -> n t c", t=L), ucm[0:NPT, :].rearrange("p (t c) -> p t c", t=L), reads=["ucm"], writes=["h_d"])
            release(n0)

        n0 = len(stack)
        R = 20
        wstg = sb("wstgB", [128, 2, 2048], F32)
        wqm = sb("wqm", [128, 8, 512], BF16); wkv = sb("wkv", [128, 8, 1024], BF16)
        load_w_bf16(wqm, "wqm", W["w_in"][l][:, 2816:3328], 8, 512, wstg, "wstgB")
        load_w_bf16(wkv, "wkv", W["w_mem_kv"][l], 8, 1024, wstg, "wstgB")
        KmT = sb("KmT", [128, NSEG, 4, MEM], BF16); Vm = sb("Vm", [128, NSEG, 2, 4, 129], BF16)
        memT = sb("memT", [128, NSEG, 8, MEM], BF16)
        mstg = sb("mstg", [128, 2, 1024], F32); mbf = sb("mbf", [128, 2, 1024], BF16)
        it = 0
        for sg in range(NSEG):
            for mb in range(2):
                s = it % 2; it += 1
                dma(mstg[:, s, :], mem_in[sg, mb * 128:(mb + 1) * 128, :], writes=[("mstg", s)])
                cast(mbf[:, s, :], mstg[:, s, :], reads=[("mstg", s)], writes=[("mbf", s)])
                pT = PS[s][:].bitcast(BF16)
                for kc in range(8):
                    op("pe", lambda kc=kc: PE.transpose(pT[:, kc * 128:(kc + 1) * 128], mbf[:, s, kc * 128:(kc + 1) * 128], ident_b[:]),
                       reads=[("mbf", s), "ident_b"], writes=[psk(s)])
                evac(memT[:, sg, :, mb * 128:(mb + 1) * 128], pT.rearrange("p (k m) -> p k m", k=8), reads=[psk(s)], writes=["memT"])

        op("pool", lambda: G.memset(Vm[:, :, :, :, 128:129], 1.0), writes=["Vm"])
        pb = 0
        for sg in range(NSEG):
            for h in range(4):
                b = pb % 8; pb += 1
                for kc in range(8):
                    op("pe", lambda kc=kc: PE.matmul(PS[b][:, 0:MEM], wkv[:, kc, h * 128:(h + 1) * 128], memT[:, sg, kc, :], start=(kc == 0), stop=(kc == 7)),
                       reads=["wkv", "memT"], writes=[psk(b)])
                evac(KmT[:, sg, h, :], PS[b][:, 0:MEM], reads=[psk(b)], writes=["KmT"])
            for mb in range(2):
                b = pb % 8; pb += 1
                for kc in range(8):
                    op("pe", lambda kc=kc: PE.matmul(PS[b][:], memT[:, sg, kc, mb * 128:(mb + 1) * 128], wkv[:, kc, 512:1024], start=(kc == 0), stop=(kc == 7)),
                       reads=["wkv", "memT"], writes=[psk(b)])
                evac(Vm[:, sg, mb, :, 0:128], PS[b][:].rearrange("p (h e) -> p h e", h=4), reads=[psk(b)], writes=["Vm"])
        kring = sb("kring", [128, 6, R * 128], BF16); vring = sb("vring", [128, R, 12, 65], BF16)
        op("pool", lambda: G.memset(vring[:, :, :, 64:65], 1.0), writes=[("vr", s_) for s_ in range(R)])
        qt = sb("qt", [128, 2, 6, 128], BF16); xt = sb("xtB", [128, 2, 8, 128], BF16)
        esb = sb("esb", [128, 2, 512], BF16); psb = sb("psb", [128, 2, 4, 128], BF16)
        qmT = sb("qmT", [128, 4, 128], BF16); emb = sb("emb", [128, 2, 512], BF16)
        amo = sb("amo", [128, 2, 768], BF16); rec = sb("rec", [128, 8], F32)
        zt_ = sb("zt_", [128, 260], BF16)
        op("pool", lambda: G.memset(zt_[:], 0.0), writes=["zt_"])

        def load_blk(bk):
            sl = bk % R
            dma(kring[:, :, sl * 128:(sl + 1) * 128], qk_d[768:1536, bk * 128:(bk + 1) * 128].rearrange("(c p) t -> p c t", p=128), reads=["qk_d"], writes=[("kr", sl)])
            dma(vring[:, sl, :, 0:64], vu_d[bk * 128:(bk + 1) * 128, 0:768].rearrange("t (h e) -> t h e", e=64), reads=["vu_d"], writes=[("vr", sl)])
        for bk in range(min(NT, 10)):
            load_blk(bk)
        SC_M = 1.0 / math.sqrt(128.0)
        for j in range(NT):
            if j + 10 < NT:
                load_blk(j + 10)
            js = j % 2
            dma(qt[:, js, :, :], qk_d[0:768, j * 128:(j + 1) * 128].rearrange("(c p) t -> p c t", p=128), reads=["qk_d"], writes=[("qt", js)])
            dma(xt[:, js, :, :], xT_d[:, :, j * 128:(j + 1) * 128].rearrange("k p t -> p k t"), reads=["xT_d"], writes=[("xtB", js)])
            cl = [(g, dl) for (g, dl) in COMBOS if 0 <= j + dl < NT]
            OA = 0
            op("pe", lambda: PE.matmul(PS[OA][:, 0:260], zt_[:, 0:128], zt_[:, 0:260], start=True, stop=False), reads=["zt_"], writes=[psk(OA)])
            def S_stage(ci):
                g, dl = cl[ci]; bk = j + dl; sl = bk % R; b = 2 + ci % 2
                for hs in range(4):
                    head = 4 * g + hs; c = head // 2; pr = slice((head % 2) * 64, (head % 2) * 64 + 64)
                    op("pe", lambda hs=hs, c=c, pr=pr: PE.matmul(PS[b][:, hs * 128:(hs + 1) * 128], kring[pr, c, sl * 128:(sl + 1) * 128], qt[pr, js, c, :], start=True, stop=True),
                       reads=[("kr", sl), ("qt", js)], writes=[psk(b)])

            def M_stage(ci):
                g, dl = cl[ci]; b = 2 + ci % 2; s = ci % 2
                cidx = COMBOS.index((g, dl))
                op("act", lambda: A.activation(esb[:, s, :], PS[b][:], AF.Exp, scale=0.125), reads=[psk(b)], writes=[("esb", s)])
                fi = j * 17 + dl + 8
                op("dve", lambda: V.scalar_tensor_tensor(psb[:, s, :, :], esb[:, s, :].rearrange("p (h q) -> p h q", h=4), aflag[:, fi:fi + 1],
                                                           amask[:, cidx, :].unsqueeze(1).to_broadcast([128, 4, 128]), op0=ALU.mult, op1=ALU.mult),
                   reads=[("esb", s), "aflag", "amask"], writes=[("psb", s)])

            def PV_stage(ci):
                g, dl = cl[ci]; bk = j + dl; sl = bk % R; s = ci % 2
                for hs in range(4):
                    op("pe", lambda hs=hs: PE.matmul(PS[OA][:, hs * 65:(hs + 1) * 65], psb[:, s, hs, :], vring[:, sl, 4 * g + hs, :], start=False, stop=(ci == len(cl) - 1 and hs == 3)),
                       reads=[("psb", s), ("vr", sl)], writes=[psk(OA)])
            S_stage(0)
            for ci in range(len(cl)):
                if ci + 1 < len(cl):
                    S_stage(ci + 1)
                M_stage(ci)
                PV_stage(ci)
            oav = PS[OA][:, 0:260].rearrange("p (h e) -> p h e", h=4)
            op("dve", lambda: V.reciprocal(rec[:, 0:4], oav[:, :, 64]), reads=[psk(OA)], writes=["rec"])
            op("dve", lambda: V.tensor_tensor(amo[:, js, 0:256].rearrange("p (h e) -> p h e", h=4), oav[:, :, 0:64], rec[:, 0:4].unsqueeze(2).to_broadcast([128, 4, 64]), op=ALU.mult),
               reads=[psk(OA), "rec"], writes=[("amo", js)])
            sg = (j * 128) // SEG
            bQ = 1
            for h in range(4):
                for kc in range(8):
                    op("pe", lambda h=h, kc=kc: PE.matmul(PS[bQ][:, h * 128:(h + 1) * 128], wqm[:, kc, h * 128:(h + 1) * 128], xt[:, js, kc, :], start=(kc == 0), stop=(kc == 7)),
                       reads=["wqm", ("xtB", js)], writes=[psk(bQ)])
            evac(qmT[:].rearrange("p h t -> p (h t)"), PS[bQ][:], reads=[psk(bQ)], writes=["qmT"])
            for half in range(2):
                b = 4 + half
                for hh in range(2):
                    h = 2 * half + hh
                    for mb in range(2):
                        o = (hh * 2 + mb) * 128
                        op("pe", lambda h=h, mb=mb, o=o: PE.matmul(PS[b][:, o:o + 128], KmT[:, sg, h, mb * 128:(mb + 1) * 128], qmT[:, h, :], start=True, stop=True),
                           reads=["KmT", "qmT"], writes=[psk(b)])
                op("act", lambda half=half: A.activation(emb[:, half, :], PS[b][:], AF.Exp, scale=SC_M), reads=[psk(b)], writes=[("emb", half)])
                bo = 6 + half
                for hh in range(2):
                    h = 2 * half + hh
                    for mb in range(2):
                        o = (hh * 2 + mb) * 128
                        op("pe", lambda h=h, mb=mb, o=o, hh=hh: PE.matmul(PS[bo][:, hh * 129:(hh + 1) * 129], emb[:, half, o:o + 128], Vm[:, sg, mb, h, :], start=(mb == 0), stop=(mb == 1)),
                           reads=[("emb", half), "Vm"], writes=[psk(bo)])
                omv = PS[bo][:, 0:258].rearrange("p (h e) -> p h e", h=2)
                op("dve", lambda half=half: V.reciprocal(rec[:, 4 + 2 * half:6 + 2 * half], omv[:, :, 128]), reads=[psk(bo)], writes=["rec"])
                op("dve", lambda half=half: V.tensor_tensor(amo[:, js, 256 + half * 256:512 + half * 256].rearrange("p (h e) -> p h e", h=2), omv[:, :, 0:128],
                                                             rec[:, 4 + 2 * half:6 + 2 * half].unsqueeze(2).to_broadcast([128, 2, 128]), op=ALU.mult),
                   reads=[psk(bo), "rec"], writes=[("amo", js)])
            dma(am_d[j * 128:(j + 1) * 128, :], amo[:, js, :], reads=[("amo", js)], writes=["am_d"])
        release(n0)

        n0 = len(stack)
        wstg = sb("wstgC", [128, 2, 2048], F32)
        wg = sb("wg", [128, 8, 3072], BF16); watt = sb("watt", [128, 2, 1024], BF16); wssm = sb("wssm", [128, 4, 1024], BF16)
        wmem = sb("wmem", [128, 4, 1024], BF16); wmix = sb("wmix", [128, 8, 1024], BF16); wglu = sb("wglu", [128, 4, 512], BF16)
        load_w_bf16(wg, "wg", W["w_in"][l][:, 3328:6400], 8, 3072, wstg, "wstgC")
        load_w_bf16(watt, "watt", W["w_att_out"][l], 2, 1024, wstg, "wstgC")
        load_w_bf16(wssm, "wssm", W["w_ssm_out"][l], 4, 1024, wstg, "wstgC")
        load_w_bf16(wmem, "wmem", W["w_mem_out"][l], 4, 1024, wstg, "wstgC")
        load_w_bf16(wmix, "wmix", W["w_mix_out"][l], 8, 1024, wstg, "wstgC")
        load_w_bf16(wglu, "wglu", W["w_glu"][l], 4, 512, wstg, "wstgC")
        bgate = sb("bgate", [128, 24], F32); bglu = sb("bglu", [128, 4], F32)
        dma(bgate[:], W["b_gate"][l].rearrange("(c p) -> p c", p=128), writes=["bgate"], allow_slow_non_contiguous=True)
        dma(bglu[:], W["b_glu"][l].rearrange("(c p) -> p c", p=128), writes=["bglu"], allow_slow_non_contiguous=True)
        g1 = sb("g1", [128, 1024], F32); b1 = sb("b1", [128, 1024], F32)
        bcast_row(g1[:], "g1", W["ln1_g"][l], 1024); bcast_row(b1[:], "b1", W["ln1_b"][l], 1024)
        xTb = sb("xTb", [128, 8, 512], BF16); amt = sb("amt", [128, 2, 768], BF16); ht = sb("htk", [128, 2, 512], BF16)
        amT = sb("amT", [128, 6, 512], BF16); hT = sb("hT", [128, 4, 512], BF16); hg = sb("hg", [128, 4, 512], BF16)
        sig = sb("sig", [128, 2, 512], BF16); mg = sb("mg", [128, 2, 512], F32); mrg = sb("mrg", [128, 8, 512], BF16)
        xfC = sb("xfC", [128, 2, 1024], F32); zs = sb("zs", [128, 2, 1024], F32); st = sb("st", [128, 2, 6], F32); mv = sb("mv", [128, 2], F32)
        x1b = sb("x1b", [128, 2, 1024], BF16); x1T = sb("x1T", [128, 2, 8, 128], BF16)

        def layer_norm(zt, zk, gt, bt, gk, bk_, out_f32, outk):
            op("dve", lambda: V.bn_stats(st[:, 0, :], zt[:, 0:512]), reads=[zk], writes=["st"])
            op("dve", lambda: V.bn_stats(st[:, 1, :], zt[:, 512:1024]), reads=[zk], writes=["st"])
            op("dve", lambda: V.bn_aggr(mv[:], st[:]), reads=["st"], writes=["mv"])
            op("dve", lambda: V.tensor_scalar(mv[:, 1:2], mv[:, 1:2], EPS, None, op0=ALU.add), reads=["mv"], writes=["mv"])
            op("act", lambda: A.activation(mv[:, 1:2], mv[:, 1:2], AF.Sqrt), reads=["mv"], writes=["mv"])
            op("dve", lambda: V.reciprocal(mv[:, 1:2], mv[:, 1:2]), reads=["mv"], writes=["mv"])
            op("dve", lambda: V.tensor_scalar(zt, zt, mv[:, 0:1], mv[:, 1:2], op0=ALU.subtract, op1=ALU.mult), reads=[zk, "mv"], writes=[zk])
            op("dve", lambda: V.tensor_tensor(zt, zt, gt, op=ALU.mult), reads=[zk, gk], writes=[zk])
            op("dve", lambda: V.tensor_tensor(out_f32, zt, bt, op=ALU.add), reads=[zk, bk_], writes=[outk])
        pb = 0
        for bi in range(NB):
            t0 = bi * 512
            dma(xTb[:], xT_d[:, :, t0:t0 + 512].rearrange("k p t -> p k t"), reads=["xT_d"], writes=["xTb"])
            for tt in range(4):
                s = tt % 2
                dma(amt[:, s, :], am_d[t0 + tt * 128:t0 + (tt + 1) * 128, :], reads=["am_d"], writes=[("amt", s)])
                dma(ht[:, s, :], h_d[t0 + tt * 128:t0 + (tt + 1) * 128, :], reads=["h_d"], writes=[("htk", s)])
                b = pb % 8; pb += 1
                pT = PS[b][:].bitcast(BF16)
                for c in range(6):
                    op("pe", lambda c=c: PE.transpose(pT[:, c * 128:(c + 1) * 128], amt[:, s, c * 128:(c + 1) * 128], ident_b[:]), reads=[("amt", s), "ident_b"], writes=[psk(b)])
                evac(amT[:, :, tt * 128:(tt + 1) * 128], pT[:, 0:768].rearrange("p (c t) -> p c t", c=6), reads=[psk(b)], writes=["amT"])
                b = pb % 8; pb += 1
                pT = PS[b][:].bitcast(BF16)
                for c in range(4):
                    op("pe", lambda c=c: PE.transpose(pT[:, c * 128:(c + 1) * 128], ht[:, s, c * 128:(c + 1) * 128], ident_b[:]), reads=[("htk", s), "ident_b"], writes=[psk(b)])
                evac(hT[:, :, tt * 128:(tt + 1) * 128], pT[:, 0:512].rearrange("p (c t) -> p c t", c=4), reads=[psk(b)], writes=["hT"])
            for c in range(4):
                b = pb % 8; pb += 1; s = c % 2
                for kc in range(4):
                    op("pe", lambda kc=kc: PE.matmul(PS[b][:], wglu[:, kc, c * 128:(c + 1) * 128], hT[:, kc, :], start=(kc == 0), stop=(kc == 3)), reads=["wglu", "hT"], writes=[psk(b)])
                op("act", lambda: A.activation(sig[:, s, :], PS[b][:], AF.Sigmoid, bias=bglu[:, c:c + 1]), reads=[psk(b), "bglu"], writes=[("sig", s)])
                op("dve", lambda: V.tensor_tensor(hg[:, c, :], hT[:, c, :], sig[:, s, :], op=ALU.mult), reads=["hT", ("sig", s)], writes=["hg"])
            for dm in range(8):
                ms = dm % 2
                for br, (wt, wk_, nk, rhs_of, rk) in enumerate(((watt, "watt", 2, lambda kc: amT[:, kc, :], "amT"),
                                                                  (wssm, "wssm", 4, lambda kc: hg[:, kc, :], "hg"),
                                                                  (wmem, "wmem", 4, lambda kc: amT[:, 2 + kc, :], "amT"))):
                    b = pb % 8; pb += 1; b2 = pb % 8; pb += 1; s = br % 2
                    for kc in range(nk):
                        op("pe", lambda kc=kc: PE.matmul(PS[b][:], wt[:, kc, dm * 128:(dm + 1) * 128], rhs_of(kc), start=(kc == 0), stop=(kc == nk - 1)), reads=[wk_, rk], writes=[psk(b)])
                    for kc in range(8):
                        op("pe", lambda kc=kc: PE.matmul(PS[b2][:], wg[:, kc, br * 1024 + dm * 128:br * 1024 + (dm + 1) * 128], xTb[:, kc, :], start=(kc == 0), stop=(kc == 7)),
                           reads=["wg", "xTb"], writes=[psk(b2)])
                    op("act", lambda: A.activation(sig[:, s, :], PS[b2][:], AF.Sigmoid, bias=bgate[:, br * 8 + dm:br * 8 + dm + 1]), reads=[psk(b2), "bgate"], writes=[("sig", s)])
                    if br == 0:
                        op("dve", lambda: V.tensor_tensor(mg[:, ms, :], PS[b][:], sig[:, s, :], op=ALU.mult), reads=[psk(b), ("sig", s)], writes=[("mg", ms)])
                    else:
                        op("dve", lambda: V.tensor_tensor(sig[:, s, :], PS[b][:], sig[:, s, :], op=ALU.mult), reads=[psk(b), ("sig", s)], writes=[("sig", s)])
                        if br == 1:
                            op("pool", lambda: G.tensor_tensor(mg[:, ms, :], mg[:, ms, :], sig[:, s, :], op=ALU.add), reads=[("mg", ms), ("sig", s)], writes=[("mg", ms)])
                        else:
                            op("pool", lambda: G.tensor_tensor(mrg[:, dm, :], mg[:, ms, :], sig[:, s, :], op=ALU.add), reads=[("mg", ms), ("sig", s)], writes=["mrg"])
            for tt in range(4):
                s = tt % 2
                dma(xfC[:, s, :], x_src[t0 + tt * 128:t0 + (tt + 1) * 128, :], reads=[xsk], writes=[("xfC", s)])
                bb = []
                for half in range(2):
                    b = pb % 8; pb += 1; bb.append(b)
                    for kc in range(8):
                        op("pe", lambda kc=kc: PE.matmul(PS[b][:], mrg[:, kc, tt * 128:(tt + 1) * 128], wmix[:, kc, half * 512:(half + 1) * 512], start=(kc == 0), stop=(kc == 7)),
                           reads=["mrg", "wmix"], writes=[psk(b)])
                    op("dve", lambda half=half: V.scalar_tensor_tensor(zs[:, s, half * 512:(half + 1) * 512], xfC[:, s, half * 512:(half + 1) * 512], ALPHA, PS[b][:], op0=ALU.mult, op1=ALU.add),
                       reads=[("xfC", s), psk(b)], writes=[("zs", s)])
                layer_norm(zs[:, s, :], ("zs", s), g1[:], b1[:], "g1", "b1", zs[:, s, :], ("zs", s))
                dma(x1_d[t0 + tt * 128:t0 + (tt + 1) * 128, :], zs[:, s, :], reads=[("zs", s)], writes=["x1_d"])
                cast(x1b[:, s, :], zs[:, s, :], reads=[("zs", s)], writes=[("x1b", s)])
                b = pb % 8; pb += 1
                pT = PS[b][:].bitcast(BF16)
                for kc in range(8):
                    op("pe", lambda kc=kc: PE.transpose(pT[:, kc * 128:(kc + 1) * 128], x1b[:, s, kc * 128:(kc + 1) * 128], ident_b[:]), reads=[("x1b", s), "ident_b"], writes=[psk(b)])
                evac(x1T[:, s, :, :], pT.rearrange("p (k t) -> p k t", k=8), reads=[psk(b)], writes=[("x1T", s)])
                dma(x1T_d[:, :, t0 + tt * 128:t0 + (tt + 1) * 128].rearrange("k p t -> p k t"), x1T[:, s, :, :], reads=[("x1T", s)], writes=["x1T_d"])
        release(n0)

        n0 = len(stack)
        ST = 256; NST = T // ST
        wq = sb("wq", [128, 8, 2048], BF16)
        keysT = sb("keysT", [128, 2, 128], BF16)
        nW = len(stack)
        wstg = sb("wstgP", [128, 2, 2048], F32)
        load_w_bf16(wq, "wq", W["w_query"][l], 8, 2048, wstg, "wstgP")
        kst = sb("kst", [128, 128], F32); ksb = sb("ksb", [128, 128], BF16)
        for hf in range(2):
            dma(kst[:], W["sub_keys"][l, hf], writes=["kst"])
            op("dve", lambda: V.tensor_copy(ksb[:], kst[:]), reads=["kst"], writes=["ksb"])
            pT = PS[hf][:].bitcast(BF16)
            op("pe", lambda: PE.transpose(pT[:, 0:128], ksb[:], ident_b[:]), reads=["ksb", "ident_b"], writes=[psk(hf)])
            evac(keysT[:, hf, :], pT[:, 0:128], reads=[psk(hf)], writes=["keysT"])
        release(nW)
        g2 = sb("g2", [128, 1024], F32); b2_ = sb("b2", [128, 1024], F32)
        bcast_row(g2[:], "g2", W["ln2_g"][l], 1024); bcast_row(b2_[:], "b2", W["ln2_b"][l], 1024)
        x1Tp = sb("x1Tp", [128, 8, ST], BF16); qpT = sb("qpT", [128, 16, ST], BF16)
        sc = sb("sc", [128, 16, 128], F32)
        mxa = sb("mxa", [128, 2, 8], F32); mxb_ = sb("mxb", [128, 2, 8], F32); mia = sb("mia", [128, 2, 8], U32); mib = sb("mib", [128, 2, 8], U32)
        tmpk = sb("tmpk", [128, 2, 256], F32)
        V12 = sb("V12", [128, 16, 16], F32); I12 = sb("I12", [128, 16, 16], F32)
        cand = sb("cand", [128, 8, 256], F32); best = sb("best", [128, 8, 16], F32); posu = sb("posu", [128, 8, 16], U32)
        pa = sb("pa", [128, 8, 16], U32); pbb = sb("pbb", [128, 8, 16], U32); paf = sb("paf", [128, 8, 16], F32); pbf = sb("pbf", [128, 8, 16], F32)
        eq = sb("eq", [128, 8, 16, 16], F32); gate = sb("gate", [128, 8, 16], F32); zsum = sb("zsum", [128, 8], F32)
        isel = sb("isel", [128, 3, 128], F32)
        iT = sb("iT", [128, 3, ST], BF16)
        Ab = sb("Ab", [128, 2, 16, 128], BF16); Bb = sb("Bb", [128, 2, 16, 128], BF16)
        Wsb = sb("Wsb", [128, ST, 128], BF16)
        UTi = sb("UTi", [128, 2, 1024], BF16); Vi = sb("Vi", [128, 2, 1024], BF16)
        gl_ = sb("gl", [128, 2, ST], BF16); Gt = sb("Gt", [128, 2, ST], BF16)
        xfP = sb("xfP", [128, 2, 1024], F32); zs = sb("zsP", [128, 2, 1024], F32); st = sb("stP", [128, 2, 6], F32); mv = sb("mvP", [128, 2], F32)

        def topk16(src_ap, srck, n, vout, iout, outk, s):
            op("dve", lambda: V.max(mxa[:, s, :], src_ap), reads=[srck], writes=[("mxa", s)]); yield
            op("dve", lambda: V.max_index(mia[:, s, :], mxa[:, s, :], src_ap), reads=[srck, ("mxa", s)], writes=[("mia", s)], inline=False); yield
            op("dve", lambda: V.match_replace(tmpk[:, s, 0:n], mxa[:, s, :], src_ap, -1e30), reads=[srck, ("mxa", s)], writes=[("tmpk", s)], inline=False); yield
            op("dve", lambda: V.max(mxb_[:, s, :], tmpk[:, s, 0:n]), reads=[("tmpk", s)], writes=[("mxb", s)]); yield
            op("dve", lambda: V.max_index(mib[:, s, :], mxb_[:, s, :], tmpk[:, s, 0:n]), reads=[("tmpk", s), ("mxb", s)], writes=[("mib", s)], inline=False)
            op("pool", lambda: G.tensor_copy(vout[:, 0:8], mxa[:, s, :]), reads=[("mxa", s)], writes=[outk])
            op("pool", lambda: G.tensor_copy(vout[:, 8:16], mxb_[:, s, :]), reads=[("mxb", s)], writes=[outk])
            op("pool", lambda: G.tensor_copy(iout[:, 0:8], mia[:, s, :]), reads=[("mia", s)], writes=[outk])
            op("pool", lambda: G.tensor_copy(iout[:, 8:16], mib[:, s, :]), reads=[("mib", s)], writes=[outk])
            yield

        def lane(gens):
            for g_ in gens:
                yield from g_

        def run_lanes(l0, l1):
            a_, b_ = lane(l0), lane(l1)
            da = db = False
            while not (da and db):
                if not da:
                    try:
                        next(a_)
                    except StopIteration:
                        da = True
                if not db:
                    try:
                        next(b_)
                    except StopIteration:
                        db = True

        def layer_norm2(zt, zk, gt, bt, gk, bk_, out_f32, outk):
            op("dve", lambda: V.bn_stats(st[:, 0, :], zt[:, 0:512]), reads=[zk], writes=["stP"])
            op("dve", lambda: V.bn_stats(st[:, 1, :], zt[:, 512:1024]), reads=[zk], writes=["stP"])
            op("dve", lambda: V.bn_aggr(mv[:], st[:]), reads=["stP"], writes=["mvP"])
            op("dve", lambda: V.tensor_scalar(mv[:, 1:2], mv[:, 1:2], EPS, None, op0=ALU.add), reads=["mvP"], writes=["mvP"])
            op("act", lambda: A.activation(mv[:, 1:2], mv[:, 1:2], AF.Sqrt), reads=["mvP"], writes=["mvP"])
            op("dve", lambda: V.reciprocal(mv[:, 1:2], mv[:, 1:2]), reads=["mvP"], writes=["mvP"])
            op("dve", lambda: V.tensor_scalar(zt, zt, mv[:, 0:1], mv[:, 1:2], op0=ALU.subtract, op1=ALU.mult), reads=[zk, "mvP"], writes=[zk])
            op("dve", lambda: V.tensor_tensor(zt, zt, gt, op=ALU.mult), reads=[zk, gk], writes=[zk])
            op("dve", lambda: V.tensor_tensor(out_f32, zt, bt, op=ALU.add), reads=[zk, bk_], writes=[outk])

        for si in range(NST):
            t0 = si * ST
            dma(x1Tp[:], x1T_d[:, :, t0:t0 + ST].rearrange("k p t -> p k t"), reads=["x1T_d"], writes=["x1Tp"])
            for c in range(16):
                b = c % 2
                for kc in range(8):
                    op("pe", lambda kc=kc: PE.matmul(PS[b][:, 0:ST], wq[:, kc, c * 128:(c + 1) * 128], x1Tp[:, kc, :], start=(kc == 0), stop=(kc == 7)), reads=["wq", "x1Tp"], writes=[psk(b)])
                evac(qpT[:, c, :], PS[b][:, 0:ST], reads=[psk(b)], writes=["qpT"])
            for tt in range(ST // 128):
                for cq in range(4):
                    b = 2 + cq % 2
                    for cc in range(4):
                        c = cq * 4 + cc
                        op("pe", lambda c=c, cc=cc: PE.matmul(PS[b][:, cc * 128:(cc + 1) * 128], qpT[:, c, tt * 128:(tt + 1) * 128], keysT[:, c % 2, :], start=True, stop=True),
                           reads=["qpT", "keysT"], writes=[psk(b)])
                    evac(sc[:, cq * 4:(cq + 1) * 4, :].rearrange("p c k -> p (c k)"), PS[b][:], reads=[psk(b)], writes=["sc"])
                run_lanes([topk16(sc[:, c, :], "sc", 128, V12[:, c, :], I12[:, c, :], ("V12", c % 2), 0) for c in range(0, 16, 2)],
                          [topk16(sc[:, c, :], "sc", 128, V12[:, c, :], I12[:, c, :], ("V12", c % 2), 1) for c in range(1, 16, 2)])
                v12 = V12[:].rearrange("p (h f) a -> p h f a", f=2); i12 = I12[:].rearrange("p (h f) a -> p h f a", f=2)
                op("dve", lambda: V.tensor_tensor(cand[:].rearrange("p h (a b) -> p h a b", a=16), v12[:, :, 0, :].unsqueeze(3).to_broadcast([128, 8, 16, 16]),
                                                  v12[:, :, 1, :].unsqueeze(2).to_broadcast([128, 8, 16, 16]), op=ALU.add), reads=[("V12", 0), ("V12", 1)], writes=["cand"])
                run_lanes([topk16(cand[:, h, :], "cand", 256, best[:, h, :], posu[:, h, :], ("best", h % 2), 0) for h in range(0, 8, 2)],
                          [topk16(cand[:, h, :], "cand", 256, best[:, h, :], posu[:, h, :], ("best", h % 2), 1) for h in range(1, 8, 2)])
                op("dve", lambda: V.tensor_tensor(gate[:], best[:], best[:, :, 0:1].to_broadcast([128, 8, 16]), op=ALU.subtract), reads=[("best", 0), ("best", 1)], writes=["gate"])
                op("act", lambda: A.activation(gate[:], gate[:], AF.Exp), reads=["gate"], writes=["gate"])
                op("dve", lambda: V.tensor_reduce(zsum[:], gate[:], op=ALU.add, axis=AX.X), reads=["gate"], writes=["zsum"])
                op("dve", lambda: V.reciprocal(zsum[:], zsum[:]), reads=["zsum"], writes=["zsum"])
                op("dve", lambda: V.tensor_tensor(isel[:, 2, :].rearrange("p (h k) -> p h k", h=8), gate[:], zsum[:].unsqueeze(2).to_broadcast([128, 8, 16]), op=ALU.mult),
                   reads=["gate", "zsum"], writes=["isel"])
                op("dve", lambda: V.tensor_scalar(pa[:], posu[:], 4, None, op0=ALU.logical_shift_right), reads=[("best", 0), ("best", 1)], writes=["pa"])
                op("dve", lambda: V.tensor_scalar(pbb[:], posu[:], 15, None, op0=ALU.bitwise_and), reads=[("best", 0), ("best", 1)], writes=["pbb"])
                op("pool", lambda: G.tensor_copy(paf[:], pa[:]), reads=["pa"], writes=["paf"])
                op("pool", lambda: G.tensor_copy(pbf[:], pbb[:]), reads=["pbb"], writes=["pbf"])
                io16 = iota_f[:, 0:16].unsqueeze(1).unsqueeze(1).to_broadcast([128, 8, 16, 16])
                for which, pf, pfk in ((0, paf, "paf"), (1, pbf, "pbf")):
                    op("dve", lambda pf=pf: V.tensor_tensor(eq[:], pf[:].unsqueeze(3).to_broadcast([128, 8, 16, 16]), io16, op=ALU.is_equal), reads=[pfk, "iota_f"], writes=["eq"])
                    op("dve", lambda which=which: V.tensor_tensor(eq[:], eq[:], i12[:, :, which, :].unsqueeze(2).to_broadcast([128, 8, 16, 16]), op=ALU.mult), reads=["eq", ("V12", 0), ("V12", 1)], writes=["eq"])
                    op("dve", lambda which=which: V.tensor_reduce(isel[:, which, :].rearrange("p (h k) -> p h k", h=8), eq[:], op=ALU.add, axis=AX.X), reads=["eq"], writes=["isel"])
                for w3 in range(3):
                    b = 2 + w3 % 2
                    op("pe", lambda w3=w3: PE.transpose(PS[b][:, 0:128], isel[:, w3, :], ident_f[:]), reads=["isel", "ident_f"], writes=[psk(b)])
                    evac(iT[:, w3, tt * 128:(tt + 1) * 128], PS[b][:, 0:128], reads=[psk(b)], writes=["iT"])
            for sbk in range(ST // 16):
                s = sbk % 2; ts = slice(sbk * 16, (sbk + 1) * 16)
                iob = iota_b[:].unsqueeze(1).to_broadcast([128, 16, 128])
                op("dve", lambda: V.tensor_tensor(Ab[:, s, :, :], iob, iT[:, 0, ts].unsqueeze(2).to_broadcast([128, 16, 128]), op=ALU.is_equal), reads=["iota_b", "iT"], writes=[("Ab", s)])
                op("pool", lambda: G.tensor_tensor(Ab[:, s, :, :], Ab[:, s, :, :], iT[:, 2, ts].unsqueeze(2).to_broadcast([128, 16, 128]), op=ALU.mult), reads=[("Ab", s), "iT"], writes=[("Ab", s)])
                op("dve", lambda: V.tensor_tensor(Bb[:, s, :, :], iob, iT[:, 1, ts].unsqueeze(2).to_broadcast([128, 16, 128]), op=ALU.is_equal), reads=["iota_b", "iT"], writes=[("Bb", s)])
                for q4 in range(4):
                    b = 2 + q4 % 2
                    for u in range(4):
                        tl = q4 * 4 + u
                        op("pe", lambda tl=tl, u=u: PE.matmul(PS[b][:, u * 128:(u + 1) * 128], Bb[:, s, tl, :], Ab[:, s, tl, :], start=True, stop=True),
                           reads=[("Ab", s), ("Bb", s)], writes=[psk(b)])
                    tg = sbk * 16 + q4 * 4
                    evac(Wsb[:, tg:tg + 4, :].rearrange("p t i -> p (t i)"), PS[b][:], reads=[psk(b)], writes=["Wsb"])
            OB = [4, 5, 6, 7]

            def U_stage(i):
                s3 = i % 2; b = i % 2
                dma(UTi[:, s3, :], UT_d[l, i], reads=["UT_d"], writes=[("UTi", s3)])
                dma(Vi[:, s3, :], VS_d[l, i], reads=["VS_d"], writes=[("Vi", s3)])
                for kc in range(8):
                    op("pe", lambda kc=kc: PE.matmul(PS[b][:, 0:ST], UTi[:, s3, kc * 128:(kc + 1) * 128], x1Tp[:, kc, :], start=(kc == 0), stop=(kc == 7)),
                       reads=[("UTi", s3), "x1Tp"], writes=[psk(b)])

            def M_stage(i):
                s = i % 2; b = i % 2
                op("act", lambda: A.activation(gl_[:, s, :], PS[b][:, 0:ST], AF.Gelu), reads=[psk(b)], writes=[("gl", s)])
                op("dve", lambda: V.tensor_tensor(Gt[:, s, :], gl_[:, s, :], Wsb[:, :, i], op=ALU.mult), reads=[("gl", s), "Wsb"], writes=[("Gt", s)])

            def V_stage(i):
                s3 = i % 2; s = i % 2
                for tt in range(2):
                    for half in range(2):
                        op("pe", lambda tt=tt, half=half: PE.matmul(PS[OB[tt * 2 + half]][:], Gt[:, s, tt * 128:(tt + 1) * 128], Vi[:, s3, half * 512:(half + 1) * 512], start=(i == 0), stop=(i == 127)),
                           reads=[("Gt", s), ("Vi", s3)], writes=[psk(OB[tt * 2 + half])])
            U_stage(0)
            for i in range(128):
                if i + 1 < 128:
                    U_stage(i + 1)
                M_stage(i)
                V_stage(i)
            for tt in range(2):
                s = tt
                dma(xfP[:, s, :], x1_d[t0 + tt * 128:t0 + (tt + 1) * 128, :], reads=["x1_d"], writes=[("xfP", s)])
                for half in range(2):
                    op("dve", lambda half=half: V.scalar_tensor_tensor(zs[:, s, half * 512:(half + 1) * 512], xfP[:, s, half * 512:(half + 1) * 512], ALPHA, PS[OB[tt * 2 + half]][:], op0=ALU.mult, op1=ALU.add),
                       reads=[("xfP", s), psk(OB[tt * 2 + half])], writes=[("zsP", s)])
                layer_norm2(zs[:, s, :], ("zsP", s), g2[:], b2_[:], "g2", "b2", zs[:, s, :], ("zsP", s))
                dma(x_dst[t0 + tt * 128:t0 + (tt + 1) * 128, :], zs[:, s, :], reads=[("zsP", s)], writes=[xdk])
        release(n0)
    kb.finish(["xsrc%d" % DEPTH])
    return nc, kb


def host_consts(SEG, seq_of_seg, n_valid_seg):
    T = NSEG * SEG; NT = T // 128
    c = {}
    c["c_ident"] = np.eye(128, dtype=np.float32)
    pm = np.zeros((128, 128), np.float32)
    for hb in (0, 64):
        for e in range(8):
            pm[hb + e + 8, hb + e] = 1.0; pm[hb + e, hb + e + 8] = 1.0
    c["c_pm"] = pm
    pos = np.zeros(T, np.float64); segstart = np.zeros(T, np.int64); segend = np.zeros(T, np.int64)
    s = 0
    while s < NSEG:
        e = s
        while e + 1 < NSEG and seq_of_seg[e + 1] == seq_of_seg[s]:
            e += 1
        n = (e - s + 1) * SEG
        pos[s * SEG:s * SEG + n] = np.arange(n)
        segstart[s * SEG:s * SEG + n] = s * SEG; segend[s * SEG:s * SEG + n] = s * SEG + n
        s = e + 1
    inv = (500000.0 ** (-np.arange(0, 16, 2, dtype=np.float32) / 16)).astype(np.float32)
    ang = (pos.astype(np.float32)[None, :] * inv[:, None]).astype(np.float32)
    cosT = np.ones((128, T), np.float32); sinT = np.zeros((128, T), np.float32)
    for hb in (0, 64):
        cosT[hb:hb + 8] = np.cos(ang); cosT[hb + 8:hb + 16] = np.cos(ang)
        sinT[hb:hb + 8] = -np.sin(ang); sinT[hb + 8:hb + 16] = np.sin(ang)
    c["c_cos"] = cosT; c["c_sin"] = sinT
    am = np.zeros((128, 25, 128), np.float32)
    kk = np.arange(128)[:, None]; qq = np.arange(128)[None, :]
    for ci, (g, dl) in enumerate(COMBOS):
        win, dil = PATTERNS[g]
        diff = 128 * dl + kk - qq
        am[:, ci, :] = ((diff % dil == 0) & (np.abs(diff) <= (win // 2 // dil) * dil)).astype(np.float32)
    c["c_amask"] = am.reshape(128, 25 * 128)
    fl = np.zeros((NT, 17), np.float32)
    for j in range(NT):
        for dl in range(-8, 9):
            b = j + dl
            if 0 <= b < NT and segstart[j * 128] <= b * 128 < segend[j * 128]:
                fl[j, dl + 8] = 1.0
    c["c_aflag"] = np.broadcast_to(fl.reshape(1, -1), (128, NT * 17)).copy()
    lf = np.zeros((128, NSEG), np.float32)
    for sg in range(1, NSEG):
        lf[0:64, sg] = 1.0 if seq_of_seg[sg] == seq_of_seg[sg - 1] else 0.0
        bs = NSEG - 1 - sg
        lf[64:128, sg] = 1.0 if seq_of_seg[bs] == seq_of_seg[bs + 1] else 0.0
    c["c_lflag"] = lf
    c["c_iota"] = np.broadcast_to(np.arange(128, dtype=np.float32)[None, :], (128, 128)).copy()
    ti = np.arange(128) // 16
    c["c_mL"] = (ti[None, :] >= ti[:, None]).astype(np.float32)
    c["c_mU"] = (ti[None, :] <= ti[:, None]).astype(np.float32)
    return c


def kernel(**inputs):
    SEG = 2048; DEPTH = 4
    wshapes = {n: inputs[n].shape for n in WNAMES}
    nc, kb = build(SEG, DEPTH, wshapes)
    xp = np.asarray(inputs["x_prompt"], np.float32); xsm = np.asarray(inputs["x_sample"], np.float32)
    mp = np.asarray(inputs["mem_prompt"], np.float32); msm = np.asarray(inputs["mem_sample"], np.float32)
    wd = {n: np.ascontiguousarray(np.asarray(inputs[n], np.float32)) for n in WNAMES}
    in_maps = []
    for core in range(8):
        if core < 4:
            x = xp[core]; mem = np.broadcast_to(mp[core][None], (NSEG, MEM, D)); seqs = [0, 0, 0, 0]
        elif core < 6:
            sl = slice((core - 4) * 4, (core - 4) * 4 + 4)
            x = xsm[sl].reshape(NSEG * SEG, D); mem = msm[sl]; seqs = [0, 1, 2, 3]
        else:
            x = np.zeros((NSEG * SEG, D), np.float32); mem = np.zeros((NSEG, MEM, D), np.float32); seqs = [0, 1, 2, 3]
        m = {"x": np.ascontiguousarray(x), "mem": np.ascontiguousarray(mem)}
        m.update(wd); m.update(host_consts(SEG, seqs, NSEG))
        in_maps.append(m)
    res = run_bass_kernel_spmd(nc, in_maps, core_ids=list(range(8)))
    yp = np.stack([res.results[c]["y"] for c in range(4)], 0).astype(np.float32)
    ys = np.concatenate([res.results[c]["y"].reshape(4, SEG, D) for c in (4, 5)], 0).astype(np.float32)
    return (yp, ys)
```
